# Optimizing a Trainium2 kernel written in Bass

```python
import math
import jax, jax.numpy as jnp
from jax import lax
import numpy as np

D_MODEL = 1024
BATCH = 32
SEQ = 256
DEPTH = 2
DEC_BATCH = 2
DEC_SEQ = 2048
PAST_LEN = 256

GRID_W = 64
N_EVEN = (DEPTH + 1) // 2
N_ODD = DEPTH // 2
H_A = 4
DK_A = 128
DV_A = 128
W_A = H_A * DK_A
W_B = 512
HY_ORDER = 2
HY_EMB = 33
HY_BANDS = (HY_EMB - 1) // 2
HY_FF = 64
HY_TARGET = 1e-2
HY_FAST = 0.3
HY_SLOW = 1.5
H_C = 4
DH_C = 64
DV_C = 2 * DH_C
W_C = H_C * DV_C
ROPE_BASE = 10000.0
H_D = 4
DK_D = 64
DV_D = 128
W_D = H_D * DV_D
GLA_RANK = 16
GLA_TAU = 16.0
D_FF = 2816
CHUNK = 32
Q_BLOCK = 128
EPS = 1e-6

EVEN_COLS = (W_A, W_A, W_A, W_A, W_A, (1 + HY_ORDER) * W_B)
ODD_COLS = (H_C * 2 * DH_C, H_C * 2 * DH_C, W_C, H_D * DK_D, H_D * DK_D, W_D, W_D, GLA_RANK, GLA_RANK)
D_IN_EVEN = sum(EVEN_COLS)
D_IN_ODD = sum(ODD_COLS)
F32 = jnp.float32

kernel_name = 'hybrid_hgrn2_hyena_diffattn_gla_prefix_step'


def split_cols(x, sizes):
    idx = [int(i) for i in np.cumsum(sizes)[:-1]]
    return jnp.split(x, idx, axis=-1)


def rms_norm(x, g):
    x32 = x.astype(F32)
    y = x32 * lax.rsqrt(jnp.mean(x32 * x32, axis=-1, keepdims=True) + EPS)
    return (y * g.astype(F32)).astype(x.dtype)


def heads(x, n):
    b_, L, w = x.shape
    return x.reshape(b_, L, n, w // n).transpose(0, 2, 1, 3)


def head_rms(o, g):
    b_, h_, L, d = o.shape
    o32 = o.transpose(0, 2, 1, 3).astype(F32)
    y = o32 * lax.rsqrt(jnp.mean(o32 * o32, axis=-1, keepdims=True) + EPS) * g.reshape(h_, d).astype(F32)
    return y.reshape(b_, L, h_ * d).astype(o.dtype)


def dwconv3(x, w, b):
    L = x.shape[1]
    xp = jnp.pad(x, ((0, 0), (1, 1), (0, 0)))
    return xp[:, :L] * w[0] + xp[:, 1:L + 1] * w[1] + xp[:, 2:] * w[2] + b


def chunk_gated_scan(q, k, v, log_f, s0):
    b_, h_, L, dk = q.shape
    dv = v.shape[-1]
    n = L // CHUNK

    def to_chunks(t):
        return jnp.moveaxis(t.astype(F32).reshape(b_, h_, n, CHUNK, t.shape[-1]), 2, 0)

    mask = jnp.tril(jnp.ones((CHUNK, CHUNK), bool))[:, :, None]

    def step(S, inp):
        qc, kc, vc, gc = inp
        cum = jnp.cumsum(gc, axis=2)
        rel = jnp.where(mask, cum[:, :, :, None, :] - cum[:, :, None, :, :], -jnp.inf)
        att = jnp.einsum('bhtk,bhsk,bhtsk->bhts', qc, kc, jnp.exp(rel))
        o = jnp.einsum('bhts,bhsv->bhtv', att, vc) + jnp.einsum('bhtk,bhkv->bhtv', qc * jnp.exp(cum), S)
        last = cum[:, :, -1:, :]
        S = jnp.exp(last[:, :, 0, :, None]) * S + jnp.einsum('bhsk,bhsv->bhkv', kc * jnp.exp(last - cum), vc)
        return S, o

    s_fin, o = lax.scan(step, s0.astype(F32), (to_chunks(q), to_chunks(k), to_chunks(v), to_chunks(log_f)))
    o = jnp.moveaxis(o, 0, 2).reshape(b_, h_, L, dv)
    return o.astype(v.dtype), s_fin.astype(v.dtype)


def bidir_scan(q, k_f, k_b, v, logf_f, logf_b, s0):
    o_f, s_f = chunk_gated_scan(q, k_f, v, logf_f, s0[:, 0])
    flip = lambda t: jnp.flip(t, axis=2)
    o_b, s_b = chunk_gated_scan(flip(q), flip(k_b), flip(v), flip(logf_b), s0[:, 1])
    return o_f + flip(o_b), jnp.stack([s_f, s_b], axis=1)


def hyena_filters(L, w1, b1, w2, b2, w3, freq):
    t = jnp.linspace(0.0, 1.0, L, dtype=F32)[:, None]
    w = 2.0 * math.pi * jnp.arange(L, dtype=F32)[:, None] / L
    fb = jnp.linspace(1e-4, HY_BANDS - 1, HY_BANDS, dtype=F32)[None]
    z = jnp.concatenate([t, jnp.cos(fb * w), -jnp.sin(fb * w)], axis=-1)
    h = jnp.sin(freq.astype(F32) * (z @ w1.astype(F32) + b1.astype(F32)))
    h = jnp.sin(freq.astype(F32) * (h @ w2.astype(F32) + b2.astype(F32)))
    h = (h @ w3.astype(F32)).reshape(L, HY_ORDER, 2, W_B)
    max_decay = math.log(HY_TARGET) / HY_FAST
    min_decay = math.log(HY_TARGET) / HY_SLOW
    deltas = jnp.abs(jnp.linspace(min_decay, max_decay, W_B, dtype=F32))
    window = jnp.exp(-t * deltas[None])
    return (h * window[:, None, None, :]).transpose(1, 2, 0, 3)


def two_sided_fftconv(u, filt):
    L = u.shape[1]
    kern = jnp.concatenate([filt[0], jnp.zeros((1, filt.shape[-1]), F32), jnp.flip(filt[1, 1:], axis=0)], axis=0)
    U = jnp.fft.rfft(u.astype(F32), n=2 * L, axis=1)
    K = jnp.fft.rfft(kern, axis=0)
    return jnp.fft.irfft(U * K[None], n=2 * L, axis=1)[:, :L].astype(u.dtype)


def rope_2d(x):
    L = x.shape[2]
    rows = L // GRID_W
    row = jnp.repeat(jnp.arange(rows), GRID_W).astype(F32)
    col = jnp.tile(jnp.arange(GRID_W), rows).astype(F32)
    half = DH_C // 2
    inv = ROPE_BASE ** (-jnp.arange(0, half, 2, dtype=F32) / half)

    def rot(t, pos):
        ang = pos[:, None] * inv[None]
        cos = jnp.cos(ang)[:, None, :]
        sin = jnp.sin(ang)[:, None, :]
        t1, t2 = jnp.split(t, 2, axis=-1)
        return jnp.concatenate([t1 * cos - t2 * sin, t1 * sin + t2 * cos], axis=-1)

    xr, xc = jnp.split(x.astype(F32), 2, axis=-1)
    return jnp.concatenate([rot(xr, row), rot(xc, col)], axis=-1).astype(x.dtype)


def diff_attention(q, keys, vals, lam):
    b_, h_, Lq = q.shape[:3]
    nb = Lq // Q_BLOCK
    qb = q.reshape(b_, h_, nb, Q_BLOCK, 2, DH_C).transpose(2, 0, 1, 3, 4, 5)
    scale = DH_C ** -0.5

    def block(qi):
        s = jnp.einsum('bhqpd,bhkpd->bhpqk', qi, keys).astype(F32) * scale
        p = jax.nn.softmax(s, axis=-1)
        w = p[:, :, 0] - lam * p[:, :, 1]
        return jnp.einsum('bhqk,bhkv->bhqv', w.astype(vals.dtype), vals)

    o = lax.map(block, qb)
    return o.transpose(1, 2, 0, 3, 4).reshape(b_, h_, Lq, vals.shape[-1])


def even_mixer(h, s0, p):
    L = h.shape[1]
    q, ff, fb, i, g, hy = split_cols(h @ p['w_in'], EVEN_COLS)
    lb = p['lb']
    f_f = lb[0] + (1.0 - lb[0]) * jax.nn.sigmoid(ff.astype(F32))
    f_b = lb[1] + (1.0 - lb[1]) * jax.nn.sigmoid(fb.astype(F32))
    qh = heads(jax.nn.silu(q) * DK_A ** -0.5, H_A)
    o_a, s_new = bidir_scan(qh, heads(1.0 - f_f, H_A), heads(1.0 - f_b, H_A), heads(i, H_A),
                            heads(jnp.log(f_f), H_A), heads(jnp.log(f_b), H_A), s0)
    out_a = head_rms(o_a, p['hgrn_norm']) * jax.nn.silu(g)
    hy = dwconv3(hy, p['hy_conv_w'], p['hy_conv_b'])
    v, x1, x2 = jnp.split(hy, 1 + HY_ORDER, axis=-1)
    filt = hyena_filters(L, p['hy_w1'], p['hy_b1'], p['hy_w2'], p['hy_b2'], p['hy_w3'], p['hy_freq'])
    z = v
    for o_idx, gate in enumerate((x1, x2)):
        z = gate * (two_sided_fftconv(z, filt[o_idx]) + z * p['hy_d'][o_idx])
    return jnp.concatenate([out_a, z], axis=-1) @ p['w_out'], s_new


def odd_mixer(h, s0, ctx_k, ctx_v, l, p):
    b_, L, _ = h.shape
    cq, ck, cv, dq, dk, dv, dg, da_f, da_b = split_cols(h @ p['w_in'], ODD_COLS)
    q = cq.reshape(b_, L, H_C, 2, DH_C).transpose(0, 2, 1, 3, 4)
    k = ck.reshape(b_, L, H_C, 2, DH_C).transpose(0, 2, 1, 3, 4)
    v = heads(cv, H_C)
    if ctx_k is None:
        keys, vals = k, v
        cache = (k.reshape(b_, H_C, L, 2 * DH_C), v)
    else:
        q = rope_2d(q)
        keys = jnp.concatenate([ctx_k.reshape(b_, H_C, -1, 2, DH_C), rope_2d(k)], axis=2)
        vals = jnp.concatenate([ctx_v, v], axis=2)
        cache = None
    lam_init = 0.8 - 0.6 * math.exp(-0.3 * l)
    lp = p['diff_lambda'].astype(F32)
    lam = jnp.exp(jnp.sum(lp[0] * lp[1])) - jnp.exp(jnp.sum(lp[2] * lp[3])) + lam_init
    o_c = diff_attention(q, keys, vals, lam)
    out_c = head_rms(o_c, p['diff_norm']) * (1.0 - lam_init)
    qd = heads(dq, H_D) * DK_D ** -0.5
    kd = heads(dk, H_D)
    vd = heads(dv, H_D)
    la_f = jax.nn.log_sigmoid((da_f @ p['gla_aw'][0] + p['gla_ab'][0]).astype(F32)) / GLA_TAU
    la_b = jax.nn.log_sigmoid((da_b @ p['gla_aw'][1] + p['gla_ab'][1]).astype(F32)) / GLA_TAU
    o_d, s_new = bidir_scan(qd, kd, kd, vd, heads(la_f, H_D), heads(la_b, H_D), s0)
    out_d = head_rms(o_d, p['gla_norm']) * jax.nn.silu(dg)
    return jnp.concatenate([out_c, out_d], axis=-1) @ p['w_out'], cache, s_new


def conv_ffn(h, up, cw, cb, down):
    u = dwconv3(h @ up, cw, cb)
    a, g = jnp.split(u, 2, axis=-1)
    return (jax.nn.silu(g) * a) @ down


def modulate(x, mod, j):
    return x * (1.0 + mod[:, :, 3 * j + 1]) + mod[:, :, 3 * j]


def setup_inputs(seed: int = 0) -> dict:
    key = jax.random.key(seed)
    ks = iter(jax.random.split(key, 40))

    def nrm(shape, scale=1.0):
        return jax.random.normal(next(ks), shape, F32) * scale

    def gain(shape):
        return 1.0 + nrm(shape, 0.05)

    D = D_MODEL
    return {
        'x_prompt': nrm((BATCH, SEQ, D)),
        'x_sample': nrm((DEC_BATCH, DEC_SEQ, D)),
        'state_hgrn': nrm((DEC_BATCH, N_EVEN, 2, H_A, DK_A, DV_A), 0.5),
        'cache_diff_k': nrm((DEC_BATCH, N_ODD, H_C, PAST_LEN, 2 * DH_C)),
        'cache_diff_v': nrm((DEC_BATCH, N_ODD, H_C, PAST_LEN, DV_C)),
        'state_gla': nrm((DEC_BATCH, N_ODD, 2, H_D, DK_D, DV_D), 0.5),
        'c': nrm((DEC_BATCH, D)),
        'c_ctx': nrm((D,)),
        'ada_w': nrm((DEPTH, D, 6 * D), 0.5 * D ** -0.5),
        'ada_b': nrm((DEPTH, 6 * D), 0.01),
        'norm_g': gain((DEPTH, 4, D)),
        'ffn_up': nrm((DEPTH, D, 2 * D_FF), D ** -0.5),
        'ffn_conv_w': nrm((DEPTH, 3, 2 * D_FF), 3 ** -0.5),
        'ffn_conv_b': nrm((DEPTH, 2 * D_FF), 0.01),
        'ffn_down': nrm((DEPTH, D_FF, D), D_FF ** -0.5),
        'w_in_even': nrm((N_EVEN, D, D_IN_EVEN), D ** -0.5),
        'w_out_even': nrm((N_EVEN, W_A + W_B, D), (W_A + W_B) ** -0.5),
        'hgrn_lb': nrm((DEPTH + 1, 2, W_A), 0.5),
        'hgrn_norm': gain((N_EVEN, W_A)),
        'hy_conv_w': nrm((N_EVEN, 3, (1 + HY_ORDER) * W_B), 3 ** -0.5),
        'hy_conv_b': nrm((N_EVEN, (1 + HY_ORDER) * W_B), 0.01),
        'hy_w1': nrm((N_EVEN, HY_EMB, HY_FF), HY_EMB ** -0.5),
        'hy_b1': nrm((N_EVEN, HY_FF), 0.1),
        'hy_w2': nrm((N_EVEN, HY_FF, HY_FF), HY_FF ** -0.5),
        'hy_b2': nrm((N_EVEN, HY_FF), 0.1),
        'hy_w3': nrm((N_EVEN, HY_FF, HY_ORDER * 2 * W_B), 0.1 * HY_FF ** -0.5),
        'hy_freq': 1.0 + nrm((N_EVEN, HY_FF), 0.1),
        'hy_d': nrm((N_EVEN, HY_ORDER, W_B), 0.5),
        'w_in_odd': nrm((N_ODD, D, D_IN_ODD), D ** -0.5),
        'w_out_odd': nrm((N_ODD, W_C + W_D, D), (W_C + W_D) ** -0.5),
        'diff_lambda': nrm((N_ODD, 4, DH_C), 0.1),
        'diff_norm': gain((N_ODD, W_C)),
        'gla_aw': nrm((N_ODD, 2, GLA_RANK, H_D * DK_D), GLA_RANK ** -0.5),
        'gla_ab': nrm((N_ODD, 2, H_D * DK_D), 0.01),
        'gla_norm': gain((N_ODD, W_D)),
    }


def reference(x_prompt, x_sample, state_hgrn, cache_diff_k, cache_diff_v, state_gla, c, c_ctx,
              ada_w, ada_b, norm_g, ffn_up, ffn_conv_w, ffn_conv_b, ffn_down,
              w_in_even, w_out_even, hgrn_lb, hgrn_norm, hy_conv_w, hy_conv_b,
              hy_w1, hy_b1, hy_w2, hy_b2, hy_w3, hy_freq, hy_d,
              w_in_odd, w_out_odd, diff_lambda, diff_norm, gla_aw, gla_ab, gla_norm):
    lb_all = jnp.cumsum(jax.nn.softmax(hgrn_lb.astype(F32), axis=0), axis=0)
    yp, ys = x_prompt, x_sample
    hg_states, cache_ks, cache_vs, gla_states = [], [], [], []
    for l in range(DEPTH):
        mod_p = (jax.nn.silu(c_ctx) @ ada_w[l] + ada_b[l]).reshape(-1, 1, 6, D_MODEL)
        mod_s = (jax.nn.silu(c) @ ada_w[l] + ada_b[l]).reshape(-1, 1, 6, D_MODEL)
        hp = modulate(rms_norm(yp, norm_g[l, 0]), mod_p, 0)
        hs = modulate(rms_norm(ys, norm_g[l, 0]), mod_s, 0)
        if l % 2 == 0:
            e = l // 2
            p = {'w_in': w_in_even[e], 'w_out': w_out_even[e], 'lb': lb_all[l], 'hgrn_norm': hgrn_norm[e],
                 'hy_conv_w': hy_conv_w[e], 'hy_conv_b': hy_conv_b[e], 'hy_w1': hy_w1[e], 'hy_b1': hy_b1[e],
                 'hy_w2': hy_w2[e], 'hy_b2': hy_b2[e], 'hy_w3': hy_w3[e], 'hy_freq': hy_freq[e], 'hy_d': hy_d[e]}
            zero = jnp.zeros((yp.shape[0], 2, H_A, DK_A, DV_A), yp.dtype)
            mp, st = even_mixer(hp, zero, p)
            ms, _ = even_mixer(hs, state_hgrn[:, e], p)
            hg_states.append(st)
        else:
            o = l // 2
            p = {'w_in': w_in_odd[o], 'w_out': w_out_odd[o], 'diff_lambda': diff_lambda[o],
                 'diff_norm': diff_norm[o], 'gla_aw': gla_aw[o], 'gla_ab': gla_ab[o], 'gla_norm': gla_norm[o]}
            zero = jnp.zeros((yp.shape[0], 2, H_D, DK_D, DV_D), yp.dtype)
            mp, (kc, vc), st = odd_mixer(hp, zero, None, None, l, p)
            ms, _, _ = odd_mixer(hs, state_gla[:, o], cache_diff_k[:, o], cache_diff_v[:, o], l, p)
            cache_ks.append(kc)
            cache_vs.append(vc)
            gla_states.append(st)
        yp = yp + mod_p[:, :, 2] * rms_norm(mp, norm_g[l, 1])
        ys = ys + mod_s[:, :, 2] * rms_norm(ms, norm_g[l, 1])
        hp = modulate(rms_norm(yp, norm_g[l, 2]), mod_p, 1)
        hs = modulate(rms_norm(ys, norm_g[l, 2]), mod_s, 1)
        yp = yp + mod_p[:, :, 5] * rms_norm(conv_ffn(hp, ffn_up[l], ffn_conv_w[l], ffn_conv_b[l], ffn_down[l]), norm_g[l, 3])
        ys = ys + mod_s[:, :, 5] * rms_norm(conv_ffn(hs, ffn_up[l], ffn_conv_w[l], ffn_conv_b[l], ffn_down[l]), norm_g[l, 3])
    new_state_hgrn = jnp.stack(hg_states, axis=1)
    new_cache_diff_k = jnp.stack(cache_ks, axis=1)
    new_cache_diff_v = jnp.stack(cache_vs, axis=1)
    new_state_gla = jnp.stack(gla_states, axis=1)
    return (yp, ys, new_state_hgrn, new_cache_diff_k, new_cache_diff_v, new_state_gla)
```

```python
import contextlib
import math
import numpy as np
import ml_dtypes
import concourse.bass as bass
import concourse.mybir as mybir
from concourse.bass_utils import run_bass_kernel_spmd

F32 = mybir.dt.float32
BF16 = mybir.dt.bfloat16
AF = mybir.ActivationFunctionType
ALU = mybir.AluOpType
NPBF = ml_dtypes.bfloat16

N_DMA_SEMS = 72
D = 1024
DFF = 2816
EPS = 1e-6
CH = 32
GROUPS = {"p": dict(nseq=4, L=256, T=1024), "s": dict(nseq=1, L=2048, T=2048)}


class Sched:
    ENG = ("pe", "act", "dve", "pool", "sp")

    def __init__(self, nc, same_engine_sync=True):
        self.nc = nc
        self.sem = {e: nc.alloc_semaphore(name=f"cnt_{e}") for e in self.ENG}
        self.dsem = [nc.alloc_semaphore(name=f"dma_{i}") for i in range(N_DMA_SEMS)]
        self.dval = [0] * N_DMA_SEMS
        self.dnext = 0
        self.cnt = {e: 0 for e in self.ENG}
        self.seen = {e: {} for e in self.ENG}
        self.snap = {e: [None] for e in self.ENG}
        self.last_w = {}
        self.readers = {}
        self.same_engine_sync = same_engine_sync
        self.n_wait = 0
        self.n_ins = 0
        self.engs = {"pe": nc.tensor, "act": nc.scalar, "dve": nc.vector, "pool": nc.gpsimd, "sp": nc.sync}

    def _need(self, e, ev, waits, force=False):
        if ev is None:
            return
        kind, sname, semh, val = ev
        if kind == "eng" and sname == e and not force:
            if e == "pe" or not self.same_engine_sync:
                return
        if self.seen[e].get(sname, 0) >= val:
            return
        cur = waits.get(sname)
        if cur is None or cur[1] < val:
            waits[sname] = (semh, val, kind)

    def _emit_waits(self, e, waits):
        for sname, (semh, val, kind) in waits.items():
            self.engs[e].wait_ge(semh, val)
            self.n_wait += 1
            self.seen[e][sname] = val
            if kind == "eng":
                sn = self.snap[sname][val]
                if sn:
                    se = self.seen[e]
                    for k, v in sn.items():
                        if se.get(k, 0) < v:
                            se[k] = v

    def _collect(self, e, reads, writes, force=False):
        waits = {}
        for k in reads:
            self._need(e, self.last_w.get(k), waits, force)
        for k in writes:
            self._need(e, self.last_w.get(k), waits, force)
            rd = self.readers.get(k)
            if rd:
                for ev in rd.values():
                    self._need(e, ev, waits, force)
        return waits

    def _record(self, ev, reads, writes):
        sname = ev[1]
        for k in reads:
            self.readers.setdefault(k, {})[sname] = ev
        for k in writes:
            self.last_w[k] = ev
            self.readers[k] = {}

    def op(self, e, fn, reads=(), writes=()):
        self._emit_waits(e, self._collect(e, reads, writes))
        self.cnt[e] += 1
        idx = self.cnt[e]
        fn(self.engs[e]).then_inc(self.sem[e], 1)
        self.snap[e].append(dict(self.seen[e]))
        self.n_ins += 1
        ev = ("eng", e, self.sem[e], idx)
        self._record(ev, reads, writes)
        return ev

    def dma(self, q, out, in_, reads=(), writes=(), **kw):
        s = self.dnext
        self.dnext = (self.dnext + 1) % N_DMA_SEMS
        sname = f"d{s}"
        semh = self.dsem[s]
        waits = self._collect(q, reads, writes, True)
        if self.dval[s] > 0:
            self._need(q, ("dma", sname, semh, self.dval[s]), waits)
        self._emit_waits(q, waits)
        self.dval[s] += 16
        self.engs[q].dma_start(out=out, in_=in_, **kw).then_inc(semh, 16)
        self.n_ins += 1
        ev = ("dma", sname, semh, self.dval[s])
        self._record(ev, reads, writes)
        return ev

    def barrier(self, engines=None):
        for e in (engines or self.ENG):
            waits = {}
            for s_ in range(N_DMA_SEMS):
                if self.dval[s_] > 0:
                    self._need(e, ("dma", f"d{s_}", self.dsem[s_], self.dval[s_]), waits, True)
            for f in self.ENG:
                if f != e and self.cnt[f] > 0:
                    self._need(e, ("eng", f, self.sem[f], self.cnt[f]), waits, True)
            self._emit_waits(e, waits)


_CONST_CACHE = {}


def _dft_tables(L):
    n = 2 * L
    SC = L // 128
    NF = SC + 1
    NT = max(1, L // 512)
    TW = min(L, 512)
    s = np.arange(L, dtype=np.float64)
    f = np.arange(NF * 128, dtype=np.float64)
    valid = (f <= L)
    ang = 2.0 * np.pi * np.outer(s, f) / n
    Ac = np.cos(ang) * valid[None]
    As = -np.sin(ang) * valid[None]
    wf = np.where((f == 0) | (f == L), 1.0, 2.0) * valid / n
    Bc = (np.cos(ang) * wf[None]).T
    Bs = (-np.sin(ang) * wf[None]).T
    def fwd(A):
        return np.ascontiguousarray(A.reshape(SC, 128, NF, 128).transpose(2, 1, 0, 3)).astype(NPBF)
    def inv(B):
        return np.ascontiguousarray(B.reshape(NF, 128, NT, TW).transpose(2, 1, 0, 3)).astype(NPBF)
    return fwd(Ac), fwd(As), inv(Bc), inv(Bs)


def _hyena_pos(L):
    t = np.linspace(0.0, 1.0, L, dtype=np.float32)[:, None]
    w = (2.0 * np.float32(math.pi) * np.arange(L, dtype=np.float32)[:, None] / np.float32(L)).astype(np.float32)
    fb = np.linspace(1e-4, 15, 16, dtype=np.float32)[None]
    z = np.concatenate([t, np.cos(fb * w), -np.sin(fb * w)], axis=-1).astype(np.float32)
    max_decay = math.log(1e-2) / 0.3
    min_decay = math.log(1e-2) / 1.5
    deltas = np.abs(np.linspace(min_decay, max_decay, 512, dtype=np.float32))
    window = np.exp(-t * deltas[None]).astype(np.float32)
    w0 = window.copy()
    w1 = window.copy()
    w1[0] = 0.0
    SC = L // 128
    lay = lambda a: np.ascontiguousarray(a.reshape(SC, 128, 512).transpose(1, 0, 2))
    return np.ascontiguousarray(z.T), lay(w0), lay(w1)


def make_consts():
    if _CONST_CACHE:
        return _CONST_CACHE
    c = {}
    c["ident_f"] = np.eye(128, dtype=np.float32)
    c["ident_b"] = np.eye(128).astype(NPBF)
    c["ones_b"] = np.ones((128, 128)).astype(NPBF)
    t = np.arange(2048)
    c["mask_f"] = np.broadcast_to((t % CH != 0).astype(np.float32), (128, 2048)).copy()
    c["mask_b"] = np.broadcast_to((t % CH != CH - 1).astype(np.float32), (128, 2048)).copy()
    s_ = np.arange(128)[:, None]
    t_ = np.arange(128)[None, :]
    same = (s_ // CH) == (t_ // CH)
    c["cm_f"] = (same & (s_ <= t_)).astype(np.float32)
    c["cm_b"] = (same & (s_ >= t_)).astype(np.float32)
    c["ind4"] = ((np.arange(128)[:, None] // CH) == np.arange(4)[None, :]).astype(np.float32)
    c["mask128_f"] = np.broadcast_to((t % 128 != 0).astype(np.float32), (128, 2048)).copy()
    c["mask128_b"] = np.broadcast_to((t % 128 != 127).astype(np.float32), (128, 2048)).copy()
    c["cm128_f"] = (s_ <= t_).astype(np.float32)
    c["cm128_b"] = (s_ >= t_).astype(np.float32)
    for g, L in (("p", 256), ("s", 2048)):
        Ac, As, Bc, Bs = _dft_tables(L)
        c[f"Ac_{g}"], c[f"As_{g}"], c[f"Bc_{g}"], c[f"Bs_{g}"] = Ac, As, Bc, Bs
        zT, w0, w1 = _hyena_pos(L)
        c[f"zT_{g}"], c[f"win0_{g}"], c[f"win1_{g}"] = zT, w0, w1
    _CONST_CACHE.update(c)
    return c


class KB:
    def __init__(self, dbg=None, nlayers=2):
        self.nc = bass.Bass("TRN2", target_bir_lowering=False)
        self.S = Sched(self.nc)
        self.dbg = dbg
        self.nlayers = nlayers
        self.gstack = contextlib.ExitStack()
        self.in_shapes = {}
        self.out_names = []
        self.uid = 0
        self.marks = []

    def din(self, name, shape, dt=F32):
        self.in_shapes[name] = (tuple(shape), dt)
        return self.nc.dram_tensor(name, list(shape), dt, kind="ExternalInput").ap()

    def dout(self, name, shape, dt=F32):
        self.out_names.append(name)
        return self.nc.dram_tensor(name, list(shape), dt, kind="ExternalOutput").ap()

    def dscr(self, name, shape, dt=F32):
        return self.nc.dram_tensor(name, list(shape), dt, kind="Internal").ap()

    def gsb(self, name, shape, dt=F32):
        return self.gstack.enter_context(self.nc.sbuf_tensor(name, list(shape), dt))

    def gps(self, name, shape, dt=F32):
        return self.gstack.enter_context(self.nc.psum_tensor(name, list(shape), dt))

    @contextlib.contextmanager
    def scope(self):
        st = contextlib.ExitStack()
        kb = self

        class Sc:
            def sb(self_, name, shape, dt=F32):
                kb.uid += 1
                return st.enter_context(kb.nc.sbuf_tensor(f"{name}_{kb.uid}", list(shape), dt))
        try:
            yield Sc()
        finally:
            self.S.barrier()
            st.close()

    def dump(self, name, src_ap, shape, dt, keys):
        o = self.dout("dbg_" + name, shape, dt)
        self.dma("sp", o, src_ap, keys, ["dbg_" + name])

    def act(self, out, in_, func, r, w, **kw):
        return self.S.op("act", lambda e: e.activation(out=out, in_=in_, func=func, **kw), r, w)

    def tt(self, eng, out, a, b, op, r, w):
        return self.S.op(eng, lambda e: e.tensor_tensor(out=out, in0=a, in1=b, op=op), r, w)

    def ts(self, eng, out, a, s1, s2, op0, op1, r, w):
        if op1 is None:
            return self.S.op(eng, lambda e: e.tensor_scalar(out=out, in0=a, scalar1=s1, scalar2=None, op0=op0), r, w)
        return self.S.op(eng, lambda e: e.tensor_scalar(out=out, in0=a, scalar1=s1, scalar2=s2, op0=op0, op1=op1), r, w)

    def stt(self, out, in0, scalar, in1, op0, op1, r, w):
        return self.S.op("dve", lambda e: e.scalar_tensor_tensor(out=out, in0=in0, scalar=scalar, in1=in1, op0=op0, op1=op1), r, w)

    def cp(self, eng, out, in_, r, w):
        if eng == "act":
            return self.S.op("act", lambda e: e.copy(out=out, in_=in_), r, w)
        return self.S.op(eng, lambda e: e.tensor_copy(out=out, in_=in_), r, w)

    def mm(self, out, lhsT, rhs, start, stop, r, w):
        return self.S.op("pe", lambda e: e.matmul(out, lhsT=lhsT, rhs=rhs, start=start, stop=stop), r, w)

    def tr(self, out, in_, ident, r, w):
        return self.S.op("pe", lambda e: e.transpose(out, in_, ident), r, w)

    def dma(self, q, out, in_, r, w, **kw):
        return self.S.dma(q, out, in_, r, w, **kw)

    def build(self):
        nc, S = self.nc, self.S
        NL = self.nlayers
        I = {}
        I["xT_p"] = self.din("xT_p", [D, 1024])
        I["xT_s"] = self.din("xT_s", [D, 2048])
        I["cT"] = self.din("cT", [128, 8, 2])
        I["ada_w"] = self.din("ada_w", [2, D, 6 * D])
        I["ada_bT"] = self.din("ada_bT", [128, 2, 48])
        I["norm_gT"] = self.din("norm_gT", [128, 2, 4, 8])
        I["ffn_up"] = self.din("ffn_up", [2, D, 2 * DFF])
        I["ffn_cwT"] = self.din("ffn_cwT", [128, 2, 44, 3])
        I["ffn_cbT"] = self.din("ffn_cbT", [128, 2, 44])
        I["ffn_down"] = self.din("ffn_down", [2, DFF, D])
        I["w_in_e"] = self.din("w_in_e", [D, 4096])
        I["w_out_e"] = self.din("w_out_e", [D, D])
        I["lbT"] = self.din("lbT", [128, 3, 8])
        I["hgrn_normT"] = self.din("hgrn_normT", [128, 4])
        I["hy_cwT"] = self.din("hy_cwT", [128, 12, 3])
        I["hy_cbT"] = self.din("hy_cbT", [128, 12])
        I["hy_w1"] = self.din("hy_w1", [33, 64])
        I["hy_b1"] = self.din("hy_b1", [64, 1])
        I["hy_w2"] = self.din("hy_w2", [64, 64])
        I["hy_b2"] = self.din("hy_b2", [64, 1])
        I["hy_w3"] = self.din("hy_w3", [64, 2048])
        I["hy_freq"] = self.din("hy_freq", [64, 1])
        I["hy_dT"] = self.din("hy_dT", [128, 2, 4])
        I["st_hgrn"] = self.din("st_hgrn", [128, 2, 4, 128])
        I["w_in_o"] = self.din("w_in_o", [D, 3104])
        I["w_out_o"] = self.din("w_out_o", [D, D])
        I["dlam"] = self.din("dlam", [128, 4, 64])
        I["diff_normT"] = self.din("diff_normT", [128, 4])
        I["gla_aw"] = self.din("gla_aw", [2, 16, 256])
        I["gla_abT"] = self.din("gla_abT", [128, 2, 2])
        I["gla_normT"] = self.din("gla_normT", [128, 4])
        I["ck"] = self.din("ck", [4, 256, 128])
        I["cv"] = self.din("cv", [4, 256, 128])
        I["st_gla"] = self.din("st_gla", [64, 2, 4, 128])
        C = {}
        for k, v in make_consts().items():
            C[k] = self.din("c_" + k, v.shape, BF16 if v.dtype == NPBF else F32)
        C["ropeC"] = self.din("c_ropeC", [128, 2048])
        C["ropeS"] = self.din("c_ropeS", [128, 2048])
        C["ropeP"] = self.din("c_ropeP", [128, 128])
        self.I, self.C = I, C
        O = {}
        O["yT_p"] = self.dout("yT_p", [D, 1024])
        O["yT_s"] = self.dout("yT_s", [D, 2048])
        O["nst_hgrn"] = self.dout("nst_hgrn", [4, 2, 4, 128, 128])
        O["nck"] = self.dout("nck", [4, 4, 256, 128])
        O["ncv"] = self.dout("ncv", [4, 4, 256, 128])
        O["nst_gla"] = self.dout("nst_gla", [4, 2, 4, 64, 128])
        self.O = O
        X = {}
        for g, gi in GROUPS.items():
            T = gi["T"]
            X[f"y_{g}"] = self.dscr(f"y_{g}", [D, T])
            X[f"proj_{g}"] = self.dscr(f"proj_{g}", [4096, T])
            X[f"vtok_{g}"] = self.dscr(f"vtok_{g}", [T, 1536])
            X[f"mix_{g}"] = self.dscr(f"mix_{g}", [D, T], BF16)
            X[f"ffa_{g}"] = self.dscr(f"ffa_{g}", [DFF, T], BF16)
            X[f"hyc_{g}"] = self.dscr(f"hyc_{g}", [1536, T])
            X[f"ksp_{g}"] = self.dscr(f"ksp_{g}", [2, 2, gi["L"] // 128 + 1, 128, 512], BF16)
        self.X = X
        self.ident_f = self.gsb("ident_f", [128, 128])
        self.ident_b = self.gsb("ident_b", [128, 128], BF16)
        self.ones_b = self.gsb("ones_b", [128, 128], BF16)
        self.cm_f = self.gsb("cm_f", [128, 128])
        self.cm_b = self.gsb("cm_b", [128, 128])
        self.ind4 = self.gsb("ind4", [128, 4])
        self.cm128_f = self.gsb("cm128_f", [128, 128])
        self.cm128_b = self.gsb("cm128_b", [128, 128])
        self.ones_col = self.gsb("ones_col", [128, 1])
        self.eps_col = self.gsb("eps_col", [128, 1])
        self.mod = [self.gsb(f"mod{l}", [128, 48, 2]) for l in range(2)]
        self.normg = self.gsb("normg", [128, 2, 4, 8])
        self.gs = self.gsb("gs", [128, 2, 2, 8, 2])
        self.gg = self.gsb("gg", [128, 2, 2, 8, 2])
        self.ps = [self.gps(f"ps{i}", [128, 512]) for i in range(8)]
        self.psT = self.ps[7][:].bitcast(BF16)
        self.ps_rot = 0
        for nm in ("ident_f", "ident_b", "ones_b", "cm_f", "cm_b", "ind4", "cm128_f", "cm128_b"):
            self.dma("sp", getattr(self, nm)[:], C[nm], [], [nm])
        S.op("dve", lambda e: e.memset(self.ones_col[:], 1.0), [], ["ones_col"])
        S.op("dve", lambda e: e.memset(self.eps_col[:], EPS), [], ["eps_col"])
        self.dma("sp", self.normg[:], I["norm_gT"], [], ["normg"])

        self.bg = None
        self.mods_setup()
        with self.scope() as scm:
            aw = [scm.sb(f"aw{i}", [128, 6 * D]) for i in range(2)]
            for _ in self.mods_gen(0, aw):
                pass
            if self.dbg == "mods" and NL > 1:
                for _ in self.mods_gen(1, aw):
                    pass
        if self.dbg == "mods":
            self.dump("mod0", self.mod[0][:], [128, 48, 2], F32, ["mod0"])
            self.dump("gs", self.gs[:], [128, 2, 2, 8, 2], F32, ["gs"])
        else:
            for l in range(NL):
                for g in ("p", "s"):
                    self.layer(l, g)
        self.mark("final")
        if self.dbg is not None:
            for g in ("p", "s"):
                self.dma("sp", O[f"yT_{g}"], X[f"y_{g}"], self.ykeys(g), [f"out_y_{g}"])
        S.barrier(["sp"])
        self.gstack.close()
        return nc

    def next_ps(self):
        i = self.ps_rot
        self.ps_rot = (self.ps_rot + 1) % 7
        return self.ps[i], f"ps{i}"

    def mods_setup(self):
        I = self.I
        self.m_cT = self.gsb("m_cT", [128, 8, 2])
        self.m_sT = self.gsb("m_sT", [128, 8, 2])
        self.m_ab = self.gsb("m_ab", [128, 2, 48])
        self.dma("sp", self.m_cT[:], I["cT"], [], ["cT"])
        self.dma("sp", self.m_ab[:], I["ada_bT"], [], ["ab"])
        self.act(self.m_sT[:], self.m_cT[:], AF.Silu, ["cT"], ["sT"])

    def mods_gen(self, l, aw):
        I = self.I
        sT, ab = self.m_sT, self.m_ab
        mod = self.mod[l]
        for j in range(8):
            a = aw[j % 2]
            self.dma("sp", a[:], I["ada_w"][l, j * 128:(j + 1) * 128, :], [], [f"aw{j % 2}"])
            yield
            ps, pk = self.ps[0], "ps0"
            for ch in range(48):
                self.mm(ps[:, ch * 2:ch * 2 + 2], a[:, ch * 128:(ch + 1) * 128], sT[:, j, :], True, True,
                        [f"aw{j % 2}", "sT"], [pk])
            pv = ps[:, 0:96].rearrange("p (c k) -> p c k", k=2)
            if j == 0:
                self.cp("dve", mod[:], pv, [pk], [f"mod{l}"])
            else:
                self.tt("dve", mod[:], mod[:], pv, ALU.add, [pk, f"mod{l}"], [f"mod{l}"])
            yield
        for cnd in range(2):
            self.tt("dve", mod[:, :, cnd], mod[:, :, cnd], ab[:, l, :], ALU.add, ["ab", f"mod{l}"], [f"mod{l}"])
        for which, (nidx, sc_lo, gnidx, gate_lo) in enumerate(((0, 8, 1, 16), (2, 32, 3, 40))):
            for cnd in range(2):
                self.stt(self.gs[:, l, which, :, cnd], mod[:, sc_lo:sc_lo + 8, cnd], 1.0, self.normg[:, l, nidx, :],
                         ALU.add, ALU.mult, [f"mod{l}", "normg"], [f"gs{l}"])
                self.tt("dve", self.gg[:, l, which, :, cnd], mod[:, gate_lo:gate_lo + 8, cnd], self.normg[:, l, gnidx, :],
                        ALU.mult, [f"mod{l}", "normg"], [f"gg{l}"])
        yield

    def bg_step(self, n=1):
        for _ in range(n):
            if self.bg is not None:
                try:
                    next(self.bg)
                except StopIteration:
                    self.bg = None

    def rstd_from_ps(self, rstd, ps, pk, n, r_extra, wkey, cols=512):
        self.act(rstd, ps, AF.Ln, [pk] + r_extra, [wkey], scale=1.0 / n, bias=self.eps_col[:, 0:1])
        self.act(rstd, rstd, AF.Exp, [wkey], [wkey], scale=-0.5)

    def norm_mod(self, sc, g, l, which, src_ap, src_key, on_tile=None):
        T = GROUPS[g]["T"]
        NT_ = T // 512
        cnd = 0 if g == "p" else 1
        ysb = [sc.sb(f"ysb{i}", [128, 8, 512]) for i in range(2)]
        sq = [sc.sb(f"sq{i}", [128, 8, 512], BF16) for i in range(2)]
        tmp = sc.sb("tmp", [128, 8, 512])
        rstd = [sc.sb(f"rstd{i}", [128, 512]) for i in range(2)]
        shift_lo = 0 if which == 0 else 24
        srcv = src_ap.rearrange("(j p) t -> p j t", p=128)

        def stats(tt):
            b_ = tt % 2
            tsl = slice(tt * 512, (tt + 1) * 512)
            self.dma("sp", ysb[b_][:], srcv[:, :, tsl], [src_key if src_key.startswith("xT") else f"{src_key}_{tt}"], [f"ysb{b_}"])
            self.act(sq[b_][:], ysb[b_][:], AF.Square, [f"ysb{b_}"], [f"sq{b_}"])
            ps, pk = self.next_ps()
            for j in range(8):
                self.mm(ps[:], self.ones_b[:], sq[b_][:, j, :], j == 0, j == 7, ["ones_b", f"sq{b_}"], [pk])
            self.rstd_from_ps(rstd[b_][:], ps[:], pk, D, [], f"rstd{b_}")

        stats(0)
        for tt in range(NT_):
            b_ = tt % 2
            tsl = slice(tt * 512, (tt + 1) * 512)
            if tt + 1 < NT_:
                stats(tt + 1)
            for j in range(8):
                self.stt(tmp[:, j, :], ysb[b_][:, j, :], self.gs[:, l, which, j, cnd:cnd + 1], rstd[b_][:], ALU.mult, ALU.mult,
                         [f"ysb{b_}", f"gs{l}", f"rstd{b_}"], [f"tmp{j}"])
                self.act(self.hT[:, j, tsl], tmp[:, j, :], AF.Identity, [f"tmp{j}", f"mod{l}"], [f"hT{tt}"],
                         bias=self.mod[l][:, shift_lo + j, cnd:cnd + 1], scale=1.0)
            if on_tile is not None:
                on_tile(tt)

    def load_w(self, wb, wkey, w_ap, KC, c0, ncols):
        wv = w_ap.rearrange("(k p) n -> p k n", p=128)
        step = 4
        for k0 in range(0, KC, step):
            k1 = min(KC, k0 + step)
            self.dma("pool", wb[:, k0:k1, 0:ncols], wv[:, k0:k1, c0:c0 + ncols], [], [wkey])

    def res_tiles(self, sc):
        return [dict(msb=sc.sb(f"msb{i}", [128, 8, 512]), sq=sc.sb(f"rsq{i}", [128, 8, 512], BF16),
                     rstd=sc.sb(f"rrstd{i}", [128, 512]), ysb=sc.sb(f"rysb{i}", [128, 8, 512]), i=i) for i in range(2)]

    def ykeys(self, g):
        return [f"y_{g}_{i}" for i in range(GROUPS[g]["T"] // 512)]

    def residual_load(self, g, ts_, ysrc_ap, ysrc_key, tt):
        i = ts_["i"]
        srcv = ysrc_ap.rearrange("(j p) t -> p j t", p=128)
        tsl = slice(tt * 512, (tt + 1) * 512)
        self.dma("sp", ts_["ysb"][:], srcv[:, :, tsl], [ysrc_key if ysrc_key.startswith("xT") else f"y_{g}_{tt}"], [f"rysb{i}"])

    def residual(self, g, l, which, ts_, ysrc_ap, ysrc_key, tt):
        cnd = 0 if g == "p" else 1
        i = ts_["i"]
        msb, sq, rstd, ysb = ts_["msb"], ts_["sq"], ts_["rstd"], ts_["ysb"]
        mk, sk, rk, yk = f"msb{i}", f"rsq{i}", f"rrstd{i}", f"rysb{i}"
        tsl = slice(tt * 512, (tt + 1) * 512)
        last = (l == self.nlayers - 1 and which == 1)
        dstv = (self.O[f"yT_{g}"] if last else self.X[f"y_{g}"]).rearrange("(j p) t -> p j t", p=128)
        ykey = f"out_y_{g}_{tt}" if last else f"y_{g}_{tt}"
        self.act(sq[:], msb[:], AF.Square, [mk], [sk])
        ps, pk = self.next_ps()
        for j in range(8):
            self.mm(ps[:], self.ones_b[:], sq[:, j, :], j == 0, j == 7, ["ones_b", sk], [pk])
        self.rstd_from_ps(rstd[:], ps[:], pk, D, [], rk)
        for j in range(8):
            self.stt(msb[:, j, :], msb[:, j, :], self.gg[:, l, which, j, cnd:cnd + 1], rstd[:], ALU.mult, ALU.mult,
                     [mk, f"gg{l}", rk], [f"{mk}_{j}"])
            self.tt("pool", msb[:, j, :], msb[:, j, :], ysb[:, j, :], ALU.add, [f"{mk}_{j}", yk], [f"{mk}_{j}"])
        self.dma("sp", dstv[:, :, tsl], msb[:], [f"{mk}_{j}" for j in range(8)], [ykey])

    def mark(self, name):
        self.marks.append((name, dict(self.S.cnt)))

    def layer(self, l, g):
        X, I = self.X, self.I
        T = GROUPS[g]["T"]
        self.mark(f"L{l}{g} norm+proj")
        ysrc = (I[f"xT_{g}"], f"xT_{g}") if l == 0 else (X[f"y_{g}"], f"y_{g}")
        with self.scope() as sc:
            self.hT = sc.sb("hT", [128, 8, T], BF16)
            nrm = (l, 0, ysrc[0], ysrc[1])
            if l % 2 == 0:
                self.proj_in(sc, g, I["w_in_e"], 4096, tok_blocks={3: 0}, norm=nrm)
            else:
                self.proj_in(sc, g, I["w_in_o"], 3104, tok_blocks={1: 0, 2: 512, 4: 1024}, norm=nrm)
            if self.dbg == "proj":
                self.dump(f"hT_{g}", self.hT[:, :, 0:T], [128, 8, T], BF16, [f"hT{i}" for i in range(T // 512)])
                self.dump(f"proj_{g}", X[f"proj_{g}"], [4096, T], F32, [f"proj_{g}"])
                self.dump(f"vtok_{g}", X[f"vtok_{g}"], [T, 1536], F32, [f"vtok_{g}"])
        if self.dbg == "proj":
            return
        self.mark(f"L{l}{g} mixerA")
        if l % 2 == 0:
            if g == "p" and self.nlayers > 1:
                with self.scope() as scm:
                    aw = [scm.sb(f"aw{i}", [128, 6 * D]) for i in range(2)]
                    self.bg = self.mods_gen(1, aw)
                    self.hgrn(g)
                    self.bg_step(100)
            else:
                self.hgrn(g)
            if self.dbg == "hgrn":
                return
            self.mark(f"L{l}{g} mixerB")
            self.hyena(g)
            if self.dbg == "hyena":
                return
            w_out = I["w_out_e"]
        else:
            self.diffattn(g)
            if self.dbg == "diff":
                return
            self.mark(f"L{l}{g} mixerB")
            self.gla(g)
            if self.dbg == "gla":
                return
            w_out = I["w_out_o"]
        self.mark(f"L{l}{g} outproj")
        with self.scope() as sc:
            self.hT = sc.sb("hT", [128, 8, T], BF16)
            wb = sc.sb("wout", [128, 8, D], BF16)
            self.load_w(wb, "wout", w_out, 8, 0, D)
            self.dma("sp", self.hT[:, :, 0:T], X[f"mix_{g}"].rearrange("(j p) t -> p j t", p=128), [f"mix_{g}"], [f"hT{i}" for i in range(T // 512)])
            rts = self.res_tiles(sc)
            self.residual_load(g, rts[0], ysrc[0], ysrc[1], 0)
            for tt in range(T // 512):
                tsl = slice(tt * 512, (tt + 1) * 512)
                ts_ = rts[tt % 2]
                if tt + 1 < T // 512:
                    self.residual_load(g, rts[(tt + 1) % 2], ysrc[0], ysrc[1], tt + 1)
                for m in range(8):
                    ps, pk = self.next_ps()
                    for k in range(8):
                        self.mm(ps[:], wb[:, k, m * 128:(m + 1) * 128], self.hT[:, k, tsl], k == 0, k == 7, ["wout", f"hT{tt}"], [pk])
                    self.cp("act" if m % 2 else "dve", ts_["msb"][:, m, :], ps[:], [pk], [f"msb{tt % 2}", f"msb{tt % 2}_{m}"])
                self.residual(g, l, 0, ts_, ysrc[0], ysrc[1], tt)
        self.mark(f"L{l}{g} ffn")
        with self.scope() as scA:
            wd = scA.sb("wd", [128, 22, D], BF16)
            with self.scope() as scB:
                self.hT = scB.sb("hT", [128, 8, T], BF16)
                with self.scope() as scC:
                    self.norm_mod(scC, g, l, 1, X[f"y_{g}"], f"y_{g}")
                with self.scope() as scD:
                    self.ffn_up(scD, g, l, wd)
            self.mark(f"L{l}{g} ffn_down")
            with self.scope() as scE:
                self.ffn_down(scE, g, l, wd)

    def proj_in(self, sc, g, w_ap, ncols, tok_blocks, norm=None):
        T = GROUPS[g]["T"]
        X = self.X
        wbs = [sc.sb(f"wb{i}", [128, 8, 512], BF16) for i in range(2)]
        stg = [sc.sb(f"stg{i}", [128, 512]) for i in range(4)]
        st_ = {"si": 0}
        nblk = (ncols + 511) // 512

        def fm_tile(cb, tt):
            c0 = cb * 512
            nc_ = min(512, ncols - c0)
            wb, wk = wbs[cb % 2], f"wb{cb % 2}"
            tsl = slice(tt * 512, (tt + 1) * 512)
            for m in range((nc_ + 127) // 128):
                mw = min(128, nc_ - m * 128)
                ps, pk = self.next_ps()
                for k in range(8):
                    self.mm(ps[0:mw, :], wb[:, k, m * 128:m * 128 + mw], self.hT[:, k, tsl], k == 0, k == 7, [wk, f"hT{tt}"], [pk])
                si = st_["si"]
                st, sk = stg[si % 4], f"stg{si % 4}"
                self.cp("act" if si % 2 else "dve", st[0:mw, :], ps[0:mw, :], [pk], [sk])
                st_["si"] += 1
                self.dma("sp", X[f"proj_{g}"][c0 + m * 128:c0 + m * 128 + mw, tsl], st[0:mw, :], [sk], [f"proj_{g}"])

        self.load_w(wbs[0], "wb0", w_ap, 8, 0, min(512, ncols))
        if norm is not None:
            l, which, src_ap, src_key = norm
            self.norm_mod(sc, g, l, which, src_ap, src_key, on_tile=lambda tt: fm_tile(0, tt))
        for cb in range(nblk):
            c0 = cb * 512
            nc_ = min(512, ncols - c0)
            wb, wk = wbs[cb % 2], f"wb{cb % 2}"
            if cb > 0:
                self.load_w(wb, wk, w_ap, 8, c0, nc_)
            if cb > 0 or norm is None:
                for tt in range(T // 512):
                    fm_tile(cb, tt)
            if cb in tok_blocks and tok_blocks[cb] is not None:
                off = tok_blocks[cb]
                for tb in range(T // 128):
                    ps, pk = self.next_ps()
                    for k in range(8):
                        self.mm(ps[:], self.hT[:, k, tb * 128:(tb + 1) * 128], wb[:, k, :], k == 0, k == 7, [wk, f"hT{tb // 4}"], [pk])
                    si = st_["si"]
                    st, sk = stg[si % 4], f"stg{si % 4}"
                    self.cp("act" if si % 2 else "dve", st[:], ps[:], [pk], [sk])
                    st_["si"] += 1
                    self.dma("sp", X[f"vtok_{g}"][tb * 128:(tb + 1) * 128, off:off + 512], st[:], [sk], [f"vtok_{g}"])

    def ffn_up(self, sc, g, l, wd):
        I, X = self.I, self.X
        gi = GROUPS[g]
        T, L, nseq = gi["T"], gi["L"], gi["nseq"]
        wbs = [sc.sb(f"wu{i}", [128, 8, 256], BF16) for i in range(2)]
        cw = sc.sb("cw", [128, 44, 3])
        cb = sc.sb("cb", [128, 44])
        self.dma("sp", cw[:], I["ffn_cwT"][:, l], [], ["cw"])
        self.dma("sp", cb[:], I["ffn_cbT"][:, l], [], ["cb"])
        raw4 = [sc.sb(f"raw{i}", [128, T]) for i in range(4)]
        cv4 = [sc.sb(f"cv{i}", [128, T]) for i in range(4)]
        obs = [sc.sb(f"ffo{i}", [128, T], BF16) for i in range(2)]
        wv = I["ffn_up"][l].rearrange("(k p) n -> p k n", p=128)
        def tail(i):
            ob, obk = obs[i % 2], f"ffo{i % 2}"
            b0, b1 = (i % 2) * 2, (i % 2) * 2 + 1
            for half in range(2):
                chn = i + 22 * half
                bi = (i % 2) * 2 + half
                self.dwconv_shift(raw4[bi], f"raw{bi}", cv4[bi], f"cv{bi}", cw[:, chn, :], ["cw"], nseq, L)
            self.act(cv4[b1][:], cv4[b1][:], AF.Silu, [f"cv{b1}"], [f"cv{b1}"])
            self.tt("dve", ob[:], cv4[b1][:], cv4[b0][:], ALU.mult, [f"cv{b0}", f"cv{b1}"], [obk])
            self.dma("sp", X[f"ffa_{g}"][i * 128:(i + 1) * 128, :], ob[:], [obk], [f"ffa_{g}"])

        wdv = I["ffn_down"][l].rearrange("(k p) n -> p k n", p=128)
        for i in range(23):
            if i < 22:
                wb, wk = wbs[i % 2], f"wu{i % 2}"
                self.dma("pool", wb[:, :, 0:128], wv[:, :, i * 128:(i + 1) * 128], [], [wk])
                self.dma("pool", wb[:, :, 128:256], wv[:, :, DFF + i * 128:DFF + (i + 1) * 128], [], [wk])
                if 2 <= i < 13:
                    k0 = (i - 2) * 2
                    self.dma("pool", wd[:, k0:k0 + 2, :], wdv[:, k0:k0 + 2, :], [], ["wd"])
                for half in range(2):
                    chn = i + 22 * half
                    bi = (i % 2) * 2 + half
                    for tt in range(T // 512):
                        tsl = slice(tt * 512, (tt + 1) * 512)
                        ps, pk = self.next_ps()
                        for k in range(8):
                            self.mm(ps[:], wb[:, k, half * 128:(half + 1) * 128], self.hT[:, k, tsl], k == 0, k == 7, [wk, f"hT{tt}"], [pk])
                        self.cp("act", raw4[bi][:, tsl], ps[:], [pk], [f"raw{bi}"])
                    self.act(cv4[bi][:], raw4[bi][:], AF.Identity, [f"raw{bi}", "cw", "cb"], [f"cv{bi}"],
                             scale=cw[:, chn, 1:2], bias=cb[:, chn:chn + 1])
            if i >= 1:
                tail(i - 1)

    def dwconv_shift(self, x, xk, o, ok, w3, wkeys, nseq, L):
        xv = x[:].rearrange("p (s t) -> p s t", t=L)
        ov = o[:].rearrange("p (s t) -> p s t", t=L)
        self.stt(ov[:, :, 1:L], xv[:, :, 0:L - 1], w3[:, 0:1], ov[:, :, 1:L], ALU.mult, ALU.add, [xk, ok] + wkeys, [ok])
        self.stt(ov[:, :, 0:L - 1], xv[:, :, 1:L], w3[:, 2:3], ov[:, :, 0:L - 1], ALU.mult, ALU.add, [xk, ok] + wkeys, [ok])

    def dwconv(self, x, xk, o, ok, w3, b, wkeys, nseq, L):
        xv = x[:].rearrange("p (s t) -> p s t", t=L)
        ov = o[:].rearrange("p (s t) -> p s t", t=L)
        self.act(o[:], x[:], AF.Identity, [xk] + wkeys, [ok], scale=w3[:, 1:2], bias=b)
        self.stt(ov[:, :, 1:L], xv[:, :, 0:L - 1], w3[:, 0:1], ov[:, :, 1:L], ALU.mult, ALU.add, [xk, ok] + wkeys, [ok])
        self.stt(ov[:, :, 0:L - 1], xv[:, :, 1:L], w3[:, 2:3], ov[:, :, 0:L - 1], ALU.mult, ALU.add, [xk, ok] + wkeys, [ok])

    def ffn_down(self, sc, g, l, wd):
        I, X = self.I, self.X
        T = GROUPS[g]["T"]
        aa = [sc.sb(f"ffa{i}", [128, 22, 512], BF16) for i in range(2)]
        rts = self.res_tiles(sc)
        av = X[f"ffa_{g}"].rearrange("(k p) t -> p k t", p=128)
        NT_ = T // 512
        self.dma("sp", aa[0][:], av[:, :, 0:512], [f"ffa_{g}"], ["ffa0"])
        self.residual_load(g, rts[0], X[f"y_{g}"], f"y_{g}", 0)
        for tt in range(NT_):
            tsl = slice(tt * 512, (tt + 1) * 512)
            a, ak = aa[tt % 2], f"ffa{tt % 2}"
            ts_ = rts[tt % 2]
            if tt + 1 < NT_:
                self.dma("sp", aa[(tt + 1) % 2][:], av[:, :, (tt + 1) * 512:(tt + 2) * 512], [f"ffa_{g}"], [f"ffa{(tt + 1) % 2}"])
                self.residual_load(g, rts[(tt + 1) % 2], X[f"y_{g}"], f"y_{g}", tt + 1)
            for m in range(8):
                ps, pk = self.next_ps()
                for k in range(22):
                    self.mm(ps[:], wd[:, k, m * 128:(m + 1) * 128], a[:, k, :], k == 0, k == 21, ["wd", ak], [pk])
                self.cp("act" if m % 2 else "dve", ts_["msb"][:, m, :], ps[:], [pk], [f"msb{tt % 2}", f"msb{tt % 2}_{m}"])
            self.residual(g, l, 1, ts_, X[f"y_{g}"], f"y_{g}", tt)

    def scan_prep_dir(self, sc, d, L, dk, pb, qs, qk, qscale, kk, kkk, lf, lfk, tmpA, qtf, qt, kt, kh, el, kT, CH=CH):
        P = slice(pb, pb + dk)
        NB = L // 128
        NCH = L // CH
        cum = tmpA
        ak, hk = f"scA{d}", f"kh{d}"
        if d == "f":
            self.S.op("dve", lambda e: e.tensor_tensor_scan(out=cum[P, 0:L], data0=self.mask_f[P, 0:L], data1=lf[P, 0:L],
                                                             initial=0.0, op0=ALU.mult, op1=ALU.add), [lfk, "mask_f"], [ak])
        else:
            self.S.op("dve", lambda e: e.tensor_tensor_scan(out=cum[P, 0:L][:, ::-1],
                                                             data0=self.mask_b[P, 0:L][:, ::-1], data1=lf[P, 0:L][:, ::-1],
                                                             initial=0.0, op0=ALU.mult, op1=ALU.add), [lfk, "mask_b"], [ak])
        yield
        self.act(qtf[P, 0:L], cum[P, 0:L], AF.Exp, [ak], [f"qtf{d}"])
        yield
        ev = qtf[P, 0:L].rearrange("p (c t) -> p c t", t=CH)
        pos = CH - 1 if d == "f" else 0
        self.cp("dve", el[P, 0:NCH], ev[:, :, pos], [f"qtf{d}"], [f"el{d}"])
        yield
        self.stt(qtf[P, 0:L], qs[P, 0:L], float(qscale), qtf[P, 0:L], ALU.mult, ALU.mult, [qk, f"qtf{d}", f"el{d}"], [f"qtf{d}"])
        yield
        self.cp("act", qt[P, 0:L], qtf[P, 0:L], [f"qtf{d}"], [f"qt{d}"])
        yield
        self.act(cum[P, 0:L], cum[P, 0:L], AF.Exp, [ak], [ak], scale=-1.0)
        yield
        self.tt("dve", kt[P, 0:L], kk[P, 0:L], cum[P, 0:L], ALU.mult, [kkk, ak], [f"kt{d}"])
        yield
        elb = el[P, 0:NCH].unsqueeze(2).broadcast_to([dk, NCH, CH])
        self.tt("dve", kh[P, 0:L].rearrange("p (c t) -> p c t", t=CH), kt[P, 0:L].rearrange("p (c t) -> p c t", t=CH), elb,
                ALU.mult, [f"kt{d}", f"el{d}"], [hk])
        yield
        for blk in range(NB):
            j = blk % 8
            self.tr(self.psT[:, j * 128:j * 128 + dk], kh[P, blk * 128:(blk + 1) * 128], self.ident_b[P, P], [hk, "ident_b"], ["ps7"])
            if j == 7 or blk == NB - 1:
                n = j + 1
                b0 = blk - j
                self.cp("act" if (blk // 8) % 2 else "dve", kT[:, b0:b0 + n, 0:dk],
                        self.psT[:, 0:n * 128].rearrange("p (b c) -> p b c", c=128)[:, :, 0:dk], ["ps7"], [f"kT{d}"])
                yield

    @staticmethod
    def interleave(gens):
        gens = list(gens)
        while gens:
            for g_ in list(gens):
                try:
                    next(g_)
                except StopIteration:
                    gens.remove(g_)

    def scan_chain(self, sc, L, dk, pb, qtf, qt, kt, el, kT, Vt, vk, Sst, oT, cm, init_state, fin_fn, CH=CH, seg_blocks=None):
        P = slice(pb, pb + dk)
        NB = L // 128
        SB = seg_blocks or NB
        DIRS = ("f", "b")
        NR = 12
        PA, PAK = self.ps[0], "ps0"
        POV, POVK = self.ps[1], "ps1"
        PD = {"f": ((self.ps[2], "ps2"), (self.ps[3], "ps3")), "b": ((self.ps[4], "ps4"), (self.ps[5], "ps5"))}
        PIN = {"f": (self.ps[6], "ps6"), "b": (self.ps[7], "ps7")}
        blk_of = lambda d, i: i if d == "f" else NB - 1 - i
        NCB = 128 // CH
        order = {"f": tuple(range(NCB)), "b": tuple(reversed(range(NCB)))}
        nst = {"f": 0, "b": 0}
        cur = {}
        prevref = {}

        def stage_att(i):
            for di, d in enumerate(DIRS):
                blk = blk_of(d, i)
                t0 = blk * 128
                sl = (i % 2) * 2 + di
                pa = PA[:, sl * 128:(sl + 1) * 128]
                self.mm(pa, kt[d][P, t0:t0 + 128], qt[d][P, t0:t0 + 128], True, True, [f"kt{d}", f"qt{d}"], [PAK])
                self.tt("dve", self.AT[d][i % 2][:], pa, cm[d][:], ALU.mult, [PAK, "cm"], [f"AT{d}{i % 2}"])

        def stage_pe(i):
            for di, d in enumerate(DIRS):
                blk = blk_of(d, i)
                sl = (i % 2) * 2 + di
                pov = POV[:, sl * 128:(sl + 1) * 128]
                self.mm(pov, Vt[:, blk, :], self.AT[d][i % 2][:], True, True, [vk, f"AT{d}{i % 2}"], [POVK])
                pd, pdk = PD[d][i % 2]
                if NCB == 1:
                    self.mm(pd[P, 0:128], kT[d][:, blk, 0:dk], Vt[:, blk, :], True, True, [f"kT{d}", vk], [pdk])
                else:
                    for c in range(NCB):
                        self.mm(pd[P, c * 128:(c + 1) * 128], kT[d][:, blk, 0:dk], self.Vm[:, c, blk, :], True, True,
                                [f"kT{d}", "Vm"], [pdk])

        def stage_chain(i):
            for idx in range(NCB):
                for di, d in enumerate(DIRS):
                    blk = blk_of(d, i)
                    seg = blk // SB
                    c = order[d][idx]
                    pd, pdk = PD[d][i % 2]
                    if i % SB == 0 and idx == 0:
                        cur[d] = init_state(d, seg)
                    st0, sk0 = cur[d]
                    prevref[(d, i, idx)] = cur[d]
                    k1 = nst[d] % NR
                    nst[d] += 1
                    ci = blk * NCB + c
                    self.stt(Sst[d][k1][P, :], st0[P, :], el[d][P, ci:ci + 1], pd[P, c * 128:(c + 1) * 128], ALU.mult, ALU.add,
                             [sk0, f"el{d}", pdk], [f"S{d}{k1}"])
                    cur[d] = (Sst[d][k1], f"S{d}{k1}")
                    if i % SB == SB - 1 and idx == NCB - 1:
                        fin_fn(d, seg, Sst[d][k1], f"S{d}{k1}")
            for di, d in enumerate(DIRS):
                blk = blk_of(d, i)
                sl = (i % 2) * 2 + di
                self.cp("act", oT[d][:, blk * 128:(blk + 1) * 128], POV[:, sl * 128:(sl + 1) * 128], [POVK], [f"oT{d}"])

        def stage_inter(i):
            for di, d in enumerate(DIRS):
                blk = blk_of(d, i)
                t0 = blk * 128
                pin, pink = PIN[d]
                sl = i % 4
                for idx in range(NCB):
                    c = order[d][idx]
                    st0, sk0 = prevref.pop((d, i, idx))
                    cs = slice(t0 + CH * c, t0 + CH * (c + 1))
                    self.mm(pin[:, sl * 128 + CH * c:sl * 128 + CH * (c + 1)], st0[P, :], qtf[d][P, cs], True, True,
                            [sk0, f"qtf{d}"], [pink])
                self.tt("dve", oT[d][:, t0:t0 + 128], oT[d][:, t0:t0 + 128], pin[:, sl * 128:(sl + 1) * 128], ALU.add,
                        [f"oT{d}", pink], [f"oT{d}"])

        stage_att(0)
        for i in range(NB + 1):
            if i + 1 < NB:
                stage_att(i + 1)
            if i < NB:
                stage_pe(i)
                stage_chain(i)
            if i >= 1:
                stage_inter(i - 1)

    def head_out(self, sc, g, L, col0, seq, oTf, oTb, gate, gatek, gnorm_col, gnk, sq, rstd, outb, mixrow0, extra_scale=None):
        T0 = seq * L
        if oTb is not None:
            self.tt("dve", oTf[:, 0:L], oTf[:, 0:L], oTb[:, 0:L], ALU.add, ["oTf", "oTb"], ["oTf"])
        self.act(sq[:, 0:L], oTf[:, 0:L], AF.Square, ["oTf"], ["hsq"])
        if gate is not None:
            self.act(gate[:, 0:L], gate[:, 0:L], AF.Silu, [gatek], [gatek])
        for t0 in range(0, L, 512):
            w = min(512, L - t0)
            ps, pk = self.ps[0], "ps0"
            self.mm(ps[:, 0:w], self.ones_b[:], sq[:, t0:t0 + w], True, True, ["ones_b", "hsq"], [pk])
            self.rstd_from_ps(rstd[:, 0:w], ps[:, 0:w], pk, 128, [], "hrstd")
            self.stt(oTf[:, t0:t0 + w], oTf[:, t0:t0 + w], gnorm_col, rstd[:, 0:w], ALU.mult, ALU.mult, ["oTf", gnk, "hrstd"], ["oTf"])
        if gate is not None:
            self.tt("dve", outb[:, 0:L], oTf[:, 0:L], gate[:, 0:L], ALU.mult, ["oTf", gatek], ["houtb"])
        else:
            self.act(outb[:, 0:L], oTf[:, 0:L], AF.Copy, ["oTf"], ["houtb"], scale=float(extra_scale))
        self.dma("sp", self.X[f"mix_{g}"][mixrow0:mixrow0 + 128, T0:T0 + L], outb[:, 0:L], ["houtb"], [f"mix_{g}"])

    def load_scan_masks(self, sc, L, ch=CH):
        self.mask_f = sc.sb("mask_f", [128, L])
        self.mask_b = sc.sb("mask_b", [128, L])
        pre = "mask" if ch == CH else f"mask{ch}"
        self.dma("sp", self.mask_f[:], self.C[pre + "_f"][:, 0:L], [], ["mask_f"])
        self.dma("sp", self.mask_b[:], self.C[pre + "_b"][:, 0:L], [], ["mask_b"])

    def hgrn(self, g):
        I, X, O = self.I, self.X, self.O
        gi = GROUPS[g]
        T, Lseq, nseq0 = gi["T"], gi["L"], gi["nseq"]
        L, nseq = T, 1
        SEGB = Lseq // 128
        NB = L // 128
        with self.scope() as sc:
            self.load_scan_masks(sc, L)
            Zst = sc.sb("Zst", [128, 128])
            self.S.op("dve", lambda e: e.memset(Zst[:], 0.0), [], ["Zst"])
            Sin = {d: sc.sb(f"Sin{d}", [128, 128]) for d in "fb"}
            lbt = sc.sb("lbt", [128, 3, 8])
            lb = sc.sb("lb", [128, 8])
            oml = sc.sb("oml", [128, 8])
            gn = sc.sb("gn", [128, 4])
            self.dma("sp", lbt[:], I["lbT"], [], ["lbt"])
            self.dma("sp", gn[:], I["hgrn_normT"], [], ["gn"])
            self.act(lbt[:], lbt[:], AF.Exp, ["lbt"], ["lbt"])
            self.tt("dve", lb[:], lbt[:, 0, :], lbt[:, 1, :], ALU.add, ["lbt"], ["lb"])
            self.tt("dve", lb[:], lb[:], lbt[:, 2, :], ALU.add, ["lbt", "lb"], ["lb"])
            self.S.op("dve", lambda e: e.reciprocal(out=lb[:], in_=lb[:]), ["lb"], ["lb"])
            self.tt("dve", lb[:], lb[:], lbt[:, 0, :], ALU.mult, ["lbt", "lb"], ["lb"])
            self.ts("dve", oml[:], lb[:], -1.0, 1.0, ALU.mult, ALU.add, ["lb"], ["oml"])
            qs = sc.sb("qs", [128, L])
            t1 = {d: sc.sb(f"t1{d}", [128, L]) for d in "fb"}
            t2 = {d: sc.sb(f"t2{d}", [128, L]) for d in "fb"}
            tA = {d: sc.sb(f"tA{d}", [128, L]) for d in "fb"}
            gr = tA["f"]
            qtf = {d: sc.sb(f"qtf{d}", [128, L]) for d in "fb"}
            kh = {d: sc.sb(f"kh{d}", [128, L], BF16) for d in "fb"}
            qt = {d: sc.sb(f"qt{d}", [128, L], BF16) for d in "fb"}
            kt = {d: sc.sb(f"kt{d}", [128, L], BF16) for d in "fb"}
            el = {d: sc.sb(f"el{d}", [128, L // CH]) for d in "fb"}
            kT = {d: sc.sb(f"kT{d}", [128, NB, 128], BF16) for d in "fb"}
            Vt = sc.sb("Vt", [128, NB, 128], BF16)
            self.Vm = sc.sb("Vm", [128, 4, NB, 128], BF16)
            Sst = {d: [sc.sb(f"S{d}{i}", [128, 128]) for i in range(12)] for d in "fb"}
            oT = {d: sc.sb(f"oT{d}", [128, L]) for d in "fb"}
            self.AT = {d: [sc.sb(f"AT{d}{i}", [128, 128], BF16) for i in range(2)] for d in "fb"}
            sq = sc.sb("hsq", [128, L], BF16)
            rstd = sc.sb("hrstd", [128, 512])
            outb = sc.sb("houtb", [128, L], BF16)
            cm = {"f": self.cm_f, "b": self.cm_b}
            pj = X[f"proj_{g}"]
            for s in range(nseq):
                tsl = slice(s * L, (s + 1) * L)
                def issue_loads(h):
                    self.dma("sp", qs[:], pj[h * 128:(h + 1) * 128, tsl], [f"proj_{g}"], ["qs"])
                    for di, d in enumerate("fb"):
                        r0 = (1 + di) * 512 + h * 128
                        self.dma("sp", t1[d][:], pj[r0:r0 + 128, tsl], [f"proj_{g}"], [f"t1{d}"])
                    self.dma("pool", Vt[:], X[f"vtok_{g}"][tsl, h * 128:(h + 1) * 128].rearrange("(b p) v -> p b v", p=128),
                             [f"vtok_{g}"], ["Vt"])

                issue_loads(0)
                for h in range(4):
                    rows = lambda blk: slice(blk * 512 + h * 128, blk * 512 + (h + 1) * 128)
                    self.act(qs[:], qs[:], AF.Silu, ["qs"], ["qs"])
                    for c in range(4):
                        self.act(self.Vm[:, c], Vt[:], AF.Copy, ["Vt", "ind4"], ["Vm"], scale=self.ind4[:, c:c + 1])
                    def prep(di, d, h=h, rows=rows, tsl=tsl):
                        a1, a2 = t1[d], t2[d]
                        k1, k2 = f"t1{d}", f"t2{d}"
                        self.act(a1[:], a1[:], AF.Sigmoid, [k1], [k1])
                        yield
                        c8 = di * 4 + h
                        self.act(a1[:], a1[:], AF.Identity, [k1, "oml", "lb"], [k1], scale=oml[:, c8:c8 + 1], bias=lb[:, c8:c8 + 1])
                        yield
                        self.act(a2[:], a1[:], AF.Ln, [k1], [k2])
                        yield
                        self.act(a1[:], a1[:], AF.Identity, [k1, "ones_col"], [k1], scale=-1.0, bias=self.ones_col[:, 0:1])
                        yield
                        yield from self.scan_prep_dir(sc, d, L, 128, 0, qs, "qs", 128 ** -0.5, a1, k1, a2, k2, tA[d], qtf[d], qt[d],
                                                      kt[d], kh[d], el[d], kT[d])

                    self.interleave([prep(0, "f"), prep(1, "b")])
                    if g == "s":
                        for di, d in enumerate("fb"):
                            self.dma("sp", Sin[d][:], I["st_hgrn"][:, di, h, :], [], [f"Sin{d}"])

                    def init_state(d, seg, h=h):
                        return (Zst, "Zst") if g == "p" else (Sin[d], f"Sin{d}")

                    def fin_fn(d, seg, St, sk, h=h):
                        if g == "p":
                            self.dma("sp", O["nst_hgrn"][seg, 0 if d == "f" else 1, h], St[:], [sk], ["nst_hgrn"])

                    self.bg_step(3)
                    self.dma("sp", gr[:], pj[rows(4), tsl], [f"proj_{g}"], ["scAf"])
                    self.scan_chain(sc, L, 128, 0, qtf, qt, kt, el, kT, Vt, "Vt", Sst, oT, cm, init_state, fin_fn, seg_blocks=SEGB)
                    self.bg_step(2)
                    if h + 1 < 4:
                        issue_loads(h + 1)
                    self.head_out(sc, g, L, 0, s, oT["f"], oT["b"], gr, "scAf", gn[:, h:h + 1], "gn", sq, rstd, outb, h * 128)
            if self.dbg == "hgrn":
                self.dump(f"mixa_{g}", X[f"mix_{g}"][0:512, :], [512, T], BF16, [f"mix_{g}"])

    def wrap_pi(self, ap, key, tmp, tmpk):
        for _ in range(2):
            self.ts("dve", tmp, ap, -math.pi, 2.0 * math.pi, ALU.is_lt, ALU.mult, [key], [tmpk])
            self.tt("dve", ap, ap, tmp, ALU.add, [key, tmpk], [key])
            self.ts("dve", tmp, ap, math.pi, -2.0 * math.pi, ALU.is_gt, ALU.mult, [key], [tmpk])
            self.tt("dve", ap, ap, tmp, ALU.add, [key, tmpk], [key])

    def hyena(self, g):
        I, X, C = self.I, self.X, self.C
        gi = GROUPS[g]
        T, L, nseq = gi["T"], gi["L"], gi["nseq"]
        SC = L // 128
        NF = SC + 1
        NT = max(1, L // 512)
        TW = min(L, 512)
        ksp = X[f"ksp_{g}"]
        self.mark(f"hy{g} A:mlp")
        with self.scope() as sc:
            w1 = sc.sb("hw1", [33, 64]); w2 = sc.sb("hw2", [64, 64]); w3 = sc.sb("hw3", [64, 2048])
            b1 = sc.sb("hb1", [64, 1]); b2 = sc.sb("hb2", [64, 1]); fr = sc.sb("hfr", [64, 1])
            for t_, nm in ((w1, "hy_w1"), (w2, "hy_w2"), (w3, "hy_w3"), (b1, "hy_b1"), (b2, "hy_b2"), (fr, "hy_freq")):
                self.dma("sp", t_[:], I[nm], [], ["hyw"])
            h2 = sc.sb("h2", [64, L])
            with self.scope() as sc_mlp:
                zT = sc_mlp.sb("zT", [33, L])
                self.dma("sp", zT[:], C[f"zT_{g}"], [], ["zT"])
                h1 = sc_mlp.sb("h1", [64, L]); hw = sc_mlp.sb("hwrap", [64, L])
                for (src, srck, wt_, bb, dst, dstk) in ((zT, "zT", w1, b1, h1, "h1"), (h1, "h1", w2, b2, h2, "h2")):
                    for t0 in range(0, L, 512):
                        w = min(512, L - t0)
                        ps, pk = self.next_ps()
                        self.mm(ps[0:64, 0:w], wt_[:], src[:, t0:t0 + w], True, True, ["hyw", srck], [pk])
                        self.ts("dve", dst[:, t0:t0 + w], ps[0:64, 0:w], bb[:, 0:1], fr[:, 0:1], ALU.add, ALU.mult, [pk, "hyw"], [dstk])
                    self.wrap_pi(dst[:], dstk, hw[:], "hwrap")
                    self.act(dst[:], dst[:], AF.Sin, [dstk], [dstk])
            wt = [[sc.sb(f"win{i}{j}", [128, 512]) for j in range(2)] for i in range(2)]
            ge = [sc.sb(f"ge{o}", [128, SC, 512], BF16) for o in range(2)]
            go = [sc.sb(f"go{o}", [128, SC, 512], BF16) for o in range(2)]
            ff = [[sc.sb(f"ff{o}{i}", [128, 512]) for i in range(2)] for o in range(2)]
            Af = [[sc.sb(f"Af{i}{j}", [128, SC, 128], BF16) for j in range(2)] for i in range(2)]
            kst = [sc.sb(f"kst{i}", [128, 512], BF16) for i in range(4)]
            cw = sc.sb("hcw", [128, 12, 3]); cb = sc.sb("hcb", [128, 12])
            self.dma("sp", cw[:], I["hy_cwT"], [], ["hcw"])
            self.dma("sp", cb[:], I["hy_cbT"], [], ["hcb"])
            raw = [sc.sb(f"hraw{i}", [128, T]) for i in range(2)]
            cvo = [sc.sb(f"hcv{i}", [128, T]) for i in range(2)]

            def stage_a():
                for lc in range(SC):
                    for side in range(2):
                        wtile, wk_ = wt[side][lc % 2], f"win{side}{lc % 2}"
                        self.dma("sp", wtile[:], C[f"win{side}_{g}"][:, lc, :], [], [wk_])
                        for o in range(2):
                            ps, pk = self.next_ps()
                            col0 = o * 1024 + side * 512
                            self.mm(ps[:], h2[:, lc * 128:(lc + 1) * 128], w3[:, col0:col0 + 512], True, True, ["h2", "hyw"], [pk])
                            self.tt("dve", ff[o][side][:], ps[:], wtile[:], ALU.mult, [pk, wk_], [f"ff{o}{side}"])
                    for o in range(2):
                        self.tt("pool", ge[o][:, lc, :], ff[o][0][:], ff[o][1][:], ALU.add, [f"ff{o}0", f"ff{o}1"], [f"ge{o}"])
                        self.tt("pool", go[o][:, lc, :], ff[o][0][:], ff[o][1][:], ALU.subtract, [f"ff{o}0", f"ff{o}1"], [f"go{o}"])
                    if lc % 2 == 1:
                        yield
                self.dma("sp", Af[0][0][:], C[f"Ac_{g}"][0], [], ["Ac0"])
                self.dma("sp", Af[0][1][:], C[f"As_{g}"][0], [], ["As0"])
                for fc in range(NF):
                    a_c, a_s = Af[fc % 2]
                    if fc + 1 < NF:
                        n_c, n_s = Af[(fc + 1) % 2]
                        self.dma("sp", n_c[:], C[f"Ac_{g}"][fc + 1], [], [f"Ac{(fc + 1) % 2}"])
                        self.dma("sp", n_s[:], C[f"As_{g}"][fc + 1], [], [f"As{(fc + 1) % 2}"])
                    for o in range(2):
                        for ri, (am, amk, gm, gmk) in enumerate(((a_c, f"Ac{fc % 2}", ge[o], f"ge{o}"), (a_s, f"As{fc % 2}", go[o], f"go{o}"))):
                            ps, pk = self.next_ps()
                            for lc in range(SC):
                                self.mm(ps[:], am[:, lc, :], gm[:, lc, :], lc == 0, lc == SC - 1, [amk, gmk], [pk])
                            ki = o * 2 + ri
                            self.cp("act" if ri else "dve", kst[ki][:], ps[:], [pk], [f"kst{ki}"])
                            self.dma("sp", ksp[o, ri, fc], kst[ki][:], [f"kst{ki}"], [f"ksp_{g}"])
                    yield

            def stage_b():
                self.dma("sp", raw[0][:], X[f"proj_{g}"][2560:2560 + 128, :], [f"proj_{g}"], ["hraw0"])
                for ch in range(12):
                    r_, rk = raw[ch % 2], f"hraw{ch % 2}"
                    o_, ok = cvo[ch % 2], f"hcv{ch % 2}"
                    if ch + 1 < 12:
                        self.dma("sp", raw[(ch + 1) % 2][:], X[f"proj_{g}"][2560 + (ch + 1) * 128:2560 + (ch + 2) * 128, :],
                                 [f"proj_{g}"], [f"hraw{(ch + 1) % 2}"])
                    yield
                    self.dwconv(r_, rk, o_, ok, cw[:, ch, :], cb[:, ch:ch + 1], ["hcw", "hcb"], nseq, L)
                    yield
                    self.dma("sp", X[f"hyc_{g}"][ch * 128:(ch + 1) * 128, :], o_[:], [ok], [f"hyc_{g}"])
                    yield

            self.mark(f"hy{g} A:spectra+conv")
            self.interleave([stage_a(), stage_b()])
        with self.scope() as sc:
            hd = sc.sb("hd", [128, 2, 4])
            self.dma("sp", hd[:], I["hy_dT"], [], ["hd"])
            z = sc.sb("z", [128, 4, L])
            zb = sc.sb("zb", [128, L], BF16)
            uT = sc.sb("uT", [128, SC, 512], BF16)
            Yre = sc.sb("Yre", [128, NF, 512], BF16)
            Yim = sc.sb("Yim", [128, NF, 512], BF16)
            Af = [[sc.sb(f"Af{i}{j}", [128, SC, 128], BF16) for j in range(2)] for i in range(2)]
            Kt = [[sc.sb(f"Kt{i}{j}", [128, 512], BF16) for j in range(2)] for i in range(2)]
            tm = [sc.sb(f"tm{i}", [128, 512]) for i in range(4)]
            Bc = sc.sb("Bc", [128, NF, TW], BF16)
            Bs = sc.sb("Bs", [128, NF, TW], BF16)
            gt = [sc.sb(f"gt{i}", [128, TW]) for i in range(2)]
            zo = sc.sb("zo", [128, L], BF16)
            for s_ in range(nseq):
                tsl = slice(s_ * L, (s_ + 1) * L)
                self.dma("sp", z[:], X[f"hyc_{g}"][0:512, tsl].rearrange("(c p) t -> p c t", p=128), [f"hyc_{g}"], ["z"])
                for o in range(2):
                    self.mark(f"hy{g} C{o}:transp")
                    for cc in range(4):
                        self.cp("act", zb[:], z[:, cc, :], ["z"], ["zb"])
                        for lc in range(SC):
                            self.tr(self.psT[:, (lc % 8) * 128:(lc % 8 + 1) * 128], zb[:, lc * 128:(lc + 1) * 128], self.ident_b[:],
                                    ["zb", "ident_b"], ["ps7"])
                            if lc % 8 == 7 or lc == SC - 1:
                                n = lc % 8 + 1
                                l0 = lc - n + 1
                                self.cp("act" if cc % 2 else "dve", uT[:, l0:l0 + n, cc * 128:(cc + 1) * 128],
                                        self.psT[:, 0:n * 128].rearrange("p (l c) -> p l c", c=128), ["ps7"], ["uT"])
                    self.mark(f"hy{g} C{o}:fwd")
                    for fc in range(NF):
                        a_c, a_s = Af[fc % 2]
                        k_r, k_i = Kt[fc % 2]
                        self.dma("sp", a_c[:], C[f"Ac_{g}"][fc], [], [f"Ac{fc % 2}"])
                        self.dma("sp", a_s[:], C[f"As_{g}"][fc], [], [f"As{fc % 2}"])
                        self.dma("sp", k_r[:], ksp[o, 0, fc], [f"ksp_{g}"], [f"Kr{fc % 2}"])
                        self.dma("sp", k_i[:], ksp[o, 1, fc], [f"ksp_{g}"], [f"Ki{fc % 2}"])
                        pr, prk = self.next_ps()
                        for lc in range(SC):
                            self.mm(pr[:], a_c[:, lc, :], uT[:, lc, :], lc == 0, lc == SC - 1, [f"Ac{fc % 2}", "uT"], [prk])
                        pi_, pik = self.next_ps()
                        for lc in range(SC):
                            self.mm(pi_[:], a_s[:, lc, :], uT[:, lc, :], lc == 0, lc == SC - 1, [f"As{fc % 2}", "uT"], [pik])
                        self.tt("dve", tm[0][:], pr[:], k_r[:], ALU.mult, [prk, f"Kr{fc % 2}"], ["tm0"])
                        self.tt("dve", tm[1][:], pi_[:], k_i[:], ALU.mult, [pik, f"Ki{fc % 2}"], ["tm1"])
                        self.tt("dve", tm[2][:], pr[:], k_i[:], ALU.mult, [prk, f"Ki{fc % 2}"], ["tm2"])
                        self.tt("dve", tm[3][:], pi_[:], k_r[:], ALU.mult, [pik, f"Kr{fc % 2}"], ["tm3"])
                        self.tt("pool", Yre[:, fc, :], tm[0][:], tm[1][:], ALU.subtract, ["tm0", "tm1"], ["Yre"])
                        self.tt("pool", Yim[:, fc, :], tm[2][:], tm[3][:], ALU.add, ["tm2", "tm3"], ["Yim"])
                    self.mark(f"hy{g} C{o}:inv")
                    for tt in range(NT):
                        self.dma("sp", Bc[:], C[f"Bc_{g}"][tt], [], ["Bc"])
                        self.dma("sp", Bs[:], C[f"Bs_{g}"][tt], [], ["Bs"])
                        for cc in range(4):
                            ps, pk = self.next_ps()
                            for fc in range(NF):
                                self.mm(ps[:, 0:TW], Yre[:, fc, cc * 128:(cc + 1) * 128], Bc[:, fc, :], fc == 0, False, ["Yre", "Bc"], [pk])
                            for fc in range(NF):
                                self.mm(ps[:, 0:TW], Yim[:, fc, cc * 128:(cc + 1) * 128], Bs[:, fc, :], False, fc == NF - 1, ["Yim", "Bs"], [pk])
                            gtile, gk = gt[cc % 2], f"gt{cc % 2}"
                            grow = 512 * (o + 1) + cc * 128
                            self.dma("sp", gtile[:], X[f"hyc_{g}"][grow:grow + 128, s_ * L + tt * TW:s_ * L + (tt + 1) * TW], [f"hyc_{g}"], [gk])
                            zsl = z[:, cc, tt * TW:(tt + 1) * TW]
                            self.stt(zsl, zsl, hd[:, o, cc:cc + 1], ps[:, 0:TW], ALU.mult, ALU.add, ["z", "hd", pk], ["z"])
                            self.tt("dve", zsl, zsl, gtile[:], ALU.mult, ["z", gk], ["z"])
                for cc in range(4):
                    self.cp("act", zo[:], z[:, cc, :], ["z"], ["zo"])
                    self.dma("sp", X[f"mix_{g}"][512 + cc * 128:512 + (cc + 1) * 128, tsl], zo[:], ["zo"], [f"mix_{g}"])
            if self.dbg == "hyena":
                self.dump(f"mixz_{g}", X[f"mix_{g}"][512:1024, :], [512, T], BF16, [f"mix_{g}"])
                self.dump(f"hyc_{g}", X[f"hyc_{g}"], [1536, T], F32, [f"hyc_{g}"])
                self.dump(f"ksp_{g}", ksp, [2, 2, NF, 128, 512], BF16, [f"ksp_{g}"])

    def diffattn(self, g):
        I, X, O, C = self.I, self.X, self.O, self.C
        gi = GROUPS[g]
        T, L, nseq = gi["T"], gi["L"], gi["nseq"]
        NB = L // 128
        NCK = 2 if g == "s" else 0
        NK = NB + NCK
        QW = min(512, L)
        lam_init = 0.8 - 0.6 * math.exp(-0.3 * 1)
        with self.scope() as sc:
            dl = sc.sb("dl", [128, 4, 64])
            pr = sc.sb("dlp", [128, 2, 64])
            lam = sc.sb("lam", [128, 2])
            lamneg = sc.sb("lamneg", [128, 1])
            dn = sc.sb("dn", [128, 4])
            self.dma("sp", dl[:], I["dlam"], [], ["dl"])
            self.dma("sp", dn[:], I["diff_normT"], [], ["dn"])
            self.tt("dve", pr[:, 0, :], dl[:, 0, :], dl[:, 1, :], ALU.mult, ["dl"], ["dlp"])
            self.tt("dve", pr[:, 1, :], dl[:, 2, :], dl[:, 3, :], ALU.mult, ["dl"], ["dlp"])
            self.S.op("dve", lambda e: e.reduce_sum(out=lam[:], in_=pr[:], axis=mybir.AxisListType.X), ["dlp"], ["lam"])
            self.act(lam[:], lam[:], AF.Exp, ["lam"], ["lam"])
            self.tt("dve", lamneg[:], lam[:, 1:2], lam[:, 0:1], ALU.subtract, ["lam"], ["lamneg"])
            self.ts("dve", lamneg[:], lamneg[:], -lam_init, None, ALU.add, None, ["lamneg"], ["lamneg"])
            self.ts("dve", dn[:], dn[:], 1.0 - lam_init, None, ALU.mult, None, ["dn"], ["dn"])
            q = sc.sb("aq", [128, L]); k = sc.sb("ak", [128, L])
            qb2 = [sc.sb(f"aqb{i}", [128, L], BF16) for i in range(2)]
            kall2 = [sc.sb(f"akall{i}", [128, NK * 128], BF16) for i in range(2)]
            Vall2 = [sc.sb(f"aV{i}", [128, NK, 128], BF16) for i in range(2)]
            E = [sc.sb(f"aE{i}", [128, NK, QW], BF16) for i in range(2)]
            on = [sc.sb(f"aon{i}", [128, QW]) for i in range(2)]
            rl = sc.sb("arl", [128, QW])
            Er = sc.sb("aEr", [128, QW])
            ones_f = sc.sb("ones_f", [128, 128])
            self.S.op("pool", lambda e: e.memset(ones_f[:], 1.0), [], ["ones_f"])
            oc = sc.sb("aoc", [128, L])
            sq = sc.sb("hsq", [128, L], BF16)
            rstd = sc.sb("hrstd", [128, 512])
            outb = sc.sb("houtb", [128, L], BF16)
            if g == "s":
                rC = sc.sb("ropeC", [128, L]); rS = sc.sb("ropeS", [128, L]); rP = sc.sb("ropeP", [128, 128])
                self.dma("sp", rC[:], C["ropeC"], [], ["ropeC"])
                self.dma("sp", rS[:], C["ropeS"], [], ["ropeS"])
                self.dma("sp", rP[:], C["ropeP"], [], ["ropeP"])
                rt = sc.sb("ropet", [128, 512])
                kc32 = sc.sb("kc32", [128, 2, 128])
            pj = X[f"proj_{g}"]
            units = [(s_, h) for s_ in range(nseq) for h in range(4)]
            steps = [(qt, p) for qt in range(L // QW) for p in range(2)]

            def setup(u):
                s_, h = units[u]
                pb_ = u % 2
                tsl = slice(s_ * L, (s_ + 1) * L)
                kal, kalk = kall2[pb_], f"akall{pb_}"
                Va, Vak = Vall2[pb_], f"aV{pb_}"
                self.dma("sp", q[:], pj[h * 128:(h + 1) * 128, tsl], [f"proj_{g}"], ["aq"])
                self.dma("sp", k[:], pj[512 + h * 128:512 + (h + 1) * 128, tsl], [f"proj_{g}"], ["ak"])
                self.dma("pool", Va[:, NCK:NK, :],
                         X[f"vtok_{g}"][tsl, 512 + h * 128:512 + (h + 1) * 128].rearrange("(b p) v -> p b v", p=128),
                         [f"vtok_{g}"], [Vak])
                yield
                if g == "s":
                    self.dma("sp", kc32[:], I["ck"][h].rearrange("(c p) d -> p c d", p=128), [], ["kc32"])
                    self.dma("pool", Va[:, 0:2, :], I["cv"][h].rearrange("(c p) d -> p c d", p=128), [], [Vak])
                    yield
                    for (x, xk) in ((q, "aq"), (k, "ak")):
                        for t0 in range(0, L, 512):
                            ps, pk = self.next_ps()
                            self.mm(ps[:], rP[:], x[:, t0:t0 + 512], True, True, ["ropeP", xk], [pk])
                            self.tt("dve", rt[:], ps[:], rS[:, t0:t0 + 512], ALU.mult, [pk, "ropeS"], ["ropet"])
                            yield
                            self.tt("pool", x[:, t0:t0 + 512], x[:, t0:t0 + 512], rC[:, t0:t0 + 512], ALU.mult, [xk, "ropeC"], [xk])
                            self.tt("pool", x[:, t0:t0 + 512], x[:, t0:t0 + 512], rt[:], ALU.add, [xk, "ropet"], [xk])
                            yield
                    for c in range(2):
                        ps, pk = self.next_ps()
                        self.tr(ps[:, 0:128], kc32[:, c, :], self.ident_f[:], ["kc32", "ident_f"], [pk])
                        self.cp("act", kal[:, c * 128:(c + 1) * 128], ps[:, 0:128], [pk], [kalk])
                        yield
                self.cp("act", qb2[pb_][:], q[:], ["aq"], [f"aqb{pb_}"])
                yield
                self.cp("dve", kal[:, NCK * 128:NK * 128], k[:], ["ak"], [kalk])
                yield

            def run(u):
                s_, h = units[u]
                pb_ = u % 2
                tsl = slice(s_ * L, (s_ + 1) * L)
                kal, kalk = kall2[pb_], f"akall{pb_}"
                Va, Vak = Vall2[pb_], f"aV{pb_}"
                qb_, qbk = qb2[pb_], f"aqb{pb_}"

                NKP = NK if NK <= 4 else (2 * NK) // 3
                def s_mm(n, kc):
                    qt, p = steps[n]
                    qsl = slice(qt * QW, (qt + 1) * QW)
                    PP = slice(64 * p, 64 * p + 64)
                    Et, Ek = E[n % 2], f"aE{n % 2}"
                    ps, pk = self.ps[kc % 4], f"ps{kc % 4}"
                    self.mm(ps[:, 0:QW], kal[PP, kc * 128:(kc + 1) * 128], qb_[PP, qsl], True, True, [kalk, qbk], [pk])
                    self.act(Et[:, kc, :], ps[:, 0:QW], AF.Exp, [pk], [Ek], scale=0.125)

                def step(n):
                    qt, p = steps[n]
                    qsl = slice(qt * QW, (qt + 1) * QW)
                    Et, Ek = E[n % 2], f"aE{n % 2}"
                    has_next = n + 1 < len(steps)
                    PSO, PSOK = self.ps[4 + n % 2], f"ps{4 + n % 2}"
                    PSL, PSLK = self.ps[6 + n % 2], f"ps{6 + n % 2}"
                    if NKP < NK:
                        self.S.op("dve", lambda e: e.tensor_reduce(out=Er[:], in_=Et[:, NKP:NK, :].rearrange("p k q -> p q k"),
                                                                   axis=mybir.AxisListType.X, op=ALU.add), [Ek], ["aEr"])
                    for kc in range(NK):
                        if has_next:
                            s_mm(n + 1, kc)
                        self.mm(PSO[:, 0:QW], Va[:, kc, :], Et[:, kc, :], kc == 0, kc == NK - 1, [Vak, Ek], [PSOK])
                        if kc < NKP:
                            self.mm(PSL[:, 0:QW], self.ones_b[:], Et[:, kc, :], kc == 0, (kc == NKP - 1) and NKP == NK,
                                    ["ones_b", Ek], [PSLK])
                    if NKP < NK:
                        self.mm(PSL[:, 0:QW], ones_f[:], Er[:], False, True, ["ones_f", "aEr"], [PSLK])
                    self.act(rl[:], PSL[:, 0:QW], AF.Ln, [PSLK], ["arl"])
                    self.act(rl[:], rl[:], AF.Exp, ["arl"], ["arl"], scale=-1.0)
                    self.tt("dve", on[p][:], PSO[:, 0:QW], rl[:], ALU.mult, [PSOK, "arl"], [f"aon{p}"])
                    if p == 1:
                        self.stt(oc[:, qsl], on[1][:], lamneg[:, 0:1], on[0][:], ALU.mult, ALU.add, ["aon0", "aon1", "lamneg"], ["oTf"])

                for kc in range(NK):
                    s_mm(0, kc)
                yield
                for n in range(len(steps)):
                    step(n)
                    yield
                self.head_out(sc, g, L, 0, s_, oc, None, None, None, dn[:, h:h + 1], "dn", sq, rstd, outb, h * 128, extra_scale=1.0)
                if g == "p":
                    self.dma("sp", O["nck"][s_, h], X[f"vtok_{g}"][tsl, h * 128:(h + 1) * 128], [f"vtok_{g}"], ["nck"])
                    self.dma("sp", O["ncv"][s_, h], X[f"vtok_{g}"][tsl, 512 + h * 128:512 + (h + 1) * 128], [f"vtok_{g}"], ["ncv"])
                yield

            self.interleave([setup(0)])
            for u in range(len(units)):
                gens = [run(u)]
                if u + 1 < len(units):
                    gens.append(setup(u + 1))
                self.interleave(gens)
            if self.dbg == "diff":
                self.dump(f"mixc_{g}", X[f"mix_{g}"][0:512, :], [512, T], BF16, [f"mix_{g}"])

    def gla(self, g):
        I, X, O = self.I, self.X, self.O
        gi = GROUPS[g]
        T, Lseq, nseq0 = gi["T"], gi["L"], gi["nseq"]
        L, nseq = T, 1
        SEGB = Lseq // 128
        NB = L // 128
        GCH = 128
        with self.scope() as sc:
            self.load_scan_masks(sc, L, GCH)
            Zst = sc.sb("Zst", [128, 128])
            self.S.op("dve", lambda e: e.memset(Zst[:], 0.0), [], ["Zst"])
            Sin = {d: sc.sb(f"Sin{d}", [128, 128]) for d in "fb"}
            aw = sc.sb("gaw", [16, 2, 256])
            nab = sc.sb("gnab", [128, 2, 2])
            gn = sc.sb("ggn", [128, 4])
            self.dma("sp", aw[:], I["gla_aw"].rearrange("d r c -> r d c"), [], ["gaw"])
            self.dma("sp", nab[:], I["gla_abT"], [], ["gnab"])
            self.dma("sp", gn[:], I["gla_normT"], [], ["ggn"])
            self.ts("dve", nab[:], nab[:], -1.0, None, ALU.mult, None, ["gnab"], ["gnab"])
            da = {d: sc.sb(f"gda{d}", [16, L]) for d in "fb"}
            qs = sc.sb("qs", [128, L]); kk = sc.sb("kk", [128, L])
            lft = {d: sc.sb(f"lft{d}", [128, L]) for d in "fb"}
            tA = {d: sc.sb(f"tA{d}", [128, L]) for d in "fb"}
            gr = tA["f"]
            qtf = {d: sc.sb(f"qtf{d}", [128, L]) for d in "fb"}
            kh = {d: sc.sb(f"kh{d}", [128, L], BF16) for d in "fb"}
            qt = {d: sc.sb(f"qt{d}", [128, L], BF16) for d in "fb"}
            kt = {d: sc.sb(f"kt{d}", [128, L], BF16) for d in "fb"}
            el = {d: sc.sb(f"el{d}", [128, L // GCH]) for d in "fb"}
            kT = {d: sc.sb(f"kT{d}", [128, NB, 128], BF16) for d in "fb"}
            Vt = sc.sb("Vt", [128, NB, 128], BF16)
            Sst = {d: [sc.sb(f"S{d}{i}", [128, 128]) for i in range(12)] for d in "fb"}
            oT = {d: sc.sb(f"oT{d}", [128, L]) for d in "fb"}
            self.AT = {d: [sc.sb(f"AT{d}{i}", [128, 128], BF16) for i in range(2)] for d in "fb"}
            sq = sc.sb("hsq", [128, L], BF16)
            rstd = sc.sb("hrstd", [128, 512])
            outb = sc.sb("houtb", [128, L], BF16)
            cm = {"f": self.cm128_f, "b": self.cm128_b}
            pj = X[f"proj_{g}"]
            for s in range(nseq):
                tsl = slice(s * L, (s + 1) * L)
                def issue_loads(h):
                    P_ = slice(64 * (h % 2), 64 * (h % 2) + 64)
                    self.dma("sp", qs[P_, :], pj[1536 + 64 * h:1536 + 64 * (h + 1), tsl], [f"proj_{g}"], ["qs"])
                    self.dma("sp", kk[P_, :], pj[1792 + 64 * h:1792 + 64 * (h + 1), tsl], [f"proj_{g}"], ["kk"])
                    if h == 0:
                        for di, d in enumerate("fb"):
                            self.dma("sp", da[d][:], pj[3072 + 16 * di:3088 + 16 * di, tsl], [f"proj_{g}"], [f"gda{d}"])
                    self.dma("pool", Vt[:], X[f"vtok_{g}"][tsl, 1024 + h * 128:1024 + (h + 1) * 128].rearrange("(b p) v -> p b v", p=128),
                             [f"vtok_{g}"], ["Vt"])

                issue_loads(0)
                for h in range(4):
                    pb = 64 * (h % 2)
                    chk = h // 2
                    P = slice(pb, pb + 64)
                    def prep(di, d, h=h, pb=pb, chk=chk, P=P, tsl=tsl):
                        dd, lf_ = da[d], lft[d]
                        dk_, lk_ = f"gda{d}", f"lft{d}"
                        for t0 in range(0, L, 512):
                            w = min(512, L - t0)
                            ps, pk = self.next_ps()
                            self.mm(ps[P, 0:w], aw[:, di, 64 * h:64 * (h + 1)], dd[:, t0:t0 + w], True, True, ["gaw", dk_], [pk])
                            self.act(lf_[P, t0:t0 + w], ps[P, 0:w], AF.Exp, [pk, "gnab"], [lk_], scale=-1.0, bias=nab[P, di, chk:chk + 1])
                            yield
                        self.act(lf_[P, :], lf_[P, :], AF.Ln, [lk_, "ones_col"], [lk_], bias=self.ones_col[P, 0:1])
                        yield
                        self.act(lf_[P, :], lf_[P, :], AF.Copy, [lk_], [lk_], scale=-1.0 / 16.0)
                        yield
                        yield from self.scan_prep_dir(sc, d, L, 64, pb, qs, "qs", 64 ** -0.5, kk, "kk", lf_, lk_, tA[d], qtf[d], qt[d],
                                                      kt[d], kh[d], el[d], kT[d], CH=GCH)

                    self.interleave([prep(0, "f"), prep(1, "b")])
                    if g == "s":
                        for di, d in enumerate("fb"):
                            self.dma("sp", Sin[d][P, :], I["st_gla"][:, di, h, :], [], [f"Sin{d}"])

                    def init_state(d, seg, h=h):
                        return (Zst, "Zst") if g == "p" else (Sin[d], f"Sin{d}")

                    def fin_fn(d, seg, St, sk, h=h, P=P):
                        if g == "p":
                            self.dma("sp", O["nst_gla"][seg, 0 if d == "f" else 1, h], St[P, :], [sk], ["nst_gla"])

                    self.dma("sp", gr[:], pj[2560 + 128 * h:2560 + 128 * (h + 1), tsl], [f"proj_{g}"], ["scAf"])
                    self.scan_chain(sc, L, 64, pb, qtf, qt, kt, el, kT, Vt, "Vt", Sst, oT, cm, init_state, fin_fn, CH=GCH, seg_blocks=SEGB)
                    if h + 1 < 4:
                        issue_loads(h + 1)
                    self.head_out(sc, g, L, 0, s, oT["f"], oT["b"], gr, "scAf", gn[:, h:h + 1], "ggn", sq, rstd, outb, 512 + h * 128)
            if self.dbg == "gla":
                self.dump(f"mixd_{g}", X[f"mix_{g}"][512:1024, :], [512, T], BF16, [f"mix_{g}"])


def _fm(vec, nchunk):
    v = np.asarray(vec, dtype=np.float32)
    lead = v.shape[:-1]
    v = v.reshape(lead + (nchunk, 128))
    return np.ascontiguousarray(np.moveaxis(v, -1, 0))


def _rope_tables():
    half = 32
    inv = (10000.0 ** (-np.arange(0, half, 2, dtype=np.float32) / half)).astype(np.float32)
    t = np.arange(2048)
    row = (t // 64).astype(np.float32)
    col = (t % 64).astype(np.float32)
    Cc = np.zeros((128, 2048), np.float32)
    Ss = np.zeros((128, 2048), np.float32)
    P = np.zeros((128, 128), np.float32)
    for r in range(128):
        d = r % 64
        pos = row if d < 32 else col
        i = d % 32
        fi = i % 16
        ang = (pos * inv[fi]).astype(np.float32)
        Cc[r] = np.cos(ang)
        if i < 16:
            Ss[r] = -np.sin(ang)
            partner = r + 16
        else:
            Ss[r] = np.sin(ang)
            partner = r - 16
        P[partner, r] = 1.0
    return Cc, Ss, P


def prep_core_inputs(inp, core):
    b = core // 4
    f32 = lambda a: np.ascontiguousarray(np.asarray(a, dtype=np.float32))
    m = {}
    xp = f32(inp["x_prompt"][4 * core:4 * core + 4]).reshape(1024, D)
    m["xT_p"] = np.ascontiguousarray(xp.T)
    m["xT_s"] = np.ascontiguousarray(f32(inp["x_sample"][b]).T)
    m["cT"] = np.ascontiguousarray(np.stack([_fm(inp["c_ctx"], 8), _fm(inp["c"][b], 8)], axis=-1))
    m["ada_w"] = f32(inp["ada_w"])
    m["ada_bT"] = _fm(inp["ada_b"], 48)
    m["norm_gT"] = _fm(inp["norm_g"], 8)
    m["ffn_up"] = f32(inp["ffn_up"])
    m["ffn_cwT"] = np.ascontiguousarray(_fm(inp["ffn_conv_w"], 44).transpose(0, 1, 3, 2))
    m["ffn_cbT"] = _fm(inp["ffn_conv_b"], 44)
    m["ffn_down"] = f32(inp["ffn_down"])
    m["w_in_e"] = f32(inp["w_in_even"][0])
    m["w_out_e"] = f32(inp["w_out_even"][0])
    m["lbT"] = np.ascontiguousarray(_fm(inp["hgrn_lb"], 4).reshape(128, 3, 8))
    m["hgrn_normT"] = _fm(inp["hgrn_norm"][0], 4)
    m["hy_cwT"] = np.ascontiguousarray(_fm(inp["hy_conv_w"][0], 12).transpose(0, 2, 1))
    m["hy_cbT"] = _fm(inp["hy_conv_b"][0], 12)
    m["hy_w1"] = f32(inp["hy_w1"][0])
    m["hy_b1"] = f32(inp["hy_b1"][0]).reshape(64, 1)
    m["hy_w2"] = f32(inp["hy_w2"][0])
    m["hy_b2"] = f32(inp["hy_b2"][0]).reshape(64, 1)
    m["hy_w3"] = f32(inp["hy_w3"][0])
    m["hy_freq"] = f32(inp["hy_freq"][0]).reshape(64, 1)
    m["hy_dT"] = _fm(inp["hy_d"][0], 4)
    m["st_hgrn"] = np.ascontiguousarray(f32(inp["state_hgrn"][b, 0]).transpose(2, 0, 1, 3))
    m["w_in_o"] = f32(inp["w_in_odd"][0])
    m["w_out_o"] = f32(inp["w_out_odd"][0])
    m["dlam"] = np.ascontiguousarray(np.broadcast_to(f32(inp["diff_lambda"][0])[None], (128, 4, 64)))
    m["diff_normT"] = _fm(inp["diff_norm"][0], 4)
    m["gla_aw"] = f32(inp["gla_aw"][0])
    m["gla_abT"] = _fm(inp["gla_ab"][0], 2)
    m["gla_normT"] = _fm(inp["gla_norm"][0], 4)
    m["ck"] = f32(inp["cache_diff_k"][b, 0])
    m["cv"] = f32(inp["cache_diff_v"][b, 0])
    m["st_gla"] = np.ascontiguousarray(f32(inp["state_gla"][b, 0]).transpose(2, 0, 1, 3))
    for k, v in make_consts().items():
        m["c_" + k] = v
    Cc, Ss, P = _rope_tables()
    m["c_ropeC"], m["c_ropeS"], m["c_ropeP"] = Cc, Ss, P
    return m


_PROG = {}


def get_prog(dbg=None, nlayers=2):
    key = (dbg, nlayers)
    if key not in _PROG:
        kb = KB(dbg=dbg, nlayers=nlayers)
        kb.build()
        _PROG[key] = kb
    return _PROG[key]


def run_cores(inputs, cores, dbg=None, nlayers=2):
    kb = get_prog(dbg, nlayers)
    in_maps = []
    for c in cores:
        m = prep_core_inputs(inputs, c)
        in_maps.append({k: m[k] for k in kb.in_shapes})
    res = run_bass_kernel_spmd(kb.nc, in_maps, core_ids=list(range(len(cores))))
    return res.results


def kernel(**inputs):
    res = run_cores(inputs, list(range(8)))
    yp = np.zeros((32, 256, D), np.float32)
    ys = np.zeros((2, 2048, D), np.float32)
    nsh = np.zeros((32, 1, 2, 4, 128, 128), np.float32)
    nck = np.zeros((32, 1, 4, 256, 128), np.float32)
    ncv = np.zeros((32, 1, 4, 256, 128), np.float32)
    nsg = np.zeros((32, 1, 2, 4, 64, 128), np.float32)
    for c in range(8):
        r = res[c]
        yp[4 * c:4 * c + 4] = np.asarray(r["yT_p"]).T.reshape(4, 256, D)
        if c % 4 == 0:
            ys[c // 4] = np.asarray(r["yT_s"]).T
        nsh[4 * c:4 * c + 4, 0] = np.asarray(r["nst_hgrn"])
        nck[4 * c:4 * c + 4, 0] = np.asarray(r["nck"])
        ncv[4 * c:4 * c + 4, 0] = np.asarray(r["ncv"])
        nsg[4 * c:4 * c + 4, 0] = np.asarray(r["nst_gla"])
    return (yp, ys, nsh, nck, ncv, nsg)
```

```python
import contextlib
import math
import numpy as np
import ml_dtypes
import concourse.bass as bass
import concourse.mybir as mybir
from concourse.bass_utils import run_bass_kernel_spmd

F32 = mybir.dt.float32
BF16 = mybir.dt.bfloat16
AF = mybir.ActivationFunctionType
ALU = mybir.AluOpType
NPBF = ml_dtypes.bfloat16

N_DMA_SEMS = 72
D = 1024
DFF = 2816
EPS = 1e-6
CH = 32
GROUPS = {"p": dict(nseq=4, L=256, T=1024), "s": dict(nseq=1, L=2048, T=2048)}


class Sched:
    ENG = ("pe", "act", "dve", "pool", "sp")

    def __init__(self, nc, same_engine_sync=True):
        self.nc = nc
        self.sem = {e: nc.alloc_semaphore(name=f"cnt_{e}") for e in self.ENG}
        self.dsem = [nc.alloc_semaphore(name=f"dma_{i}") for i in range(N_DMA_SEMS)]
        self.dval = [0] * N_DMA_SEMS
        self.dnext = 0
        self.cnt = {e: 0 for e in self.ENG}
        self.seen = {e: {} for e in self.ENG}
        self.snap = {e: [None] for e in self.ENG}
        self.last_w = {}
        self.readers = {}
        self.same_engine_sync = same_engine_sync
        self.n_wait = 0
        self.n_ins = 0
        self.engs = {"pe": nc.tensor, "act": nc.scalar, "dve": nc.vector, "pool": nc.gpsimd, "sp": nc.sync}

    def _need(self, e, ev, waits, force=False):
        if ev is None:
            return
        kind, sname, semh, val = ev
        if kind == "eng" and sname == e and not force:
            if e == "pe" or not self.same_engine_sync:
                return
        if self.seen[e].get(sname, 0) >= val:
            return
        cur = waits.get(sname)
        if cur is None or cur[1] < val:
            waits[sname] = (semh, val, kind)

    def _emit_waits(self, e, waits):
        for sname, (semh, val, kind) in waits.items():
            self.engs[e].wait_ge(semh, val)
            self.n_wait += 1
            self.seen[e][sname] = val
            if kind == "eng":
                sn = self.snap[sname][val]
                if sn:
                    se = self.seen[e]
                    for k, v in sn.items():
                        if se.get(k, 0) < v:
                            se[k] = v

    def _collect(self, e, reads, writes, force=False):
        waits = {}
        for k in reads:
            self._need(e, self.last_w.get(k), waits, force)
        for k in writes:
            self._need(e, self.last_w.get(k), waits, force)
            rd = self.readers.get(k)
            if rd:
                for ev in rd.values():
                    self._need(e, ev, waits, force)
        return waits

    def _record(self, ev, reads, writes):
        sname = ev[1]
        for k in reads:
            self.readers.setdefault(k, {})[sname] = ev
        for k in writes:
            self.last_w[k] = ev
            self.readers[k] = {}

    def op(self, e, fn, reads=(), writes=()):
        self._emit_waits(e, self._collect(e, reads, writes))
        self.cnt[e] += 1
        idx = self.cnt[e]
        fn(self.engs[e]).then_inc(self.sem[e], 1)
        self.snap[e].append(dict(self.seen[e]))
        self.n_ins += 1
        ev = ("eng", e, self.sem[e], idx)
        self._record(ev, reads, writes)
        return ev

    def dma(self, q, out, in_, reads=(), writes=(), **kw):
        s = self.dnext
        self.dnext = (self.dnext + 1) % N_DMA_SEMS
        sname = f"d{s}"
        semh = self.dsem[s]
        waits = self._collect(q, reads, writes, True)
        if self.dval[s] > 0:
            self._need(q, ("dma", sname, semh, self.dval[s]), waits)
        self._emit_waits(q, waits)
        self.dval[s] += 16
        self.engs[q].dma_start(out=out, in_=in_, **kw).then_inc(semh, 16)
        self.n_ins += 1
        ev = ("dma", sname, semh, self.dval[s])
        self._record(ev, reads, writes)
        return ev

    def barrier(self, engines=None):
        for e in (engines or self.ENG):
            waits = {}
            for s_ in range(N_DMA_SEMS):
                if self.dval[s_] > 0:
                    self._need(e, ("dma", f"d{s_}", self.dsem[s_], self.dval[s_]), waits, True)
            for f in self.ENG:
                if f != e and self.cnt[f] > 0:
                    self._need(e, ("eng", f, self.sem[f], self.cnt[f]), waits, True)
            self._emit_waits(e, waits)


_CONST_CACHE = {}


def _dft_tables(L):
    n = 2 * L
    SC = L // 128
    NF = SC + 1
    NT = max(1, L // 512)
    TW = min(L, 512)
    s = np.arange(L, dtype=np.float64)
    f = np.arange(NF * 128, dtype=np.float64)
    valid = (f <= L)
    ang = 2.0 * np.pi * np.outer(s, f) / n
    Ac = np.cos(ang) * valid[None]
    As = -np.sin(ang) * valid[None]
    wf = np.where((f == 0) | (f == L), 1.0, 2.0) * valid / n
    Bc = (np.cos(ang) * wf[None]).T
    Bs = (-np.sin(ang) * wf[None]).T
    def fwd(A):
        return np.ascontiguousarray(A.reshape(SC, 128, NF, 128).transpose(2, 1, 0, 3)).astype(NPBF)
    def inv(B):
        return np.ascontiguousarray(B.reshape(NF, 128, NT, TW).transpose(2, 1, 0, 3)).astype(NPBF)
    return fwd(Ac), fwd(As), inv(Bc), inv(Bs)


def _hyena_pos(L):
    t = np.linspace(0.0, 1.0, L, dtype=np.float32)[:, None]
    w = (2.0 * np.float32(math.pi) * np.arange(L, dtype=np.float32)[:, None] / np.float32(L)).astype(np.float32)
    fb = np.linspace(1e-4, 15, 16, dtype=np.float32)[None]
    z = np.concatenate([t, np.cos(fb * w), -np.sin(fb * w)], axis=-1).astype(np.float32)
    max_decay = math.log(1e-2) / 0.3
    min_decay = math.log(1e-2) / 1.5
    deltas = np.abs(np.linspace(min_decay, max_decay, 512, dtype=np.float32))
    window = np.exp(-t * deltas[None]).astype(np.float32)
    w0 = window.copy()
    w1 = window.copy()
    w1[0] = 0.0
    SC = L // 128
    lay = lambda a: np.ascontiguousarray(a.reshape(SC, 128, 512).transpose(1, 0, 2))
    return np.ascontiguousarray(z.T), lay(w0), lay(w1)


def make_consts():
    if _CONST_CACHE:
        return _CONST_CACHE
    c = {}
    c["ident_f"] = np.eye(128, dtype=np.float32)
    c["ident_b"] = np.eye(128).astype(NPBF)
    c["ones_b"] = np.ones((128, 128)).astype(NPBF)
    t = np.arange(2048)
    c["mask_f"] = np.broadcast_to((t % CH != 0).astype(np.float32), (128, 2048)).copy()
    c["mask_b"] = np.broadcast_to((t % CH != CH - 1).astype(np.float32), (128, 2048)).copy()
    s_ = np.arange(128)[:, None]
    t_ = np.arange(128)[None, :]
    same = (s_ // CH) == (t_ // CH)
    c["cm_f"] = (same & (s_ <= t_)).astype(np.float32)
    c["cm_b"] = (same & (s_ >= t_)).astype(np.float32)
    c["ind4"] = ((np.arange(128)[:, None] // CH) == np.arange(4)[None, :]).astype(np.float32)
    c["mask128_f"] = np.broadcast_to((t % 128 != 0).astype(np.float32), (128, 2048)).copy()
    c["mask128_b"] = np.broadcast_to((t % 128 != 127).astype(np.float32), (128, 2048)).copy()
    c["cm128_f"] = (s_ <= t_).astype(np.float32)
    c["cm128_b"] = (s_ >= t_).astype(np.float32)
    for g, L in (("p", 256), ("s", 2048)):
        Ac, As, Bc, Bs = _dft_tables(L)
        c[f"Ac_{g}"], c[f"As_{g}"], c[f"Bc_{g}"], c[f"Bs_{g}"] = Ac, As, Bc, Bs
        zT, w0, w1 = _hyena_pos(L)
        c[f"zT_{g}"], c[f"win0_{g}"], c[f"win1_{g}"] = zT, w0, w1
    _CONST_CACHE.update(c)
    return c


class KB:
    def __init__(self, dbg=None, nlayers=2):
        self.nc = bass.Bass("TRN2", target_bir_lowering=False)
        self.S = Sched(self.nc)
        self.dbg = dbg
        self.nlayers = nlayers
        self.gstack = contextlib.ExitStack()
        self.in_shapes = {}
        self.out_names = []
        self.uid = 0
        self.marks = []

    def din(self, name, shape, dt=F32):
        self.in_shapes[name] = (tuple(shape), dt)
        return self.nc.dram_tensor(name, list(shape), dt, kind="ExternalInput").ap()

    def dout(self, name, shape, dt=F32):
        self.out_names.append(name)
        return self.nc.dram_tensor(name, list(shape), dt, kind="ExternalOutput").ap()

    def dscr(self, name, shape, dt=F32):
        return self.nc.dram_tensor(name, list(shape), dt, kind="Internal").ap()

    def gsb(self, name, shape, dt=F32):
        return self.gstack.enter_context(self.nc.sbuf_tensor(name, list(shape), dt))

    def gps(self, name, shape, dt=F32):
        return self.gstack.enter_context(self.nc.psum_tensor(name, list(shape), dt))

    @contextlib.contextmanager
    def scope(self):
        st = contextlib.ExitStack()
        kb = self

        class Sc:
            def sb(self_, name, shape, dt=F32):
                kb.uid += 1
                return st.enter_context(kb.nc.sbuf_tensor(f"{name}_{kb.uid}", list(shape), dt))
        try:
            yield Sc()
        finally:
            self.S.barrier()
            st.close()

    def dump(self, name, src_ap, shape, dt, keys):
        o = self.dout("dbg_" + name, shape, dt)
        self.dma("sp", o, src_ap, keys, ["dbg_" + name])

    def act(self, out, in_, func, r, w, **kw):
        return self.S.op("act", lambda e: e.activation(out=out, in_=in_, func=func, **kw), r, w)

    def tt(self, eng, out, a, b, op, r, w):
        return self.S.op(eng, lambda e: e.tensor_tensor(out=out, in0=a, in1=b, op=op), r, w)

    def ts(self, eng, out, a, s1, s2, op0, op1, r, w):
        if op1 is None:
            return self.S.op(eng, lambda e: e.tensor_scalar(out=out, in0=a, scalar1=s1, scalar2=None, op0=op0), r, w)
        return self.S.op(eng, lambda e: e.tensor_scalar(out=out, in0=a, scalar1=s1, scalar2=s2, op0=op0, op1=op1), r, w)

    def stt(self, out, in0, scalar, in1, op0, op1, r, w):
        return self.S.op("dve", lambda e: e.scalar_tensor_tensor(out=out, in0=in0, scalar=scalar, in1=in1, op0=op0, op1=op1), r, w)

    def cp(self, eng, out, in_, r, w):
        if eng == "act":
            return self.S.op("act", lambda e: e.copy(out=out, in_=in_), r, w)
        return self.S.op(eng, lambda e: e.tensor_copy(out=out, in_=in_), r, w)

    def mm(self, out, lhsT, rhs, start, stop, r, w):
        return self.S.op("pe", lambda e: e.matmul(out, lhsT=lhsT, rhs=rhs, start=start, stop=stop), r, w)

    def tr(self, out, in_, ident, r, w):
        return self.S.op("pe", lambda e: e.transpose(out, in_, ident), r, w)

    def dma(self, q, out, in_, r, w, **kw):
        return self.S.dma(q, out, in_, r, w, **kw)

    def build(self):
        nc, S = self.nc, self.S
        NL = self.nlayers
        I = {}
        I["xT_p"] = self.din("xT_p", [D, 1024])
        I["xT_s"] = self.din("xT_s", [D, 2048])
        I["cT"] = self.din("cT", [128, 8, 2])
        I["ada_w"] = self.din("ada_w", [2, D, 6 * D])
        I["ada_bT"] = self.din("ada_bT", [128, 2, 48])
        I["norm_gT"] = self.din("norm_gT", [128, 2, 4, 8])
        I["ffn_up"] = self.din("ffn_up", [2, D, 2 * DFF])
        I["ffn_cwT"] = self.din("ffn_cwT", [128, 2, 44, 3])
        I["ffn_cbT"] = self.din("ffn_cbT", [128, 2, 44])
        I["ffn_down"] = self.din("ffn_down", [2, DFF, D])
        I["w_in_e"] = self.din("w_in_e", [D, 4096])
        I["w_out_e"] = self.din("w_out_e", [D, D])
        I["lbT"] = self.din("lbT", [128, 3, 8])
        I["hgrn_normT"] = self.din("hgrn_normT", [128, 4])
        I["hy_cwT"] = self.din("hy_cwT", [128, 12, 3])
        I["hy_cbT"] = self.din("hy_cbT", [128, 12])
        I["hy_w1"] = self.din("hy_w1", [33, 64])
        I["hy_b1"] = self.din("hy_b1", [64, 1])
        I["hy_w2"] = self.din("hy_w2", [64, 64])
        I["hy_b2"] = self.din("hy_b2", [64, 1])
        I["hy_w3"] = self.din("hy_w3", [64, 2048])
        I["hy_freq"] = self.din("hy_freq", [64, 1])
        I["hy_dT"] = self.din("hy_dT", [128, 2, 4])
        I["st_hgrn"] = self.din("st_hgrn", [128, 2, 4, 128])
        I["w_in_o"] = self.din("w_in_o", [D, 3104])
        I["w_out_o"] = self.din("w_out_o", [D, D])
        I["dlam"] = self.din("dlam", [128, 4, 64])
        I["diff_normT"] = self.din("diff_normT", [128, 4])
        I["gla_aw"] = self.din("gla_aw", [2, 16, 256])
        I["gla_abT"] = self.din("gla_abT", [128, 2, 2])
        I["gla_normT"] = self.din("gla_normT", [128, 4])
        I["ck"] = self.din("ck", [4, 256, 128])
        I["cv"] = self.din("cv", [4, 256, 128])
        I["st_gla"] = self.din("st_gla", [64, 2, 4, 128])
        C = {}
        for k, v in make_consts().items():
            C[k] = self.din("c_" + k, v.shape, BF16 if v.dtype == NPBF else F32)
        C["ropeC"] = self.din("c_ropeC", [128, 2048])
        C["ropeS"] = self.din("c_ropeS", [128, 2048])
        C["ropeP"] = self.din("c_ropeP", [128, 128])
        self.I, self.C = I, C
        O = {}
        O["yT_p"] = self.dout("yT_p", [D, 1024])
        O["yT_s"] = self.dout("yT_s", [D, 2048])
        O["nst_hgrn"] = self.dout("nst_hgrn", [4, 2, 4, 128, 128])
        O["nck"] = self.dout("nck", [4, 4, 256, 128])
        O["ncv"] = self.dout("ncv", [4, 4, 256, 128])
        O["nst_gla"] = self.dout("nst_gla", [4, 2, 4, 64, 128])
        self.O = O
        X = {}
        for g, gi in GROUPS.items():
            T = gi["T"]
            X[f"y_{g}"] = self.dscr(f"y_{g}", [D, T])
            X[f"proj_{g}"] = self.dscr(f"proj_{g}", [4096, T])
            X[f"vtok_{g}"] = self.dscr(f"vtok_{g}", [T, 1536])
            X[f"mix_{g}"] = self.dscr(f"mix_{g}", [D, T], BF16)
            X[f"ffa_{g}"] = self.dscr(f"ffa_{g}", [DFF, T], BF16)
            X[f"hyc_{g}"] = self.dscr(f"hyc_{g}", [1536, T])
            X[f"ksp_{g}"] = self.dscr(f"ksp_{g}", [2, 2, gi["L"] // 128 + 1, 128, 512], BF16)
        self.X = X
        self.ident_f = self.gsb("ident_f", [128, 128])
        self.ident_b = self.gsb("ident_b", [128, 128], BF16)
        self.ones_b = self.gsb("ones_b", [128, 128], BF16)
        self.cm_f = self.gsb("cm_f", [128, 128])
        self.cm_b = self.gsb("cm_b", [128, 128])
        self.ind4 = self.gsb("ind4", [128, 4])
        self.cm128_f = self.gsb("cm128_f", [128, 128])
        self.cm128_b = self.gsb("cm128_b", [128, 128])
        self.ones_col = self.gsb("ones_col", [128, 1])
        self.eps_col = self.gsb("eps_col", [128, 1])
        self.mod = [self.gsb(f"mod{l}", [128, 48, 2]) for l in range(2)]
        self.normg = self.gsb("normg", [128, 2, 4, 8])
        self.gs = self.gsb("gs", [128, 2, 2, 8, 2])
        self.gg = self.gsb("gg", [128, 2, 2, 8, 2])
        self.ps = [self.gps(f"ps{i}", [128, 512]) for i in range(8)]
        self.psT = self.ps[7][:].bitcast(BF16)
        self.ps_rot = 0
        for nm in ("ident_f", "ident_b", "ones_b", "cm_f", "cm_b", "ind4", "cm128_f", "cm128_b"):
            self.dma("sp", getattr(self, nm)[:], C[nm], [], [nm])
        S.op("dve", lambda e: e.memset(self.ones_col[:], 1.0), [], ["ones_col"])
        S.op("dve", lambda e: e.memset(self.eps_col[:], EPS), [], ["eps_col"])
        self.dma("sp", self.normg[:], I["norm_gT"], [], ["normg"])

        self.bg = None
        self.mods_setup()
        with self.scope() as scm:
            aw = [scm.sb(f"aw{i}", [128, 6 * D]) for i in range(2)]
            for _ in self.mods_gen(0, aw):
                pass
            if self.dbg == "mods" and NL > 1:
                for _ in self.mods_gen(1, aw):
                    pass
        if self.dbg == "mods":
            self.dump("mod0", self.mod[0][:], [128, 48, 2], F32, ["mod0"])
            self.dump("gs", self.gs[:], [128, 2, 2, 8, 2], F32, ["gs"])
        else:
            for l in range(NL):
                for g in ("p", "s"):
                    self.layer(l, g)
        self.mark("final")
        for g in ("p", "s"):
            self.dma("sp", O[f"yT_{g}"], X[f"y_{g}"], self.ykeys(g), [f"out_y_{g}"])
        S.barrier(["sp"])
        self.gstack.close()
        return nc

    def next_ps(self):
        i = self.ps_rot
        self.ps_rot = (self.ps_rot + 1) % 7
        return self.ps[i], f"ps{i}"

    def mods_setup(self):
        I = self.I
        self.m_cT = self.gsb("m_cT", [128, 8, 2])
        self.m_sT = self.gsb("m_sT", [128, 8, 2])
        self.m_ab = self.gsb("m_ab", [128, 2, 48])
        self.dma("sp", self.m_cT[:], I["cT"], [], ["cT"])
        self.dma("sp", self.m_ab[:], I["ada_bT"], [], ["ab"])
        self.act(self.m_sT[:], self.m_cT[:], AF.Silu, ["cT"], ["sT"])

    def mods_gen(self, l, aw):
        I = self.I
        sT, ab = self.m_sT, self.m_ab
        mod = self.mod[l]
        for j in range(8):
            a = aw[j % 2]
            self.dma("sp", a[:], I["ada_w"][l, j * 128:(j + 1) * 128, :], [], [f"aw{j % 2}"])
            yield
            ps, pk = self.ps[0], "ps0"
            for ch in range(48):
                self.mm(ps[:, ch * 2:ch * 2 + 2], a[:, ch * 128:(ch + 1) * 128], sT[:, j, :], True, True,
                        [f"aw{j % 2}", "sT"], [pk])
            pv = ps[:, 0:96].rearrange("p (c k) -> p c k", k=2)
            if j == 0:
                self.cp("dve", mod[:], pv, [pk], [f"mod{l}"])
            else:
                self.tt("dve", mod[:], mod[:], pv, ALU.add, [pk, f"mod{l}"], [f"mod{l}"])
            yield
        for cnd in range(2):
            self.tt("dve", mod[:, :, cnd], mod[:, :, cnd], ab[:, l, :], ALU.add, ["ab", f"mod{l}"], [f"mod{l}"])
        for which, (nidx, sc_lo, gnidx, gate_lo) in enumerate(((0, 8, 1, 16), (2, 32, 3, 40))):
            for cnd in range(2):
                self.stt(self.gs[:, l, which, :, cnd], mod[:, sc_lo:sc_lo + 8, cnd], 1.0, self.normg[:, l, nidx, :],
                         ALU.add, ALU.mult, [f"mod{l}", "normg"], [f"gs{l}"])
                self.tt("dve", self.gg[:, l, which, :, cnd], mod[:, gate_lo:gate_lo + 8, cnd], self.normg[:, l, gnidx, :],
                        ALU.mult, [f"mod{l}", "normg"], [f"gg{l}"])
        yield

    def bg_step(self, n=1):
        for _ in range(n):
            if self.bg is not None:
                try:
                    next(self.bg)
                except StopIteration:
                    self.bg = None

    def rstd_from_ps(self, rstd, ps, pk, n, r_extra, wkey, cols=512):
        self.act(rstd, ps, AF.Ln, [pk] + r_extra, [wkey], scale=1.0 / n, bias=self.eps_col[:, 0:1])
        self.act(rstd, rstd, AF.Exp, [wkey], [wkey], scale=-0.5)

    def norm_mod(self, sc, g, l, which, src_ap, src_key, on_tile=None):
        T = GROUPS[g]["T"]
        NT_ = T // 512
        cnd = 0 if g == "p" else 1
        ysb = [sc.sb(f"ysb{i}", [128, 8, 512]) for i in range(2)]
        sq = [sc.sb(f"sq{i}", [128, 8, 512], BF16) for i in range(2)]
        tmp = sc.sb("tmp", [128, 8, 512])
        rstd = [sc.sb(f"rstd{i}", [128, 512]) for i in range(2)]
        shift_lo = 0 if which == 0 else 24
        srcv = src_ap.rearrange("(j p) t -> p j t", p=128)

        def stats(tt):
            b_ = tt % 2
            tsl = slice(tt * 512, (tt + 1) * 512)
            self.dma("sp", ysb[b_][:], srcv[:, :, tsl], [src_key if src_key.startswith("xT") else f"{src_key}_{tt}"], [f"ysb{b_}"])
            self.act(sq[b_][:], ysb[b_][:], AF.Square, [f"ysb{b_}"], [f"sq{b_}"])
            ps, pk = self.next_ps()
            for j in range(8):
                self.mm(ps[:], self.ones_b[:], sq[b_][:, j, :], j == 0, j == 7, ["ones_b", f"sq{b_}"], [pk])
            self.rstd_from_ps(rstd[b_][:], ps[:], pk, D, [], f"rstd{b_}")

        stats(0)
        for tt in range(NT_):
            b_ = tt % 2
            tsl = slice(tt * 512, (tt + 1) * 512)
            if tt + 1 < NT_:
                stats(tt + 1)
            for j in range(8):
                self.stt(tmp[:, j, :], ysb[b_][:, j, :], self.gs[:, l, which, j, cnd:cnd + 1], rstd[b_][:], ALU.mult, ALU.mult,
                         [f"ysb{b_}", f"gs{l}", f"rstd{b_}"], [f"tmp{j}"])
                self.act(self.hT[:, j, tsl], tmp[:, j, :], AF.Identity, [f"tmp{j}", f"mod{l}"], [f"hT{tt}"],
                         bias=self.mod[l][:, shift_lo + j, cnd:cnd + 1], scale=1.0)
            if on_tile is not None:
                on_tile(tt)

    def load_w(self, wb, wkey, w_ap, KC, c0, ncols):
        wv = w_ap.rearrange("(k p) n -> p k n", p=128)
        step = 4
        for k0 in range(0, KC, step):
            k1 = min(KC, k0 + step)
            self.dma("pool", wb[:, k0:k1, 0:ncols], wv[:, k0:k1, c0:c0 + ncols], [], [wkey])

    def res_tiles(self, sc):
        return [dict(msb=sc.sb(f"msb{i}", [128, 8, 512]), sq=sc.sb(f"rsq{i}", [128, 8, 512], BF16),
                     rstd=sc.sb(f"rrstd{i}", [128, 512]), ysb=sc.sb(f"rysb{i}", [128, 8, 512]), i=i) for i in range(2)]

    def ykeys(self, g):
        return [f"y_{g}_{i}" for i in range(GROUPS[g]["T"] // 512)]

    def residual_load(self, g, ts_, ysrc_ap, ysrc_key, tt):
        i = ts_["i"]
        srcv = ysrc_ap.rearrange("(j p) t -> p j t", p=128)
        tsl = slice(tt * 512, (tt + 1) * 512)
        self.dma("sp", ts_["ysb"][:], srcv[:, :, tsl], [ysrc_key if ysrc_key.startswith("xT") else f"y_{g}_{tt}"], [f"rysb{i}"])

    def residual(self, g, l, which, ts_, ysrc_ap, ysrc_key, tt):
        cnd = 0 if g == "p" else 1
        i = ts_["i"]
        msb, sq, rstd, ysb = ts_["msb"], ts_["sq"], ts_["rstd"], ts_["ysb"]
        mk, sk, rk, yk = f"msb{i}", f"rsq{i}", f"rrstd{i}", f"rysb{i}"
        tsl = slice(tt * 512, (tt + 1) * 512)
        dstv = self.X[f"y_{g}"].rearrange("(j p) t -> p j t", p=128)
        ykey = f"y_{g}_{tt}"
        self.act(sq[:], msb[:], AF.Square, [mk], [sk])
        ps, pk = self.next_ps()
        for j in range(8):
            self.mm(ps[:], self.ones_b[:], sq[:, j, :], j == 0, j == 7, ["ones_b", sk], [pk])
        self.rstd_from_ps(rstd[:], ps[:], pk, D, [], rk)
        for j in range(8):
            self.stt(msb[:, j, :], msb[:, j, :], self.gg[:, l, which, j, cnd:cnd + 1], rstd[:], ALU.mult, ALU.mult,
                     [mk, f"gg{l}", rk], [f"{mk}_{j}"])
            self.tt("pool", msb[:, j, :], msb[:, j, :], ysb[:, j, :], ALU.add, [f"{mk}_{j}", yk], [f"{mk}_{j}"])
        self.dma("sp", dstv[:, :, tsl], msb[:], [f"{mk}_{j}" for j in range(8)], [ykey])

    def mark(self, name):
        self.marks.append((name, dict(self.S.cnt)))

    def layer(self, l, g):
        X, I = self.X, self.I
        T = GROUPS[g]["T"]
        self.mark(f"L{l}{g} norm+proj")
        ysrc = (I[f"xT_{g}"], f"xT_{g}") if l == 0 else (X[f"y_{g}"], f"y_{g}")
        with self.scope() as sc:
            self.hT = sc.sb("hT", [128, 8, T], BF16)
            nrm = (l, 0, ysrc[0], ysrc[1])
            if l % 2 == 0:
                self.proj_in(sc, g, I["w_in_e"], 4096, tok_blocks={3: 0}, norm=nrm)
            else:
                self.proj_in(sc, g, I["w_in_o"], 3104, tok_blocks={1: 0, 2: 512, 4: 1024}, norm=nrm)
            if self.dbg == "proj":
                self.dump(f"hT_{g}", self.hT[:, :, 0:T], [128, 8, T], BF16, [f"hT{i}" for i in range(T // 512)])
                self.dump(f"proj_{g}", X[f"proj_{g}"], [4096, T], F32, [f"proj_{g}"])
                self.dump(f"vtok_{g}", X[f"vtok_{g}"], [T, 1536], F32, [f"vtok_{g}"])
        if self.dbg == "proj":
            return
        self.mark(f"L{l}{g} mixerA")
        if l % 2 == 0:
            if g == "p" and self.nlayers > 1:
                with self.scope() as scm:
                    aw = [scm.sb(f"aw{i}", [128, 6 * D]) for i in range(2)]
                    self.bg = self.mods_gen(1, aw)
                    self.hgrn(g)
                    self.bg_step(100)
            else:
                self.hgrn(g)
            if self.dbg == "hgrn":
                return
            self.mark(f"L{l}{g} mixerB")
            self.hyena(g)
            if self.dbg == "hyena":
                return
            w_out = I["w_out_e"]
        else:
            self.diffattn(g)
            if self.dbg == "diff":
                return
            self.mark(f"L{l}{g} mixerB")
            self.gla(g)
            if self.dbg == "gla":
                return
            w_out = I["w_out_o"]
        self.mark(f"L{l}{g} outproj")
        with self.scope() as sc:
            self.hT = sc.sb("hT", [128, 8, T], BF16)
            wb = sc.sb("wout", [128, 8, D], BF16)
            self.load_w(wb, "wout", w_out, 8, 0, D)
            self.dma("sp", self.hT[:, :, 0:T], X[f"mix_{g}"].rearrange("(j p) t -> p j t", p=128), [f"mix_{g}"], [f"hT{i}" for i in range(T // 512)])
            rts = self.res_tiles(sc)
            self.residual_load(g, rts[0], ysrc[0], ysrc[1], 0)
            for tt in range(T // 512):
                tsl = slice(tt * 512, (tt + 1) * 512)
                ts_ = rts[tt % 2]
                if tt + 1 < T // 512:
                    self.residual_load(g, rts[(tt + 1) % 2], ysrc[0], ysrc[1], tt + 1)
                for m in range(8):
                    ps, pk = self.next_ps()
                    for k in range(8):
                        self.mm(ps[:], wb[:, k, m * 128:(m + 1) * 128], self.hT[:, k, tsl], k == 0, k == 7, ["wout", f"hT{tt}"], [pk])
                    self.cp("act" if m % 2 else "dve", ts_["msb"][:, m, :], ps[:], [pk], [f"msb{tt % 2}", f"msb{tt % 2}_{m}"])
                self.residual(g, l, 0, ts_, ysrc[0], ysrc[1], tt)
        self.mark(f"L{l}{g} ffn")
        with self.scope() as scA:
            wd = scA.sb("wd", [128, 22, D], BF16)
            with self.scope() as scB:
                self.hT = scB.sb("hT", [128, 8, T], BF16)
                with self.scope() as scC:
                    self.norm_mod(scC, g, l, 1, X[f"y_{g}"], f"y_{g}")
                with self.scope() as scD:
                    self.ffn_up(scD, g, l, wd)
            self.mark(f"L{l}{g} ffn_down")
            with self.scope() as scE:
                self.ffn_down(scE, g, l, wd)

    def proj_in(self, sc, g, w_ap, ncols, tok_blocks, norm=None):
        T = GROUPS[g]["T"]
        X = self.X
        wbs = [sc.sb(f"wb{i}", [128, 8, 512], BF16) for i in range(2)]
        stg = [sc.sb(f"stg{i}", [128, 512]) for i in range(4)]
        st_ = {"si": 0}
        nblk = (ncols + 511) // 512

        stg4 = [sc.sb(f"stgq{i}", [128, 4, 512]) for i in range(2)]

        def fm_tile(cb, tt):
            c0 = cb * 512
            nc_ = min(512, ncols - c0)
            wb, wk = wbs[cb % 2], f"wb{cb % 2}"
            tsl = slice(tt * 512, (tt + 1) * 512)
            if nc_ == 512:
                qi = st_["qi"] = st_.get("qi", 0) + 1
                st4, s4k = stg4[qi % 2], f"stgq{qi % 2}"
                for m in range(4):
                    ps, pk = self.next_ps()
                    for k in range(8):
                        self.mm(ps[:], wb[:, k, m * 128:(m + 1) * 128], self.hT[:, k, tsl], k == 0, k == 7, [wk, f"hT{tt}"], [pk])
                    self.cp("act" if m % 2 else "dve", st4[:, m, :], ps[:], [pk], [f"{s4k}_{m}"])
                self.dma("sp", X[f"proj_{g}"][c0:c0 + 512, tsl].rearrange("(m p) t -> p m t", p=128), st4[:],
                         [f"{s4k}_{m}" for m in range(4)], [f"proj_{g}"])
                return
            for m in range((nc_ + 127) // 128):
                mw = min(128, nc_ - m * 128)
                ps, pk = self.next_ps()
                for k in range(8):
                    self.mm(ps[0:mw, :], wb[:, k, m * 128:m * 128 + mw], self.hT[:, k, tsl], k == 0, k == 7, [wk, f"hT{tt}"], [pk])
                si = st_["si"]
                st, sk = stg[si % 4], f"stg{si % 4}"
                self.cp("act" if si % 2 else "dve", st[0:mw, :], ps[0:mw, :], [pk], [sk])
                st_["si"] += 1
                self.dma("sp", X[f"proj_{g}"][c0 + m * 128:c0 + m * 128 + mw, tsl], st[0:mw, :], [sk], [f"proj_{g}"])

        self.load_w(wbs[0], "wb0", w_ap, 8, 0, min(512, ncols))
        if norm is not None:
            l, which, src_ap, src_key = norm
            self.norm_mod(sc, g, l, which, src_ap, src_key, on_tile=lambda tt: fm_tile(0, tt))
        for cb in range(nblk):
            c0 = cb * 512
            nc_ = min(512, ncols - c0)
            wb, wk = wbs[cb % 2], f"wb{cb % 2}"
            if cb > 0:
                self.load_w(wb, wk, w_ap, 8, c0, nc_)
            if cb > 0 or norm is None:
                for tt in range(T // 512):
                    fm_tile(cb, tt)
            if cb in tok_blocks and tok_blocks[cb] is not None:
                off = tok_blocks[cb]
                for tb in range(T // 128):
                    ps, pk = self.next_ps()
                    for k in range(8):
                        self.mm(ps[:], self.hT[:, k, tb * 128:(tb + 1) * 128], wb[:, k, :], k == 0, k == 7, [wk, f"hT{tb // 4}"], [pk])
                    si = st_["si"]
                    st, sk = stg[si % 4], f"stg{si % 4}"
                    self.cp("act" if si % 2 else "dve", st[:], ps[:], [pk], [sk])
                    st_["si"] += 1
                    self.dma("sp", X[f"vtok_{g}"][tb * 128:(tb + 1) * 128, off:off + 512], st[:], [sk], [f"vtok_{g}"])

    def ffn_up(self, sc, g, l, wd):
        I, X = self.I, self.X
        gi = GROUPS[g]
        T, L, nseq = gi["T"], gi["L"], gi["nseq"]
        wbs = [sc.sb(f"wu{i}", [128, 8, 256], BF16) for i in range(2)]
        cw = sc.sb("cw", [128, 44, 3])
        cb = sc.sb("cb", [128, 44])
        self.dma("sp", cw[:], I["ffn_cwT"][:, l], [], ["cw"])
        self.dma("sp", cb[:], I["ffn_cbT"][:, l], [], ["cb"])
        raw4 = [sc.sb(f"raw{i}", [128, T]) for i in range(4)]
        cv4 = [sc.sb(f"cv{i}", [128, T]) for i in range(4)]
        obs = [sc.sb(f"ffo{i}", [128, T], BF16) for i in range(2)]
        wv = I["ffn_up"][l].rearrange("(k p) n -> p k n", p=128)
        def tail(i):
            ob, obk = obs[i % 2], f"ffo{i % 2}"
            b0, b1 = (i % 2) * 2, (i % 2) * 2 + 1
            for half in range(2):
                chn = i + 22 * half
                bi = (i % 2) * 2 + half
                self.dwconv_shift(raw4[bi], f"raw{bi}", cv4[bi], f"cv{bi}", cw[:, chn, :], ["cw"], nseq, L)
            self.act(cv4[b1][:], cv4[b1][:], AF.Silu, [f"cv{b1}"], [f"cv{b1}"])
            self.tt("dve", ob[:], cv4[b1][:], cv4[b0][:], ALU.mult, [f"cv{b0}", f"cv{b1}"], [obk])
            self.dma("sp", X[f"ffa_{g}"][i * 128:(i + 1) * 128, :], ob[:], [obk], [f"ffa_{g}"])

        wdv = I["ffn_down"][l].rearrange("(k p) n -> p k n", p=128)
        for i in range(23):
            if i < 22:
                wb, wk = wbs[i % 2], f"wu{i % 2}"
                self.dma("pool", wb[:, :, 0:128], wv[:, :, i * 128:(i + 1) * 128], [], [wk])
                self.dma("pool", wb[:, :, 128:256], wv[:, :, DFF + i * 128:DFF + (i + 1) * 128], [], [wk])
                if 2 <= i < 13:
                    k0 = (i - 2) * 2
                    self.dma("pool", wd[:, k0:k0 + 2, :], wdv[:, k0:k0 + 2, :], [], ["wd"])
                for half in range(2):
                    chn = i + 22 * half
                    bi = (i % 2) * 2 + half
                    for tt in range(T // 512):
                        tsl = slice(tt * 512, (tt + 1) * 512)
                        ps, pk = self.next_ps()
                        for k in range(8):
                            self.mm(ps[:], wb[:, k, half * 128:(half + 1) * 128], self.hT[:, k, tsl], k == 0, k == 7, [wk, f"hT{tt}"], [pk])
                        self.cp("act", raw4[bi][:, tsl], ps[:], [pk], [f"raw{bi}"])
                    self.act(cv4[bi][:], raw4[bi][:], AF.Identity, [f"raw{bi}", "cw", "cb"], [f"cv{bi}"],
                             scale=cw[:, chn, 1:2], bias=cb[:, chn:chn + 1])
            if i >= 1:
                tail(i - 1)

    def dwconv_shift(self, x, xk, o, ok, w3, wkeys, nseq, L):
        xv = x[:].rearrange("p (s t) -> p s t", t=L)
        ov = o[:].rearrange("p (s t) -> p s t", t=L)
        self.stt(ov[:, :, 1:L], xv[:, :, 0:L - 1], w3[:, 0:1], ov[:, :, 1:L], ALU.mult, ALU.add, [xk, ok] + wkeys, [ok])
        self.stt(ov[:, :, 0:L - 1], xv[:, :, 1:L], w3[:, 2:3], ov[:, :, 0:L - 1], ALU.mult, ALU.add, [xk, ok] + wkeys, [ok])

    def dwconv(self, x, xk, o, ok, w3, b, wkeys, nseq, L):
        xv = x[:].rearrange("p (s t) -> p s t", t=L)
        ov = o[:].rearrange("p (s t) -> p s t", t=L)
        self.act(o[:], x[:], AF.Identity, [xk] + wkeys, [ok], scale=w3[:, 1:2], bias=b)
        self.stt(ov[:, :, 1:L], xv[:, :, 0:L - 1], w3[:, 0:1], ov[:, :, 1:L], ALU.mult, ALU.add, [xk, ok] + wkeys, [ok])
        self.stt(ov[:, :, 0:L - 1], xv[:, :, 1:L], w3[:, 2:3], ov[:, :, 0:L - 1], ALU.mult, ALU.add, [xk, ok] + wkeys, [ok])

    def ffn_down(self, sc, g, l, wd):
        I, X = self.I, self.X
        T = GROUPS[g]["T"]
        aa = [sc.sb(f"ffa{i}", [128, 22, 512], BF16) for i in range(2)]
        rts = self.res_tiles(sc)
        av = X[f"ffa_{g}"].rearrange("(k p) t -> p k t", p=128)
        NT_ = T // 512
        self.dma("sp", aa[0][:], av[:, :, 0:512], [f"ffa_{g}"], ["ffa0"])
        self.residual_load(g, rts[0], X[f"y_{g}"], f"y_{g}", 0)
        for tt in range(NT_):
            tsl = slice(tt * 512, (tt + 1) * 512)
            a, ak = aa[tt % 2], f"ffa{tt % 2}"
            ts_ = rts[tt % 2]
            if tt + 1 < NT_:
                self.dma("sp", aa[(tt + 1) % 2][:], av[:, :, (tt + 1) * 512:(tt + 2) * 512], [f"ffa_{g}"], [f"ffa{(tt + 1) % 2}"])
                self.residual_load(g, rts[(tt + 1) % 2], X[f"y_{g}"], f"y_{g}", tt + 1)
            for m in range(8):
                ps, pk = self.next_ps()
                for k in range(22):
                    self.mm(ps[:], wd[:, k, m * 128:(m + 1) * 128], a[:, k, :], k == 0, k == 21, ["wd", ak], [pk])
                self.cp("act" if m % 2 else "dve", ts_["msb"][:, m, :], ps[:], [pk], [f"msb{tt % 2}", f"msb{tt % 2}_{m}"])
            self.residual(g, l, 1, ts_, X[f"y_{g}"], f"y_{g}", tt)

    def scan_prep_dir(self, sc, d, L, dk, pb, qs, qk, qscale, kk, kkk, lf, lfk, tmpA, qtf, qt, kt, kh, el, kT, CH=CH):
        P = slice(pb, pb + dk)
        NB = L // 128
        NCH = L // CH
        cum = tmpA
        ak, hk = f"scA{d}", f"kh{d}"
        if d == "f":
            self.S.op("dve", lambda e: e.tensor_tensor_scan(out=cum[P, 0:L], data0=self.mask_f[P, 0:L], data1=lf[P, 0:L],
                                                             initial=0.0, op0=ALU.mult, op1=ALU.add), [lfk, "mask_f"], [ak])
        else:
            self.S.op("dve", lambda e: e.tensor_tensor_scan(out=cum[P, 0:L][:, ::-1],
                                                             data0=self.mask_b[P, 0:L][:, ::-1], data1=lf[P, 0:L][:, ::-1],
                                                             initial=0.0, op0=ALU.mult, op1=ALU.add), [lfk, "mask_b"], [ak])
        yield
        self.act(qtf[P, 0:L], cum[P, 0:L], AF.Exp, [ak], [f"qtf{d}"])
        yield
        ev = qtf[P, 0:L].rearrange("p (c t) -> p c t", t=CH)
        pos = CH - 1 if d == "f" else 0
        self.cp("dve", el[P, 0:NCH], ev[:, :, pos], [f"qtf{d}"], [f"el{d}"])
        yield
        self.stt(qtf[P, 0:L], qs[P, 0:L], float(qscale), qtf[P, 0:L], ALU.mult, ALU.mult, [qk, f"qtf{d}", f"el{d}"], [f"qtf{d}"])
        yield
        self.cp("act", qt[P, 0:L], qtf[P, 0:L], [f"qtf{d}"], [f"qt{d}"])
        yield
        self.act(cum[P, 0:L], cum[P, 0:L], AF.Exp, [ak], [ak], scale=-1.0)
        yield
        self.tt("dve", kt[P, 0:L], kk[P, 0:L], cum[P, 0:L], ALU.mult, [kkk, ak], [f"kt{d}"])
        yield
        elb = el[P, 0:NCH].unsqueeze(2).broadcast_to([dk, NCH, CH])
        self.tt("dve", kh[P, 0:L].rearrange("p (c t) -> p c t", t=CH), kt[P, 0:L].rearrange("p (c t) -> p c t", t=CH), elb,
                ALU.mult, [f"kt{d}", f"el{d}"], [hk])
        yield
        for blk in range(NB):
            j = blk % 8
            self.tr(self.psT[:, j * 128:j * 128 + dk], kh[P, blk * 128:(blk + 1) * 128], self.ident_b[P, P], [hk, "ident_b"], ["ps7"])
            if j == 7 or blk == NB - 1:
                n = j + 1
                b0 = blk - j
                self.cp("act" if (blk // 8) % 2 else "dve", kT[:, b0:b0 + n, 0:dk],
                        self.psT[:, 0:n * 128].rearrange("p (b c) -> p b c", c=128)[:, :, 0:dk], ["ps7"], [f"kT{d}"])
                yield

    @staticmethod
    def interleave(gens):
        gens = list(gens)
        while gens:
            for g_ in list(gens):
                try:
                    next(g_)
                except StopIteration:
                    gens.remove(g_)

    def scan_chain(self, sc, L, dk, pb, qtf, qt, kt, el, kT, Vt, vk, Sst, oT, cm, init_state, fin_fn, CH=CH, seg_blocks=None):
        P = slice(pb, pb + dk)
        NB = L // 128
        SB = seg_blocks or NB
        DIRS = ("f", "b")
        NR = 12
        PA, PAK = self.ps[0], "ps0"
        POV, POVK = self.ps[1], "ps1"
        PD = {"f": ((self.ps[2], "ps2"), (self.ps[3], "ps3")), "b": ((self.ps[4], "ps4"), (self.ps[5], "ps5"))}
        PIN = {"f": (self.ps[6], "ps6"), "b": (self.ps[7], "ps7")}
        blk_of = lambda d, i: i if d == "f" else NB - 1 - i
        NCB = 128 // CH
        order = {"f": tuple(range(NCB)), "b": tuple(reversed(range(NCB)))}
        nst = {"f": 0, "b": 0}
        cur = {}
        prevref = {}

        def stage_att(i):
            for di, d in enumerate(DIRS):
                blk = blk_of(d, i)
                t0 = blk * 128
                sl = (i % 2) * 2 + di
                pa = PA[:, sl * 128:(sl + 1) * 128]
                self.mm(pa, kt[d][P, t0:t0 + 128], qt[d][P, t0:t0 + 128], True, True, [f"kt{d}", f"qt{d}"], [PAK])
                self.tt("dve", self.AT[d][i % 2][:], pa, cm[d][:], ALU.mult, [PAK, "cm"], [f"AT{d}{i % 2}"])

        def stage_pe(i):
            for di, d in enumerate(DIRS):
                blk = blk_of(d, i)
                sl = (i % 2) * 2 + di
                pov = POV[:, sl * 128:(sl + 1) * 128]
                self.mm(pov, Vt[:, blk, :], self.AT[d][i % 2][:], True, True, [vk, f"AT{d}{i % 2}"], [POVK])
                pd, pdk = PD[d][i % 2]
                if NCB == 1:
                    self.mm(pd[P, 0:128], kT[d][:, blk, 0:dk], Vt[:, blk, :], True, True, [f"kT{d}", vk], [pdk])
                else:
                    for c in range(NCB):
                        self.mm(pd[P, c * 128:(c + 1) * 128], kT[d][:, blk, 0:dk], self.Vm[:, c, blk, :], True, True,
                                [f"kT{d}", "Vm"], [pdk])

        def stage_chain(i):
            for idx in range(NCB):
                for di, d in enumerate(DIRS):
                    blk = blk_of(d, i)
                    seg = blk // SB
                    c = order[d][idx]
                    pd, pdk = PD[d][i % 2]
                    if i % SB == 0 and idx == 0:
                        cur[d] = init_state(d, seg)
                    st0, sk0 = cur[d]
                    prevref[(d, i, idx)] = cur[d]
                    k1 = nst[d] % NR
                    nst[d] += 1
                    ci = blk * NCB + c
                    self.stt(Sst[d][k1][P, :], st0[P, :], el[d][P, ci:ci + 1], pd[P, c * 128:(c + 1) * 128], ALU.mult, ALU.add,
                             [sk0, f"el{d}", pdk], [f"S{d}{k1}"])
                    cur[d] = (Sst[d][k1], f"S{d}{k1}")
                    if i % SB == SB - 1 and idx == NCB - 1:
                        fin_fn(d, seg, Sst[d][k1], f"S{d}{k1}")
            for di, d in enumerate(DIRS):
                blk = blk_of(d, i)
                sl = (i % 2) * 2 + di
                self.cp("act", oT[d][:, blk * 128:(blk + 1) * 128], POV[:, sl * 128:(sl + 1) * 128], [POVK], [f"oT{d}"])

        def stage_inter(i):
            for di, d in enumerate(DIRS):
                blk = blk_of(d, i)
                t0 = blk * 128
                pin, pink = PIN[d]
                sl = i % 4
                for idx in range(NCB):
                    c = order[d][idx]
                    st0, sk0 = prevref.pop((d, i, idx))
                    cs = slice(t0 + CH * c, t0 + CH * (c + 1))
                    self.mm(pin[:, sl * 128 + CH * c:sl * 128 + CH * (c + 1)], st0[P, :], qtf[d][P, cs], True, True,
                            [sk0, f"qtf{d}"], [pink])
                self.tt("dve", oT[d][:, t0:t0 + 128], oT[d][:, t0:t0 + 128], pin[:, sl * 128:(sl + 1) * 128], ALU.add,
                        [f"oT{d}", pink], [f"oT{d}"])

        stage_att(0)
        for i in range(NB + 1):
            if i + 1 < NB:
                stage_att(i + 1)
            if i < NB:
                stage_pe(i)
                stage_chain(i)
            if i >= 1:
                stage_inter(i - 1)

    def head_out(self, sc, g, L, col0, seq, oTf, oTb, gate, gatek, gnorm_col, gnk, sq, rstd, outb, mixrow0, extra_scale=None):
        T0 = seq * L
        if oTb is not None:
            self.tt("dve", oTf[:, 0:L], oTf[:, 0:L], oTb[:, 0:L], ALU.add, ["oTf", "oTb"], ["oTf"])
        self.act(sq[:, 0:L], oTf[:, 0:L], AF.Square, ["oTf"], ["hsq"])
        if gate is not None:
            self.act(gate[:, 0:L], gate[:, 0:L], AF.Silu, [gatek], [gatek])
        for t0 in range(0, L, 512):
            w = min(512, L - t0)
            ps, pk = self.ps[0], "ps0"
            self.mm(ps[:, 0:w], self.ones_b[:], sq[:, t0:t0 + w], True, True, ["ones_b", "hsq"], [pk])
            self.rstd_from_ps(rstd[:, 0:w], ps[:, 0:w], pk, 128, [], "hrstd")
            self.stt(oTf[:, t0:t0 + w], oTf[:, t0:t0 + w], gnorm_col, rstd[:, 0:w], ALU.mult, ALU.mult, ["oTf", gnk, "hrstd"], ["oTf"])
        if gate is not None:
            self.tt("dve", outb[:, 0:L], oTf[:, 0:L], gate[:, 0:L], ALU.mult, ["oTf", gatek], ["houtb"])
        else:
            self.act(outb[:, 0:L], oTf[:, 0:L], AF.Copy, ["oTf"], ["houtb"], scale=float(extra_scale))
        self.dma("sp", self.X[f"mix_{g}"][mixrow0:mixrow0 + 128, T0:T0 + L], outb[:, 0:L], ["houtb"], [f"mix_{g}"])

    def load_scan_masks(self, sc, L, ch=CH):
        self.mask_f = sc.sb("mask_f", [128, L])
        self.mask_b = sc.sb("mask_b", [128, L])
        pre = "mask" if ch == CH else f"mask{ch}"
        self.dma("sp", self.mask_f[:], self.C[pre + "_f"][:, 0:L], [], ["mask_f"])
        self.dma("sp", self.mask_b[:], self.C[pre + "_b"][:, 0:L], [], ["mask_b"])

    def hgrn(self, g):
        I, X, O = self.I, self.X, self.O
        gi = GROUPS[g]
        T, Lseq, nseq0 = gi["T"], gi["L"], gi["nseq"]
        L, nseq = T, 1
        SEGB = Lseq // 128
        NB = L // 128
        with self.scope() as sc:
            self.load_scan_masks(sc, L)
            Zst = sc.sb("Zst", [128, 128])
            self.S.op("dve", lambda e: e.memset(Zst[:], 0.0), [], ["Zst"])
            Sin = {d: sc.sb(f"Sin{d}", [128, 128]) for d in "fb"}
            lbt = sc.sb("lbt", [128, 3, 8])
            lb = sc.sb("lb", [128, 8])
            oml = sc.sb("oml", [128, 8])
            gn = sc.sb("gn", [128, 4])
            self.dma("sp", lbt[:], I["lbT"], [], ["lbt"])
            self.dma("sp", gn[:], I["hgrn_normT"], [], ["gn"])
            self.act(lbt[:], lbt[:], AF.Exp, ["lbt"], ["lbt"])
            self.tt("dve", lb[:], lbt[:, 0, :], lbt[:, 1, :], ALU.add, ["lbt"], ["lb"])
            self.tt("dve", lb[:], lb[:], lbt[:, 2, :], ALU.add, ["lbt", "lb"], ["lb"])
            self.S.op("dve", lambda e: e.reciprocal(out=lb[:], in_=lb[:]), ["lb"], ["lb"])
            self.tt("dve", lb[:], lb[:], lbt[:, 0, :], ALU.mult, ["lbt", "lb"], ["lb"])
            self.ts("dve", oml[:], lb[:], -1.0, 1.0, ALU.mult, ALU.add, ["lb"], ["oml"])
            qs = sc.sb("qs", [128, L])
            t1 = {d: sc.sb(f"t1{d}", [128, L]) for d in "fb"}
            t2 = {d: sc.sb(f"t2{d}", [128, L]) for d in "fb"}
            tA = {d: sc.sb(f"tA{d}", [128, L]) for d in "fb"}
            gr = tA["f"]
            qtf = {d: sc.sb(f"qtf{d}", [128, L]) for d in "fb"}
            kh = {d: sc.sb(f"kh{d}", [128, L], BF16) for d in "fb"}
            qt = {d: sc.sb(f"qt{d}", [128, L], BF16) for d in "fb"}
            kt = {d: sc.sb(f"kt{d}", [128, L], BF16) for d in "fb"}
            el = {d: sc.sb(f"el{d}", [128, L // CH]) for d in "fb"}
            kT = {d: sc.sb(f"kT{d}", [128, NB, 128], BF16) for d in "fb"}
            Vt = sc.sb("Vt", [128, NB, 128], BF16)
            self.Vm = sc.sb("Vm", [128, 4, NB, 128], BF16)
            Sst = {d: [sc.sb(f"S{d}{i}", [128, 128]) for i in range(12)] for d in "fb"}
            oT = {d: sc.sb(f"oT{d}", [128, L]) for d in "fb"}
            self.AT = {d: [sc.sb(f"AT{d}{i}", [128, 128], BF16) for i in range(2)] for d in "fb"}
            sq = sc.sb("hsq", [128, L], BF16)
            rstd = sc.sb("hrstd", [128, 512])
            outb = sc.sb("houtb", [128, L], BF16)
            cm = {"f": self.cm_f, "b": self.cm_b}
            pj = X[f"proj_{g}"]
            for s in range(nseq):
                tsl = slice(s * L, (s + 1) * L)
                def issue_loads(h):
                    self.dma("sp", qs[:], pj[h * 128:(h + 1) * 128, tsl], [f"proj_{g}"], ["qs"])
                    for di, d in enumerate("fb"):
                        r0 = (1 + di) * 512 + h * 128
                        self.dma("sp", t1[d][:], pj[r0:r0 + 128, tsl], [f"proj_{g}"], [f"t1{d}"])
                    self.dma("pool", Vt[:], X[f"vtok_{g}"][tsl, h * 128:(h + 1) * 128].rearrange("(b p) v -> p b v", p=128),
                             [f"vtok_{g}"], ["Vt"])

                issue_loads(0)
                for h in range(4):
                    rows = lambda blk: slice(blk * 512 + h * 128, blk * 512 + (h + 1) * 128)
                    self.act(qs[:], qs[:], AF.Silu, ["qs"], ["qs"])
                    for c in range(4):
                        self.act(self.Vm[:, c], Vt[:], AF.Copy, ["Vt", "ind4"], ["Vm"], scale=self.ind4[:, c:c + 1])
                    def prep(di, d, h=h, rows=rows, tsl=tsl):
                        a1, a2 = t1[d], t2[d]
                        k1, k2 = f"t1{d}", f"t2{d}"
                        self.act(a1[:], a1[:], AF.Sigmoid, [k1], [k1])
                        yield
                        c8 = di * 4 + h
                        self.act(a1[:], a1[:], AF.Identity, [k1, "oml", "lb"], [k1], scale=oml[:, c8:c8 + 1], bias=lb[:, c8:c8 + 1])
                        yield
                        self.act(a2[:], a1[:], AF.Ln, [k1], [k2])
                        yield
                        self.act(a1[:], a1[:], AF.Identity, [k1, "ones_col"], [k1], scale=-1.0, bias=self.ones_col[:, 0:1])
                        yield
                        yield from self.scan_prep_dir(sc, d, L, 128, 0, qs, "qs", 128 ** -0.5, a1, k1, a2, k2, tA[d], qtf[d], qt[d],
                                                      kt[d], kh[d], el[d], kT[d])

                    self.interleave([prep(0, "f"), prep(1, "b")])
                    if g == "s":
                        for di, d in enumerate("fb"):
                            self.dma("sp", Sin[d][:], I["st_hgrn"][:, di, h, :], [], [f"Sin{d}"])

                    def init_state(d, seg, h=h):
                        return (Zst, "Zst") if g == "p" else (Sin[d], f"Sin{d}")

                    def fin_fn(d, seg, St, sk, h=h):
                        if g == "p":
                            self.dma("sp", O["nst_hgrn"][seg, 0 if d == "f" else 1, h], St[:], [sk], ["nst_hgrn"])

                    self.bg_step(3)
                    self.dma("sp", gr[:], pj[rows(4), tsl], [f"proj_{g}"], ["scAf"])
                    self.scan_chain(sc, L, 128, 0, qtf, qt, kt, el, kT, Vt, "Vt", Sst, oT, cm, init_state, fin_fn, seg_blocks=SEGB)
                    self.bg_step(2)
                    if h + 1 < 4:
                        issue_loads(h + 1)
                    self.head_out(sc, g, L, 0, s, oT["f"], oT["b"], gr, "scAf", gn[:, h:h + 1], "gn", sq, rstd, outb, h * 128)
            if self.dbg == "hgrn":
                self.dump(f"mixa_{g}", X[f"mix_{g}"][0:512, :], [512, T], BF16, [f"mix_{g}"])

    def wrap_pi(self, ap, key, tmp, tmpk):
        for _ in range(2):
            self.ts("dve", tmp, ap, -math.pi, 2.0 * math.pi, ALU.is_lt, ALU.mult, [key], [tmpk])
            self.tt("dve", ap, ap, tmp, ALU.add, [key, tmpk], [key])
            self.ts("dve", tmp, ap, math.pi, -2.0 * math.pi, ALU.is_gt, ALU.mult, [key], [tmpk])
            self.tt("dve", ap, ap, tmp, ALU.add, [key, tmpk], [key])

    def hyena(self, g):
        I, X, C = self.I, self.X, self.C
        gi = GROUPS[g]
        T, L, nseq = gi["T"], gi["L"], gi["nseq"]
        SC = L // 128
        NF = SC + 1
        NT = max(1, L // 512)
        TW = min(L, 512)
        ksp = X[f"ksp_{g}"]
        self.mark(f"hy{g} A:mlp")
        with self.scope() as sc:
            w1 = sc.sb("hw1", [33, 64]); w2 = sc.sb("hw2", [64, 64]); w3 = sc.sb("hw3", [64, 2048])
            b1 = sc.sb("hb1", [64, 1]); b2 = sc.sb("hb2", [64, 1]); fr = sc.sb("hfr", [64, 1])
            for t_, nm in ((w1, "hy_w1"), (w2, "hy_w2"), (w3, "hy_w3"), (b1, "hy_b1"), (b2, "hy_b2"), (fr, "hy_freq")):
                self.dma("sp", t_[:], I[nm], [], ["hyw"])
            h2 = sc.sb("h2", [64, L])
            with self.scope() as sc_mlp:
                zT = sc_mlp.sb("zT", [33, L])
                self.dma("sp", zT[:], C[f"zT_{g}"], [], ["zT"])
                h1 = sc_mlp.sb("h1", [64, L]); hw = sc_mlp.sb("hwrap", [64, L])
                for (src, srck, wt_, bb, dst, dstk) in ((zT, "zT", w1, b1, h1, "h1"), (h1, "h1", w2, b2, h2, "h2")):
                    for t0 in range(0, L, 512):
                        w = min(512, L - t0)
                        ps, pk = self.next_ps()
                        self.mm(ps[0:64, 0:w], wt_[:], src[:, t0:t0 + w], True, True, ["hyw", srck], [pk])
                        self.ts("dve", dst[:, t0:t0 + w], ps[0:64, 0:w], bb[:, 0:1], fr[:, 0:1], ALU.add, ALU.mult, [pk, "hyw"], [dstk])
                    self.wrap_pi(dst[:], dstk, hw[:], "hwrap")
                    self.act(dst[:], dst[:], AF.Sin, [dstk], [dstk])
            wt = [[sc.sb(f"win{i}{j}", [128, 512]) for j in range(2)] for i in range(2)]
            ge = [sc.sb(f"ge{o}", [128, SC, 512], BF16) for o in range(2)]
            go = [sc.sb(f"go{o}", [128, SC, 512], BF16) for o in range(2)]
            ff = [[sc.sb(f"ff{o}{i}", [128, 512]) for i in range(2)] for o in range(2)]
            Af = [[sc.sb(f"Af{i}{j}", [128, SC, 128], BF16) for j in range(2)] for i in range(2)]
            kst = [sc.sb(f"kst{i}", [128, 512], BF16) for i in range(4)]
            cw = sc.sb("hcw", [128, 12, 3]); cb = sc.sb("hcb", [128, 12])
            self.dma("sp", cw[:], I["hy_cwT"], [], ["hcw"])
            self.dma("sp", cb[:], I["hy_cbT"], [], ["hcb"])
            raw = [sc.sb(f"hraw{i}", [128, T]) for i in range(2)]
            cvo = [sc.sb(f"hcv{i}", [128, T]) for i in range(2)]

            def stage_a():
                for lc in range(SC):
                    for side in range(2):
                        wtile, wk_ = wt[side][lc % 2], f"win{side}{lc % 2}"
                        self.dma("sp", wtile[:], C[f"win{side}_{g}"][:, lc, :], [], [wk_])
                        for o in range(2):
                            ps, pk = self.next_ps()
                            col0 = o * 1024 + side * 512
                            self.mm(ps[:], h2[:, lc * 128:(lc + 1) * 128], w3[:, col0:col0 + 512], True, True, ["h2", "hyw"], [pk])
                            self.tt("dve", ff[o][side][:], ps[:], wtile[:], ALU.mult, [pk, wk_], [f"ff{o}{side}"])
                    for o in range(2):
                        self.tt("pool", ge[o][:, lc, :], ff[o][0][:], ff[o][1][:], ALU.add, [f"ff{o}0", f"ff{o}1"], [f"ge{o}"])
                        self.tt("pool", go[o][:, lc, :], ff[o][0][:], ff[o][1][:], ALU.subtract, [f"ff{o}0", f"ff{o}1"], [f"go{o}"])
                    if lc % 2 == 1:
                        yield
                self.dma("sp", Af[0][0][:], C[f"Ac_{g}"][0], [], ["Ac0"])
                self.dma("sp", Af[0][1][:], C[f"As_{g}"][0], [], ["As0"])
                for fc in range(NF):
                    a_c, a_s = Af[fc % 2]
                    if fc + 1 < NF:
                        n_c, n_s = Af[(fc + 1) % 2]
                        self.dma("sp", n_c[:], C[f"Ac_{g}"][fc + 1], [], [f"Ac{(fc + 1) % 2}"])
                        self.dma("sp", n_s[:], C[f"As_{g}"][fc + 1], [], [f"As{(fc + 1) % 2}"])
                    for o in range(2):
                        for ri, (am, amk, gm, gmk) in enumerate(((a_c, f"Ac{fc % 2}", ge[o], f"ge{o}"), (a_s, f"As{fc % 2}", go[o], f"go{o}"))):
                            ps, pk = self.next_ps()
                            for lc in range(SC):
                                self.mm(ps[:], am[:, lc, :], gm[:, lc, :], lc == 0, lc == SC - 1, [amk, gmk], [pk])
                            ki = o * 2 + ri
                            self.cp("act" if ri else "dve", kst[ki][:], ps[:], [pk], [f"kst{ki}"])
                            self.dma("sp", ksp[o, ri, fc], kst[ki][:], [f"kst{ki}"], [f"ksp_{g}"])
                    yield

            def stage_b():
                self.dma("sp", raw[0][:], X[f"proj_{g}"][2560:2560 + 128, :], [f"proj_{g}"], ["hraw0"])
                for ch in range(12):
                    r_, rk = raw[ch % 2], f"hraw{ch % 2}"
                    o_, ok = cvo[ch % 2], f"hcv{ch % 2}"
                    if ch + 1 < 12:
                        self.dma("sp", raw[(ch + 1) % 2][:], X[f"proj_{g}"][2560 + (ch + 1) * 128:2560 + (ch + 2) * 128, :],
                                 [f"proj_{g}"], [f"hraw{(ch + 1) % 2}"])
                    yield
                    self.dwconv(r_, rk, o_, ok, cw[:, ch, :], cb[:, ch:ch + 1], ["hcw", "hcb"], nseq, L)
                    yield
                    self.dma("sp", X[f"hyc_{g}"][ch * 128:(ch + 1) * 128, :], o_[:], [ok], [f"hyc_{g}"])
                    yield

            self.mark(f"hy{g} A:spectra+conv")
            self.interleave([stage_a(), stage_b()])
        with self.scope() as sc:
            hd = sc.sb("hd", [128, 2, 4])
            self.dma("sp", hd[:], I["hy_dT"], [], ["hd"])
            z = sc.sb("z", [128, 4, L])
            zb = sc.sb("zb", [128, L], BF16)
            uT = sc.sb("uT", [128, SC, 512], BF16)
            Yre = sc.sb("Yre", [128, NF, 512], BF16)
            Yim = sc.sb("Yim", [128, NF, 512], BF16)
            Af = [[sc.sb(f"Af{i}{j}", [128, SC, 128], BF16) for j in range(2)] for i in range(2)]
            Kt = [[sc.sb(f"Kt{i}{j}", [128, 512], BF16) for j in range(2)] for i in range(2)]
            tm = [sc.sb(f"tm{i}", [128, 512]) for i in range(4)]
            Bc = sc.sb("Bc", [128, NF, TW], BF16)
            Bs = sc.sb("Bs", [128, NF, TW], BF16)
            gt = [sc.sb(f"gt{i}", [128, TW]) for i in range(2)]
            zo = sc.sb("zo", [128, L], BF16)
            for s_ in range(nseq):
                tsl = slice(s_ * L, (s_ + 1) * L)
                self.dma("sp", z[:], X[f"hyc_{g}"][0:512, tsl].rearrange("(c p) t -> p c t", p=128), [f"hyc_{g}"], ["z"])
                for o in range(2):
                    self.mark(f"hy{g} C{o}:transp")
                    for cc in range(4):
                        self.cp("act", zb[:], z[:, cc, :], ["z"], ["zb"])
                        for lc in range(SC):
                            self.tr(self.psT[:, (lc % 8) * 128:(lc % 8 + 1) * 128], zb[:, lc * 128:(lc + 1) * 128], self.ident_b[:],
                                    ["zb", "ident_b"], ["ps7"])
                            if lc % 8 == 7 or lc == SC - 1:
                                n = lc % 8 + 1
                                l0 = lc - n + 1
                                self.cp("act" if cc % 2 else "dve", uT[:, l0:l0 + n, cc * 128:(cc + 1) * 128],
                                        self.psT[:, 0:n * 128].rearrange("p (l c) -> p l c", c=128), ["ps7"], ["uT"])
                    self.mark(f"hy{g} C{o}:fwd")
                    for fc in range(NF):
                        a_c, a_s = Af[fc % 2]
                        k_r, k_i = Kt[fc % 2]
                        self.dma("sp", a_c[:], C[f"Ac_{g}"][fc], [], [f"Ac{fc % 2}"])
                        self.dma("sp", a_s[:], C[f"As_{g}"][fc], [], [f"As{fc % 2}"])
                        self.dma("sp", k_r[:], ksp[o, 0, fc], [f"ksp_{g}"], [f"Kr{fc % 2}"])
                        self.dma("sp", k_i[:], ksp[o, 1, fc], [f"ksp_{g}"], [f"Ki{fc % 2}"])
                        pr, prk = self.next_ps()
                        for lc in range(SC):
                            self.mm(pr[:], a_c[:, lc, :], uT[:, lc, :], lc == 0, lc == SC - 1, [f"Ac{fc % 2}", "uT"], [prk])
                        pi_, pik = self.next_ps()
                        for lc in range(SC):
                            self.mm(pi_[:], a_s[:, lc, :], uT[:, lc, :], lc == 0, lc == SC - 1, [f"As{fc % 2}", "uT"], [pik])
                        self.tt("dve", tm[0][:], pr[:], k_r[:], ALU.mult, [prk, f"Kr{fc % 2}"], ["tm0"])
                        self.tt("dve", tm[1][:], pi_[:], k_i[:], ALU.mult, [pik, f"Ki{fc % 2}"], ["tm1"])
                        self.tt("dve", tm[2][:], pr[:], k_i[:], ALU.mult, [prk, f"Ki{fc % 2}"], ["tm2"])
                        self.tt("dve", tm[3][:], pi_[:], k_r[:], ALU.mult, [pik, f"Kr{fc % 2}"], ["tm3"])
                        self.tt("pool", Yre[:, fc, :], tm[0][:], tm[1][:], ALU.subtract, ["tm0", "tm1"], ["Yre"])
                        self.tt("pool", Yim[:, fc, :], tm[2][:], tm[3][:], ALU.add, ["tm2", "tm3"], ["Yim"])
                    self.mark(f"hy{g} C{o}:inv")
                    for tt in range(NT):
                        self.dma("sp", Bc[:], C[f"Bc_{g}"][tt], [], ["Bc"])
                        self.dma("sp", Bs[:], C[f"Bs_{g}"][tt], [], ["Bs"])
                        for cc in range(4):
                            ps, pk = self.next_ps()
                            for fc in range(NF):
                                self.mm(ps[:, 0:TW], Yre[:, fc, cc * 128:(cc + 1) * 128], Bc[:, fc, :], fc == 0, False, ["Yre", "Bc"], [pk])
                            for fc in range(NF):
                                self.mm(ps[:, 0:TW], Yim[:, fc, cc * 128:(cc + 1) * 128], Bs[:, fc, :], False, fc == NF - 1, ["Yim", "Bs"], [pk])
                            gtile, gk = gt[cc % 2], f"gt{cc % 2}"
                            grow = 512 * (o + 1) + cc * 128
                            self.dma("sp", gtile[:], X[f"hyc_{g}"][grow:grow + 128, s_ * L + tt * TW:s_ * L + (tt + 1) * TW], [f"hyc_{g}"], [gk])
                            zsl = z[:, cc, tt * TW:(tt + 1) * TW]
                            self.stt(zsl, zsl, hd[:, o, cc:cc + 1], ps[:, 0:TW], ALU.mult, ALU.add, ["z", "hd", pk], ["z"])
                            self.tt("dve", zsl, zsl, gtile[:], ALU.mult, ["z", gk], ["z"])
                for cc in range(4):
                    self.cp("act", zo[:], z[:, cc, :], ["z"], ["zo"])
                    self.dma("sp", X[f"mix_{g}"][512 + cc * 128:512 + (cc + 1) * 128, tsl], zo[:], ["zo"], [f"mix_{g}"])
            if self.dbg == "hyena":
                self.dump(f"mixz_{g}", X[f"mix_{g}"][512:1024, :], [512, T], BF16, [f"mix_{g}"])
                self.dump(f"hyc_{g}", X[f"hyc_{g}"], [1536, T], F32, [f"hyc_{g}"])
                self.dump(f"ksp_{g}", ksp, [2, 2, NF, 128, 512], BF16, [f"ksp_{g}"])

    def diffattn(self, g):
        I, X, O, C = self.I, self.X, self.O, self.C
        gi = GROUPS[g]
        T, L, nseq = gi["T"], gi["L"], gi["nseq"]
        NB = L // 128
        NCK = 2 if g == "s" else 0
        NK = NB + NCK
        QW = min(512, L)
        lam_init = 0.8 - 0.6 * math.exp(-0.3 * 1)
        with self.scope() as sc:
            dl = sc.sb("dl", [128, 4, 64])
            pr = sc.sb("dlp", [128, 2, 64])
            lam = sc.sb("lam", [128, 2])
            lamneg = sc.sb("lamneg", [128, 1])
            dn = sc.sb("dn", [128, 4])
            self.dma("sp", dl[:], I["dlam"], [], ["dl"])
            self.dma("sp", dn[:], I["diff_normT"], [], ["dn"])
            self.tt("dve", pr[:, 0, :], dl[:, 0, :], dl[:, 1, :], ALU.mult, ["dl"], ["dlp"])
            self.tt("dve", pr[:, 1, :], dl[:, 2, :], dl[:, 3, :], ALU.mult, ["dl"], ["dlp"])
            self.S.op("dve", lambda e: e.reduce_sum(out=lam[:], in_=pr[:], axis=mybir.AxisListType.X), ["dlp"], ["lam"])
            self.act(lam[:], lam[:], AF.Exp, ["lam"], ["lam"])
            self.tt("dve", lamneg[:], lam[:, 1:2], lam[:, 0:1], ALU.subtract, ["lam"], ["lamneg"])
            self.ts("dve", lamneg[:], lamneg[:], -lam_init, None, ALU.add, None, ["lamneg"], ["lamneg"])
            self.ts("dve", dn[:], dn[:], 1.0 - lam_init, None, ALU.mult, None, ["dn"], ["dn"])
            q = sc.sb("aq", [128, L]); k = sc.sb("ak", [128, L])
            qb2 = [sc.sb(f"aqb{i}", [128, L], BF16) for i in range(2)]
            kall2 = [sc.sb(f"akall{i}", [128, NK * 128], BF16) for i in range(2)]
            Vall2 = [sc.sb(f"aV{i}", [128, NK, 128], BF16) for i in range(2)]
            E = [sc.sb(f"aE{i}", [128, NK, QW], BF16) for i in range(2)]
            on = [sc.sb(f"aon{i}", [128, QW]) for i in range(2)]
            rl = sc.sb("arl", [128, QW])
            Er = sc.sb("aEr", [128, QW])
            ones_f = sc.sb("ones_f", [128, 128])
            self.S.op("pool", lambda e: e.memset(ones_f[:], 1.0), [], ["ones_f"])
            oc = sc.sb("aoc", [128, L])
            sq = sc.sb("hsq", [128, L], BF16)
            rstd = sc.sb("hrstd", [128, 512])
            outb = sc.sb("houtb", [128, L], BF16)
            if g == "s":
                rC = sc.sb("ropeC", [128, L]); rS = sc.sb("ropeS", [128, L]); rP = sc.sb("ropeP", [128, 128])
                self.dma("sp", rC[:], C["ropeC"], [], ["ropeC"])
                self.dma("sp", rS[:], C["ropeS"], [], ["ropeS"])
                self.dma("sp", rP[:], C["ropeP"], [], ["ropeP"])
                rt = sc.sb("ropet", [128, 512])
                kc32 = sc.sb("kc32", [128, 2, 128])
            pj = X[f"proj_{g}"]
            units = [(s_, h) for s_ in range(nseq) for h in range(4)]
            steps = [(qt, p) for qt in range(L // QW) for p in range(2)]

            def setup(u):
                s_, h = units[u]
                pb_ = u % 2
                tsl = slice(s_ * L, (s_ + 1) * L)
                kal, kalk = kall2[pb_], f"akall{pb_}"
                Va, Vak = Vall2[pb_], f"aV{pb_}"
                self.dma("sp", q[:], pj[h * 128:(h + 1) * 128, tsl], [f"proj_{g}"], ["aq"])
                self.dma("sp", k[:], pj[512 + h * 128:512 + (h + 1) * 128, tsl], [f"proj_{g}"], ["ak"])
                self.dma("pool", Va[:, NCK:NK, :],
                         X[f"vtok_{g}"][tsl, 512 + h * 128:512 + (h + 1) * 128].rearrange("(b p) v -> p b v", p=128),
                         [f"vtok_{g}"], [Vak])
                yield
                if g == "s":
                    self.dma("sp", kc32[:], I["ck"][h].rearrange("(c p) d -> p c d", p=128), [], ["kc32"])
                    self.dma("pool", Va[:, 0:2, :], I["cv"][h].rearrange("(c p) d -> p c d", p=128), [], [Vak])
                    yield
                    for (x, xk) in ((q, "aq"), (k, "ak")):
                        for t0 in range(0, L, 512):
                            ps, pk = self.next_ps()
                            self.mm(ps[:], rP[:], x[:, t0:t0 + 512], True, True, ["ropeP", xk], [pk])
                            self.tt("dve", rt[:], ps[:], rS[:, t0:t0 + 512], ALU.mult, [pk, "ropeS"], ["ropet"])
                            yield
                            self.tt("pool", x[:, t0:t0 + 512], x[:, t0:t0 + 512], rC[:, t0:t0 + 512], ALU.mult, [xk, "ropeC"], [xk])
                            self.tt("pool", x[:, t0:t0 + 512], x[:, t0:t0 + 512], rt[:], ALU.add, [xk, "ropet"], [xk])
                            yield
                    for c in range(2):
                        ps, pk = self.next_ps()
                        self.tr(ps[:, 0:128], kc32[:, c, :], self.ident_f[:], ["kc32", "ident_f"], [pk])
                        self.cp("act", kal[:, c * 128:(c + 1) * 128], ps[:, 0:128], [pk], [kalk])
                        yield
                self.cp("act", qb2[pb_][:], q[:], ["aq"], [f"aqb{pb_}"])
                yield
                self.cp("dve", kal[:, NCK * 128:NK * 128], k[:], ["ak"], [kalk])
                yield

            def run(u):
                s_, h = units[u]
                pb_ = u % 2
                tsl = slice(s_ * L, (s_ + 1) * L)
                kal, kalk = kall2[pb_], f"akall{pb_}"
                Va, Vak = Vall2[pb_], f"aV{pb_}"
                qb_, qbk = qb2[pb_], f"aqb{pb_}"

                NKP = NK if NK <= 4 else (2 * NK) // 3
                def s_mm(n, kc):
                    qt, p = steps[n]
                    qsl = slice(qt * QW, (qt + 1) * QW)
                    PP = slice(64 * p, 64 * p + 64)
                    Et, Ek = E[n % 2], f"aE{n % 2}"
                    ps, pk = self.ps[kc % 4], f"ps{kc % 4}"
                    self.mm(ps[:, 0:QW], kal[PP, kc * 128:(kc + 1) * 128], qb_[PP, qsl], True, True, [kalk, qbk], [pk])
                    self.act(Et[:, kc, :], ps[:, 0:QW], AF.Exp, [pk], [Ek], scale=0.125)

                def step(n):
                    qt, p = steps[n]
                    qsl = slice(qt * QW, (qt + 1) * QW)
                    Et, Ek = E[n % 2], f"aE{n % 2}"
                    has_next = n + 1 < len(steps)
                    PSO, PSOK = self.ps[4 + n % 2], f"ps{4 + n % 2}"
                    PSL, PSLK = self.ps[6 + n % 2], f"ps{6 + n % 2}"
                    if NKP < NK:
                        self.S.op("dve", lambda e: e.tensor_reduce(out=Er[:], in_=Et[:, NKP:NK, :].rearrange("p k q -> p q k"),
                                                                   axis=mybir.AxisListType.X, op=ALU.add), [Ek], ["aEr"])
                    for kc in range(NK):
                        if has_next:
                            s_mm(n + 1, kc)
                        self.mm(PSO[:, 0:QW], Va[:, kc, :], Et[:, kc, :], kc == 0, kc == NK - 1, [Vak, Ek], [PSOK])
                        if kc < NKP:
                            self.mm(PSL[:, 0:QW], self.ones_b[:], Et[:, kc, :], kc == 0, (kc == NKP - 1) and NKP == NK,
                                    ["ones_b", Ek], [PSLK])
                    if NKP < NK:
                        self.mm(PSL[:, 0:QW], ones_f[:], Er[:], False, True, ["ones_f", "aEr"], [PSLK])
                    self.act(rl[:], PSL[:, 0:QW], AF.Ln, [PSLK], ["arl"])
                    self.act(rl[:], rl[:], AF.Exp, ["arl"], ["arl"], scale=-1.0)
                    self.tt("dve", on[p][:], PSO[:, 0:QW], rl[:], ALU.mult, [PSOK, "arl"], [f"aon{p}"])
                    if p == 1:
                        self.stt(oc[:, qsl], on[1][:], lamneg[:, 0:1], on[0][:], ALU.mult, ALU.add, ["aon0", "aon1", "lamneg"], ["oTf"])

                for kc in range(NK):
                    s_mm(0, kc)
                yield
                for n in range(len(steps)):
                    step(n)
                    yield
                self.head_out(sc, g, L, 0, s_, oc, None, None, None, dn[:, h:h + 1], "dn", sq, rstd, outb, h * 128, extra_scale=1.0)
                if g == "p":
                    self.dma("sp", O["nck"][s_, h], X[f"vtok_{g}"][tsl, h * 128:(h + 1) * 128], [f"vtok_{g}"], ["nck"])
                    self.dma("sp", O["ncv"][s_, h], X[f"vtok_{g}"][tsl, 512 + h * 128:512 + (h + 1) * 128], [f"vtok_{g}"], ["ncv"])
                yield

            self.interleave([setup(0)])
            for u in range(len(units)):
                gens = [run(u)]
                if u + 1 < len(units):
                    gens.append(setup(u + 1))
                self.interleave(gens)
            if self.dbg == "diff":
                self.dump(f"mixc_{g}", X[f"mix_{g}"][0:512, :], [512, T], BF16, [f"mix_{g}"])

    def gla(self, g):
        I, X, O = self.I, self.X, self.O
        gi = GROUPS[g]
        T, Lseq, nseq0 = gi["T"], gi["L"], gi["nseq"]
        L, nseq = T, 1
        SEGB = Lseq // 128
        NB = L // 128
        GCH = 128
        with self.scope() as sc:
            self.load_scan_masks(sc, L, GCH)
            Zst = sc.sb("Zst", [128, 128])
            self.S.op("dve", lambda e: e.memset(Zst[:], 0.0), [], ["Zst"])
            Sin = {d: sc.sb(f"Sin{d}", [128, 128]) for d in "fb"}
            aw = sc.sb("gaw", [16, 2, 256])
            nab = sc.sb("gnab", [128, 2, 2])
            gn = sc.sb("ggn", [128, 4])
            self.dma("sp", aw[:], I["gla_aw"].rearrange("d r c -> r d c"), [], ["gaw"])
            self.dma("sp", nab[:], I["gla_abT"], [], ["gnab"])
            self.dma("sp", gn[:], I["gla_normT"], [], ["ggn"])
            self.ts("dve", nab[:], nab[:], -1.0, None, ALU.mult, None, ["gnab"], ["gnab"])
            da = {d: sc.sb(f"gda{d}", [16, L]) for d in "fb"}
            qs = sc.sb("qs", [128, L]); kk = sc.sb("kk", [128, L])
            lft = {d: sc.sb(f"lft{d}", [128, L]) for d in "fb"}
            tA = {d: sc.sb(f"tA{d}", [128, L]) for d in "fb"}
            gr = tA["f"]
            qtf = {d: sc.sb(f"qtf{d}", [128, L]) for d in "fb"}
            kh = {d: sc.sb(f"kh{d}", [128, L], BF16) for d in "fb"}
            qt = {d: sc.sb(f"qt{d}", [128, L], BF16) for d in "fb"}
            kt = {d: sc.sb(f"kt{d}", [128, L], BF16) for d in "fb"}
            el = {d: sc.sb(f"el{d}", [128, L // GCH]) for d in "fb"}
            kT = {d: sc.sb(f"kT{d}", [128, NB, 128], BF16) for d in "fb"}
            Vt = sc.sb("Vt", [128, NB, 128], BF16)
            Sst = {d: [sc.sb(f"S{d}{i}", [128, 128]) for i in range(12)] for d in "fb"}
            oT = {d: sc.sb(f"oT{d}", [128, L]) for d in "fb"}
            self.AT = {d: [sc.sb(f"AT{d}{i}", [128, 128], BF16) for i in range(2)] for d in "fb"}
            sq = sc.sb("hsq", [128, L], BF16)
            rstd = sc.sb("hrstd", [128, 512])
            outb = sc.sb("houtb", [128, L], BF16)
            cm = {"f": self.cm128_f, "b": self.cm128_b}
            pj = X[f"proj_{g}"]
            for s in range(nseq):
                tsl = slice(s * L, (s + 1) * L)
                def issue_loads(h):
                    P_ = slice(64 * (h % 2), 64 * (h % 2) + 64)
                    self.dma("sp", qs[P_, :], pj[1536 + 64 * h:1536 + 64 * (h + 1), tsl], [f"proj_{g}"], ["qs"])
                    self.dma("sp", kk[P_, :], pj[1792 + 64 * h:1792 + 64 * (h + 1), tsl], [f"proj_{g}"], ["kk"])
                    if h == 0:
                        for di, d in enumerate("fb"):
                            self.dma("sp", da[d][:], pj[3072 + 16 * di:3088 + 16 * di, tsl], [f"proj_{g}"], [f"gda{d}"])
                    self.dma("pool", Vt[:], X[f"vtok_{g}"][tsl, 1024 + h * 128:1024 + (h + 1) * 128].rearrange("(b p) v -> p b v", p=128),
                             [f"vtok_{g}"], ["Vt"])

                issue_loads(0)
                for h in range(4):
                    pb = 64 * (h % 2)
                    chk = h // 2
                    P = slice(pb, pb + 64)
                    def prep(di, d, h=h, pb=pb, chk=chk, P=P, tsl=tsl):
                        dd, lf_ = da[d], lft[d]
                        dk_, lk_ = f"gda{d}", f"lft{d}"
                        for t0 in range(0, L, 512):
                            w = min(512, L - t0)
                            ps, pk = self.next_ps()
                            self.mm(ps[P, 0:w], aw[:, di, 64 * h:64 * (h + 1)], dd[:, t0:t0 + w], True, True, ["gaw", dk_], [pk])
                            self.act(lf_[P, t0:t0 + w], ps[P, 0:w], AF.Exp, [pk, "gnab"], [lk_], scale=-1.0, bias=nab[P, di, chk:chk + 1])
                            yield
                        self.act(lf_[P, :], lf_[P, :], AF.Ln, [lk_, "ones_col"], [lk_], bias=self.ones_col[P, 0:1])
                        yield
                        self.act(lf_[P, :], lf_[P, :], AF.Copy, [lk_], [lk_], scale=-1.0 / 16.0)
                        yield
                        yield from self.scan_prep_dir(sc, d, L, 64, pb, qs, "qs", 64 ** -0.5, kk, "kk", lf_, lk_, tA[d], qtf[d], qt[d],
                                                      kt[d], kh[d], el[d], kT[d], CH=GCH)

                    self.interleave([prep(0, "f"), prep(1, "b")])
                    if g == "s":
                        for di, d in enumerate("fb"):
                            self.dma("sp", Sin[d][P, :], I["st_gla"][:, di, h, :], [], [f"Sin{d}"])

                    def init_state(d, seg, h=h):
                        return (Zst, "Zst") if g == "p" else (Sin[d], f"Sin{d}")

                    def fin_fn(d, seg, St, sk, h=h, P=P):
                        if g == "p":
                            self.dma("sp", O["nst_gla"][seg, 0 if d == "f" else 1, h], St[P, :], [sk], ["nst_gla"])

                    self.dma("sp", gr[:], pj[2560 + 128 * h:2560 + 128 * (h + 1), tsl], [f"proj_{g}"], ["scAf"])
                    self.scan_chain(sc, L, 64, pb, qtf, qt, kt, el, kT, Vt, "Vt", Sst, oT, cm, init_state, fin_fn, CH=GCH, seg_blocks=SEGB)
                    if h + 1 < 4:
                        issue_loads(h + 1)
                    self.head_out(sc, g, L, 0, s, oT["f"], oT["b"], gr, "scAf", gn[:, h:h + 1], "ggn", sq, rstd, outb, 512 + h * 128)
            if self.dbg == "gla":
                self.dump(f"mixd_{g}", X[f"mix_{g}"][512:1024, :], [512, T], BF16, [f"mix_{g}"])


def _fm(vec, nchunk):
    v = np.asarray(vec, dtype=np.float32)
    lead = v.shape[:-1]
    v = v.reshape(lead + (nchunk, 128))
    return np.ascontiguousarray(np.moveaxis(v, -1, 0))


def _rope_tables():
    half = 32
    inv = (10000.0 ** (-np.arange(0, half, 2, dtype=np.float32) / half)).astype(np.float32)
    t = np.arange(2048)
    row = (t // 64).astype(np.float32)
    col = (t % 64).astype(np.float32)
    Cc = np.zeros((128, 2048), np.float32)
    Ss = np.zeros((128, 2048), np.float32)
    P = np.zeros((128, 128), np.float32)
    for r in range(128):
        d = r % 64
        pos = row if d < 32 else col
        i = d % 32
        fi = i % 16
        ang = (pos * inv[fi]).astype(np.float32)
        Cc[r] = np.cos(ang)
        if i < 16:
            Ss[r] = -np.sin(ang)
            partner = r + 16
        else:
            Ss[r] = np.sin(ang)
            partner = r - 16
        P[partner, r] = 1.0
    return Cc, Ss, P


def prep_core_inputs(inp, core):
    b = core // 4
    f32 = lambda a: np.ascontiguousarray(np.asarray(a, dtype=np.float32))
    m = {}
    xp = f32(inp["x_prompt"][4 * core:4 * core + 4]).reshape(1024, D)
    m["xT_p"] = np.ascontiguousarray(xp.T)
    m["xT_s"] = np.ascontiguousarray(f32(inp["x_sample"][b]).T)
    m["cT"] = np.ascontiguousarray(np.stack([_fm(inp["c_ctx"], 8), _fm(inp["c"][b], 8)], axis=-1))
    m["ada_w"] = f32(inp["ada_w"])
    m["ada_bT"] = _fm(inp["ada_b"], 48)
    m["norm_gT"] = _fm(inp["norm_g"], 8)
    m["ffn_up"] = f32(inp["ffn_up"])
    m["ffn_cwT"] = np.ascontiguousarray(_fm(inp["ffn_conv_w"], 44).transpose(0, 1, 3, 2))
    m["ffn_cbT"] = _fm(inp["ffn_conv_b"], 44)
    m["ffn_down"] = f32(inp["ffn_down"])
    m["w_in_e"] = f32(inp["w_in_even"][0])
    m["w_out_e"] = f32(inp["w_out_even"][0])
    m["lbT"] = np.ascontiguousarray(_fm(inp["hgrn_lb"], 4).reshape(128, 3, 8))
    m["hgrn_normT"] = _fm(inp["hgrn_norm"][0], 4)
    m["hy_cwT"] = np.ascontiguousarray(_fm(inp["hy_conv_w"][0], 12).transpose(0, 2, 1))
    m["hy_cbT"] = _fm(inp["hy_conv_b"][0], 12)
    m["hy_w1"] = f32(inp["hy_w1"][0])
    m["hy_b1"] = f32(inp["hy_b1"][0]).reshape(64, 1)
    m["hy_w2"] = f32(inp["hy_w2"][0])
    m["hy_b2"] = f32(inp["hy_b2"][0]).reshape(64, 1)
    m["hy_w3"] = f32(inp["hy_w3"][0])
    m["hy_freq"] = f32(inp["hy_freq"][0]).reshape(64, 1)
    m["hy_dT"] = _fm(inp["hy_d"][0], 4)
    m["st_hgrn"] = np.ascontiguousarray(f32(inp["state_hgrn"][b, 0]).transpose(2, 0, 1, 3))
    m["w_in_o"] = f32(inp["w_in_odd"][0])
    m["w_out_o"] = f32(inp["w_out_odd"][0])
    m["dlam"] = np.ascontiguousarray(np.broadcast_to(f32(inp["diff_lambda"][0])[None], (128, 4, 64)))
    m["diff_normT"] = _fm(inp["diff_norm"][0], 4)
    m["gla_aw"] = f32(inp["gla_aw"][0])
    m["gla_abT"] = _fm(inp["gla_ab"][0], 2)
    m["gla_normT"] = _fm(inp["gla_norm"][0], 4)
    m["ck"] = f32(inp["cache_diff_k"][b, 0])
    m["cv"] = f32(inp["cache_diff_v"][b, 0])
    m["st_gla"] = np.ascontiguousarray(f32(inp["state_gla"][b, 0]).transpose(2, 0, 1, 3))
    for k, v in make_consts().items():
        m["c_" + k] = v
    Cc, Ss, P = _rope_tables()
    m["c_ropeC"], m["c_ropeS"], m["c_ropeP"] = Cc, Ss, P
    return m


_PROG = {}


def get_prog(dbg=None, nlayers=2):
    key = (dbg, nlayers)
    if key not in _PROG:
        kb = KB(dbg=dbg, nlayers=nlayers)
        kb.build()
        _PROG[key] = kb
    return _PROG[key]


def run_cores(inputs, cores, dbg=None, nlayers=2):
    kb = get_prog(dbg, nlayers)
    in_maps = []
    for c in cores:
        m = prep_core_inputs(inputs, c)
        in_maps.append({k: m[k] for k in kb.in_shapes})
    res = run_bass_kernel_spmd(kb.nc, in_maps, core_ids=list(range(len(cores))))
    return res.results


def kernel(**inputs):
    res = run_cores(inputs, list(range(8)))
    yp = np.zeros((32, 256, D), np.float32)
    ys = np.zeros((2, 2048, D), np.float32)
    nsh = np.zeros((32, 1, 2, 4, 128, 128), np.float32)
    nck = np.zeros((32, 1, 4, 256, 128), np.float32)
    ncv = np.zeros((32, 1, 4, 256, 128), np.float32)
    nsg = np.zeros((32, 1, 2, 4, 64, 128), np.float32)
    for c in range(8):
        r = res[c]
        yp[4 * c:4 * c + 4] = np.asarray(r["yT_p"]).T.reshape(4, 256, D)
        if c % 4 == 0:
            ys[c // 4] = np.asarray(r["yT_s"]).T
        nsh[4 * c:4 * c + 4, 0] = np.asarray(r["nst_hgrn"])
        nck[4 * c:4 * c + 4, 0] = np.asarray(r["nck"])
        ncv[4 * c:4 * c + 4, 0] = np.asarray(r["ncv"])
        nsg[4 * c:4 * c + 4, 0] = np.asarray(r["nst_gla"])
    return (yp, ys, nsh, nck, ncv, nsg)
```

```python
import contextlib
import math
import numpy as np
import ml_dtypes
import concourse.bass as bass
import concourse.mybir as mybir
from concourse.bass_utils import run_bass_kernel_spmd

F32 = mybir.dt.float32
BF16 = mybir.dt.bfloat16
AF = mybir.ActivationFunctionType
ALU = mybir.AluOpType
NPBF = ml_dtypes.bfloat16

N_DMA_SEMS = 72
D = 1024
DFF = 2816
EPS = 1e-6
CH = 32
GROUPS = {"p": dict(nseq=4, L=256, T=1024), "s": dict(nseq=1, L=2048, T=2048)}


class Sched:
    ENG = ("pe", "act", "dve", "pool", "sp")

    def __init__(self, nc, same_engine_sync=True):
        self.nc = nc
        self.sem = {e: nc.alloc_semaphore(name=f"cnt_{e}") for e in self.ENG}
        self.dsem = [nc.alloc_semaphore(name=f"dma_{i}") for i in range(N_DMA_SEMS)]
        self.dval = [0] * N_DMA_SEMS
        self.dnext = 0
        self.cnt = {e: 0 for e in self.ENG}
        self.seen = {e: {} for e in self.ENG}
        self.snap = {e: [None] for e in self.ENG}
        self.last_w = {}
        self.readers = {}
        self.same_engine_sync = same_engine_sync
        self.n_wait = 0
        self.n_ins = 0
        self.engs = {"pe": nc.tensor, "act": nc.scalar, "dve": nc.vector, "pool": nc.gpsimd, "sp": nc.sync}

    def _need(self, e, ev, waits, force=False):
        if ev is None:
            return
        kind, sname, semh, val = ev
        if kind == "eng" and sname == e and not force:
            if e == "pe" or not self.same_engine_sync:
                return
        if self.seen[e].get(sname, 0) >= val:
            return
        cur = waits.get(sname)
        if cur is None or cur[1] < val:
            waits[sname] = (semh, val, kind)

    def _emit_waits(self, e, waits):
        for sname, (semh, val, kind) in waits.items():
            self.engs[e].wait_ge(semh, val)
            self.n_wait += 1
            self.seen[e][sname] = val
            if kind == "eng":
                sn = self.snap[sname][val]
                if sn:
                    se = self.seen[e]
                    for k, v in sn.items():
                        if se.get(k, 0) < v:
                            se[k] = v

    def _collect(self, e, reads, writes, force=False):
        waits = {}
        for k in reads:
            self._need(e, self.last_w.get(k), waits, force)
        for k in writes:
            self._need(e, self.last_w.get(k), waits, force)
            rd = self.readers.get(k)
            if rd:
                for ev in rd.values():
                    self._need(e, ev, waits, force)
        return waits

    def _record(self, ev, reads, writes):
        sname = ev[1]
        for k in reads:
            self.readers.setdefault(k, {})[sname] = ev
        for k in writes:
            self.last_w[k] = ev
            self.readers[k] = {}

    def op(self, e, fn, reads=(), writes=()):
        self._emit_waits(e, self._collect(e, reads, writes))
        self.cnt[e] += 1
        idx = self.cnt[e]
        fn(self.engs[e]).then_inc(self.sem[e], 1)
        self.snap[e].append(dict(self.seen[e]))
        self.n_ins += 1
        ev = ("eng", e, self.sem[e], idx)
        self._record(ev, reads, writes)
        return ev

    def dma(self, q, out, in_, reads=(), writes=(), **kw):
        s = self.dnext
        self.dnext = (self.dnext + 1) % N_DMA_SEMS
        sname = f"d{s}"
        semh = self.dsem[s]
        waits = self._collect(q, reads, writes, True)
        if self.dval[s] > 0:
            self._need(q, ("dma", sname, semh, self.dval[s]), waits)
        self._emit_waits(q, waits)
        self.dval[s] += 16
        self.engs[q].dma_start(out=out, in_=in_, **kw).then_inc(semh, 16)
        self.n_ins += 1
        ev = ("dma", sname, semh, self.dval[s])
        self._record(ev, reads, writes)
        return ev

    def barrier(self, engines=None):
        for e in (engines or self.ENG):
            waits = {}
            for s_ in range(N_DMA_SEMS):
                if self.dval[s_] > 0:
                    self._need(e, ("dma", f"d{s_}", self.dsem[s_], self.dval[s_]), waits, True)
            for f in self.ENG:
                if f != e and self.cnt[f] > 0:
                    self._need(e, ("eng", f, self.sem[f], self.cnt[f]), waits, True)
            self._emit_waits(e, waits)


_CONST_CACHE = {}


def _dft_tables(L):
    n = 2 * L
    SC = L // 128
    NF = SC + 1
    NT = max(1, L // 512)
    TW = min(L, 512)
    s = np.arange(L, dtype=np.float64)
    f = np.arange(NF * 128, dtype=np.float64)
    valid = (f <= L)
    ang = 2.0 * np.pi * np.outer(s, f) / n
    Ac = np.cos(ang) * valid[None]
    As = -np.sin(ang) * valid[None]
    wf = np.where((f == 0) | (f == L), 1.0, 2.0) * valid / n
    Bc = (np.cos(ang) * wf[None]).T
    Bs = (-np.sin(ang) * wf[None]).T
    def fwd(A):
        return np.ascontiguousarray(A.reshape(SC, 128, NF, 128).transpose(2, 1, 0, 3)).astype(NPBF)
    def inv(B):
        return np.ascontiguousarray(B.reshape(NF, 128, NT, TW).transpose(2, 1, 0, 3)).astype(NPBF)
    return fwd(Ac), fwd(As), inv(Bc), inv(Bs)


def _hyena_pos(L):
    t = np.linspace(0.0, 1.0, L, dtype=np.float32)[:, None]
    w = (2.0 * np.float32(math.pi) * np.arange(L, dtype=np.float32)[:, None] / np.float32(L)).astype(np.float32)
    fb = np.linspace(1e-4, 15, 16, dtype=np.float32)[None]
    z = np.concatenate([t, np.cos(fb * w), -np.sin(fb * w)], axis=-1).astype(np.float32)
    max_decay = math.log(1e-2) / 0.3
    min_decay = math.log(1e-2) / 1.5
    deltas = np.abs(np.linspace(min_decay, max_decay, 512, dtype=np.float32))
    window = np.exp(-t * deltas[None]).astype(np.float32)
    w0 = window.copy()
    w1 = window.copy()
    w1[0] = 0.0
    SC = L // 128
    lay = lambda a: np.ascontiguousarray(a.reshape(SC, 128, 512).transpose(1, 0, 2))
    return np.ascontiguousarray(z.T), lay(w0), lay(w1)


def make_consts():
    if _CONST_CACHE:
        return _CONST_CACHE
    c = {}
    c["ident_f"] = np.eye(128, dtype=np.float32)
    c["ident_b"] = np.eye(128).astype(NPBF)
    c["ones_b"] = np.ones((128, 128)).astype(NPBF)
    t = np.arange(2048)
    c["mask_f"] = np.broadcast_to((t % CH != 0).astype(np.float32), (128, 2048)).copy()
    c["mask_b"] = np.broadcast_to((t % CH != CH - 1).astype(np.float32), (128, 2048)).copy()
    s_ = np.arange(128)[:, None]
    t_ = np.arange(128)[None, :]
    same = (s_ // CH) == (t_ // CH)
    c["cm_f"] = (same & (s_ <= t_)).astype(np.float32)
    c["cm_b"] = (same & (s_ >= t_)).astype(np.float32)
    c["ind4"] = ((np.arange(128)[:, None] // CH) == np.arange(4)[None, :]).astype(np.float32)
    c["mask128_f"] = np.broadcast_to((t % 128 != 0).astype(np.float32), (128, 2048)).copy()
    c["mask128_b"] = np.broadcast_to((t % 128 != 127).astype(np.float32), (128, 2048)).copy()
    c["cm128_f"] = (s_ <= t_).astype(np.float32)
    c["cm128_b"] = (s_ >= t_).astype(np.float32)
    for g, L in (("p", 256), ("s", 2048)):
        Ac, As, Bc, Bs = _dft_tables(L)
        c[f"Ac_{g}"], c[f"As_{g}"], c[f"Bc_{g}"], c[f"Bs_{g}"] = Ac, As, Bc, Bs
        zT, w0, w1 = _hyena_pos(L)
        c[f"zT_{g}"], c[f"win0_{g}"], c[f"win1_{g}"] = zT, w0, w1
    _CONST_CACHE.update(c)
    return c


class KB:
    def __init__(self, dbg=None, nlayers=2):
        self.nc = bass.Bass("TRN2", target_bir_lowering=False)
        self.S = Sched(self.nc)
        self.dbg = dbg
        self.nlayers = nlayers
        self.gstack = contextlib.ExitStack()
        self.in_shapes = {}
        self.out_names = []
        self.uid = 0
        self.marks = []

    def din(self, name, shape, dt=F32):
        self.in_shapes[name] = (tuple(shape), dt)
        return self.nc.dram_tensor(name, list(shape), dt, kind="ExternalInput").ap()

    def dout(self, name, shape, dt=F32):
        self.out_names.append(name)
        return self.nc.dram_tensor(name, list(shape), dt, kind="ExternalOutput").ap()

    def dscr(self, name, shape, dt=F32):
        return self.nc.dram_tensor(name, list(shape), dt, kind="Internal").ap()

    def gsb(self, name, shape, dt=F32):
        return self.gstack.enter_context(self.nc.sbuf_tensor(name, list(shape), dt))

    def gps(self, name, shape, dt=F32):
        return self.gstack.enter_context(self.nc.psum_tensor(name, list(shape), dt))

    @contextlib.contextmanager
    def scope(self):
        st = contextlib.ExitStack()
        kb = self

        class Sc:
            def sb(self_, name, shape, dt=F32):
                kb.uid += 1
                return st.enter_context(kb.nc.sbuf_tensor(f"{name}_{kb.uid}", list(shape), dt))
        try:
            yield Sc()
        finally:
            self.S.barrier()
            st.close()

    def dump(self, name, src_ap, shape, dt, keys):
        o = self.dout("dbg_" + name, shape, dt)
        self.dma("sp", o, src_ap, keys, ["dbg_" + name])

    def act(self, out, in_, func, r, w, **kw):
        return self.S.op("act", lambda e: e.activation(out=out, in_=in_, func=func, **kw), r, w)

    def tt(self, eng, out, a, b, op, r, w):
        return self.S.op(eng, lambda e: e.tensor_tensor(out=out, in0=a, in1=b, op=op), r, w)

    def ts(self, eng, out, a, s1, s2, op0, op1, r, w):
        if op1 is None:
            return self.S.op(eng, lambda e: e.tensor_scalar(out=out, in0=a, scalar1=s1, scalar2=None, op0=op0), r, w)
        return self.S.op(eng, lambda e: e.tensor_scalar(out=out, in0=a, scalar1=s1, scalar2=s2, op0=op0, op1=op1), r, w)

    def stt(self, out, in0, scalar, in1, op0, op1, r, w):
        return self.S.op("dve", lambda e: e.scalar_tensor_tensor(out=out, in0=in0, scalar=scalar, in1=in1, op0=op0, op1=op1), r, w)

    def cp(self, eng, out, in_, r, w):
        if eng == "act":
            return self.S.op("act", lambda e: e.copy(out=out, in_=in_), r, w)
        return self.S.op(eng, lambda e: e.tensor_copy(out=out, in_=in_), r, w)

    def mm(self, out, lhsT, rhs, start, stop, r, w):
        return self.S.op("pe", lambda e: e.matmul(out, lhsT=lhsT, rhs=rhs, start=start, stop=stop), r, w)

    def tr(self, out, in_, ident, r, w):
        return self.S.op("pe", lambda e: e.transpose(out, in_, ident), r, w)

    def dma(self, q, out, in_, r, w, **kw):
        return self.S.dma(q, out, in_, r, w, **kw)

    def build(self):
        nc, S = self.nc, self.S
        NL = self.nlayers
        I = {}
        I["xT_p"] = self.din("xT_p", [D, 1024])
        I["xT_s"] = self.din("xT_s", [D, 2048])
        I["cT"] = self.din("cT", [128, 8, 2])
        I["ada_w"] = self.din("ada_w", [2, D, 6 * D])
        I["ada_bT"] = self.din("ada_bT", [128, 2, 48])
        I["norm_gT"] = self.din("norm_gT", [128, 2, 4, 8])
        I["ffn_up"] = self.din("ffn_up", [2, D, 2 * DFF])
        I["ffn_cwT"] = self.din("ffn_cwT", [128, 2, 44, 3])
        I["ffn_cbT"] = self.din("ffn_cbT", [128, 2, 44])
        I["ffn_down"] = self.din("ffn_down", [2, DFF, D])
        I["w_in_e"] = self.din("w_in_e", [D, 4096])
        I["w_out_e"] = self.din("w_out_e", [D, D])
        I["lbT"] = self.din("lbT", [128, 3, 8])
        I["hgrn_normT"] = self.din("hgrn_normT", [128, 4])
        I["hy_cwT"] = self.din("hy_cwT", [128, 12, 3])
        I["hy_cbT"] = self.din("hy_cbT", [128, 12])
        I["hy_w1"] = self.din("hy_w1", [33, 64])
        I["hy_b1"] = self.din("hy_b1", [64, 1])
        I["hy_w2"] = self.din("hy_w2", [64, 64])
        I["hy_b2"] = self.din("hy_b2", [64, 1])
        I["hy_w3"] = self.din("hy_w3", [64, 2048])
        I["hy_freq"] = self.din("hy_freq", [64, 1])
        I["hy_dT"] = self.din("hy_dT", [128, 2, 4])
        I["st_hgrn"] = self.din("st_hgrn", [128, 2, 4, 128])
        I["w_in_o"] = self.din("w_in_o", [D, 3104])
        I["w_out_o"] = self.din("w_out_o", [D, D])
        I["dlam"] = self.din("dlam", [128, 4, 64])
        I["diff_normT"] = self.din("diff_normT", [128, 4])
        I["gla_aw"] = self.din("gla_aw", [2, 16, 256])
        I["gla_abT"] = self.din("gla_abT", [128, 2, 2])
        I["gla_normT"] = self.din("gla_normT", [128, 4])
        I["ck"] = self.din("ck", [4, 256, 128])
        I["cv"] = self.din("cv", [4, 256, 128])
        I["st_gla"] = self.din("st_gla", [64, 2, 4, 128])
        C = {}
        for k, v in make_consts().items():
            C[k] = self.din("c_" + k, v.shape, BF16 if v.dtype == NPBF else F32)
        C["ropeC"] = self.din("c_ropeC", [128, 2048])
        C["ropeS"] = self.din("c_ropeS", [128, 2048])
        C["ropeP"] = self.din("c_ropeP", [128, 128])
        self.I, self.C = I, C
        O = {}
        O["yT_p"] = self.dout("yT_p", [D, 1024])
        O["yT_s"] = self.dout("yT_s", [D, 2048])
        O["nst_hgrn"] = self.dout("nst_hgrn", [4, 2, 4, 128, 128])
        O["nck"] = self.dout("nck", [4, 4, 256, 128])
        O["ncv"] = self.dout("ncv", [4, 4, 256, 128])
        O["nst_gla"] = self.dout("nst_gla", [4, 2, 4, 64, 128])
        self.O = O
        X = {}
        for g, gi in GROUPS.items():
            T = gi["T"]
            X[f"y_{g}"] = self.dscr(f"y_{g}", [D, T])
            X[f"proj_{g}"] = self.dscr(f"proj_{g}", [4096, T])
            X[f"vtok_{g}"] = self.dscr(f"vtok_{g}", [T, 1536])
            X[f"mix_{g}"] = self.dscr(f"mix_{g}", [D, T], BF16)
            X[f"ffa_{g}"] = self.dscr(f"ffa_{g}", [DFF, T], BF16)
            X[f"hyc_{g}"] = self.dscr(f"hyc_{g}", [1536, T])
            X[f"ksp_{g}"] = self.dscr(f"ksp_{g}", [2, 2, gi["L"] // 128 + 1, 128, 512], BF16)
        self.X = X
        self.ident_f = self.gsb("ident_f", [128, 128])
        self.ident_b = self.gsb("ident_b", [128, 128], BF16)
        self.ones_b = self.gsb("ones_b", [128, 128], BF16)
        self.cm_f = self.gsb("cm_f", [128, 128])
        self.cm_b = self.gsb("cm_b", [128, 128])
        self.ind4 = self.gsb("ind4", [128, 4])
        self.cm128_f = self.gsb("cm128_f", [128, 128])
        self.cm128_b = self.gsb("cm128_b", [128, 128])
        self.ones_col = self.gsb("ones_col", [128, 1])
        self.eps_col = self.gsb("eps_col", [128, 1])
        self.mod = [self.gsb(f"mod{l}", [128, 48, 2]) for l in range(2)]
        self.normg = self.gsb("normg", [128, 2, 4, 8])
        self.gs = self.gsb("gs", [128, 2, 2, 8, 2])
        self.gg = self.gsb("gg", [128, 2, 2, 8, 2])
        self.ps = [self.gps(f"ps{i}", [128, 512]) for i in range(8)]
        self.psT = self.ps[7][:].bitcast(BF16)
        self.ps_rot = 0
        for nm in ("ident_f", "ident_b", "ones_b", "cm_f", "cm_b", "ind4", "cm128_f", "cm128_b"):
            self.dma("sp", getattr(self, nm)[:], C[nm], [], [nm])
        S.op("dve", lambda e: e.memset(self.ones_col[:], 1.0), [], ["ones_col"])
        S.op("dve", lambda e: e.memset(self.eps_col[:], EPS), [], ["eps_col"])
        self.dma("sp", self.normg[:], I["norm_gT"], [], ["normg"])

        self.bg = None
        self.mods_setup()
        with self.scope() as scm:
            aw = [scm.sb(f"aw{i}", [128, 6 * D]) for i in range(2)]
            for _ in self.mods_gen(0, aw):
                pass
            if self.dbg == "mods" and NL > 1:
                for _ in self.mods_gen(1, aw):
                    pass
        if self.dbg == "mods":
            self.dump("mod0", self.mod[0][:], [128, 48, 2], F32, ["mod0"])
            self.dump("gs", self.gs[:], [128, 2, 2, 8, 2], F32, ["gs"])
        else:
            for l in range(NL):
                for g in ("p", "s"):
                    self.layer(l, g)
        self.mark("final")
        for g in ("p", "s"):
            self.dma("sp", O[f"yT_{g}"], X[f"y_{g}"], self.ykeys(g), [f"out_y_{g}"])
        S.barrier(["sp"])
        self.gstack.close()
        return nc

    def next_ps(self):
        i = self.ps_rot
        self.ps_rot = (self.ps_rot + 1) % 7
        return self.ps[i], f"ps{i}"

    def mods_setup(self):
        I = self.I
        self.m_cT = self.gsb("m_cT", [128, 8, 2])
        self.m_sT = self.gsb("m_sT", [128, 8, 2])
        self.m_ab = self.gsb("m_ab", [128, 2, 48])
        self.dma("sp", self.m_cT[:], I["cT"], [], ["cT"])
        self.dma("sp", self.m_ab[:], I["ada_bT"], [], ["ab"])
        self.act(self.m_sT[:], self.m_cT[:], AF.Silu, ["cT"], ["sT"])

    def mods_gen(self, l, aw):
        I = self.I
        sT, ab = self.m_sT, self.m_ab
        mod = self.mod[l]
        for j in range(8):
            a = aw[j % 2]
            self.dma("sp", a[:], I["ada_w"][l, j * 128:(j + 1) * 128, :], [], [f"aw{j % 2}"])
            yield
            ps, pk = self.ps[0], "ps0"
            for ch in range(48):
                self.mm(ps[:, ch * 2:ch * 2 + 2], a[:, ch * 128:(ch + 1) * 128], sT[:, j, :], True, True,
                        [f"aw{j % 2}", "sT"], [pk])
            pv = ps[:, 0:96].rearrange("p (c k) -> p c k", k=2)
            if j == 0:
                self.cp("dve", mod[:], pv, [pk], [f"mod{l}"])
            else:
                self.tt("dve", mod[:], mod[:], pv, ALU.add, [pk, f"mod{l}"], [f"mod{l}"])
            yield
        for cnd in range(2):
            self.tt("dve", mod[:, :, cnd], mod[:, :, cnd], ab[:, l, :], ALU.add, ["ab", f"mod{l}"], [f"mod{l}"])
        for which, (nidx, sc_lo, gnidx, gate_lo) in enumerate(((0, 8, 1, 16), (2, 32, 3, 40))):
            for cnd in range(2):
                self.stt(self.gs[:, l, which, :, cnd], mod[:, sc_lo:sc_lo + 8, cnd], 1.0, self.normg[:, l, nidx, :],
                         ALU.add, ALU.mult, [f"mod{l}", "normg"], [f"gs{l}"])
                self.tt("dve", self.gg[:, l, which, :, cnd], mod[:, gate_lo:gate_lo + 8, cnd], self.normg[:, l, gnidx, :],
                        ALU.mult, [f"mod{l}", "normg"], [f"gg{l}"])
        yield

    def bg_step(self, n=1):
        for _ in range(n):
            if self.bg is not None:
                try:
                    next(self.bg)
                except StopIteration:
                    self.bg = None

    def rstd_from_ps(self, rstd, ps, pk, n, r_extra, wkey, cols=512):
        self.act(rstd, ps, AF.Ln, [pk] + r_extra, [wkey], scale=1.0 / n, bias=self.eps_col[:, 0:1])
        self.act(rstd, rstd, AF.Exp, [wkey], [wkey], scale=-0.5)

    def norm_mod(self, sc, g, l, which, src_ap, src_key, on_tile=None):
        T = GROUPS[g]["T"]
        NT_ = T // 512
        cnd = 0 if g == "p" else 1
        ysb = [sc.sb(f"ysb{i}", [128, 8, 512]) for i in range(2)]
        sq = [sc.sb(f"sq{i}", [128, 8, 512], BF16) for i in range(2)]
        tmp = sc.sb("tmp", [128, 8, 512])
        rstd = [sc.sb(f"rstd{i}", [128, 512]) for i in range(2)]
        shift_lo = 0 if which == 0 else 24
        srcv = src_ap.rearrange("(j p) t -> p j t", p=128)

        def stats(tt):
            b_ = tt % 2
            tsl = slice(tt * 512, (tt + 1) * 512)
            self.dma("sp", ysb[b_][:], srcv[:, :, tsl], [src_key if src_key.startswith("xT") else f"{src_key}_{tt}"], [f"ysb{b_}"])
            self.act(sq[b_][:], ysb[b_][:], AF.Square, [f"ysb{b_}"], [f"sq{b_}"])
            ps, pk = self.next_ps()
            for j in range(8):
                self.mm(ps[:], self.ones_b[:], sq[b_][:, j, :], j == 0, j == 7, ["ones_b", f"sq{b_}"], [pk])
            self.rstd_from_ps(rstd[b_][:], ps[:], pk, D, [], f"rstd{b_}")

        stats(0)
        for tt in range(NT_):
            b_ = tt % 2
            tsl = slice(tt * 512, (tt + 1) * 512)
            if tt + 1 < NT_:
                stats(tt + 1)
            for j in range(8):
                self.stt(tmp[:, j, :], ysb[b_][:, j, :], self.gs[:, l, which, j, cnd:cnd + 1], rstd[b_][:], ALU.mult, ALU.mult,
                         [f"ysb{b_}", f"gs{l}", f"rstd{b_}"], [f"tmp{j}"])
                self.act(self.hT[:, j, tsl], tmp[:, j, :], AF.Identity, [f"tmp{j}", f"mod{l}"], [f"hT{tt}"],
                         bias=self.mod[l][:, shift_lo + j, cnd:cnd + 1], scale=1.0)
            if on_tile is not None:
                on_tile(tt)

    def load_w(self, wb, wkey, w_ap, KC, c0, ncols):
        wv = w_ap.rearrange("(k p) n -> p k n", p=128)
        step = 4
        for k0 in range(0, KC, step):
            k1 = min(KC, k0 + step)
            self.dma("pool", wb[:, k0:k1, 0:ncols], wv[:, k0:k1, c0:c0 + ncols], [], [wkey])

    def res_tiles(self, sc):
        return [dict(msb=sc.sb(f"msb{i}", [128, 8, 512]), sq=sc.sb(f"rsq{i}", [128, 8, 512], BF16),
                     rstd=sc.sb(f"rrstd{i}", [128, 512]), ysb=sc.sb(f"rysb{i}", [128, 8, 512]), i=i) for i in range(2)]

    def ykeys(self, g):
        return [f"y_{g}_{i}" for i in range(GROUPS[g]["T"] // 512)]

    def residual_load(self, g, ts_, ysrc_ap, ysrc_key, tt):
        i = ts_["i"]
        srcv = ysrc_ap.rearrange("(j p) t -> p j t", p=128)
        tsl = slice(tt * 512, (tt + 1) * 512)
        self.dma("sp", ts_["ysb"][:], srcv[:, :, tsl], [ysrc_key if ysrc_key.startswith("xT") else f"y_{g}_{tt}"], [f"rysb{i}"])

    def residual(self, g, l, which, ts_, ysrc_ap, ysrc_key, tt):
        cnd = 0 if g == "p" else 1
        i = ts_["i"]
        msb, sq, rstd, ysb = ts_["msb"], ts_["sq"], ts_["rstd"], ts_["ysb"]
        mk, sk, rk, yk = f"msb{i}", f"rsq{i}", f"rrstd{i}", f"rysb{i}"
        tsl = slice(tt * 512, (tt + 1) * 512)
        dstv = self.X[f"y_{g}"].rearrange("(j p) t -> p j t", p=128)
        ykey = f"y_{g}_{tt}"
        self.act(sq[:], msb[:], AF.Square, [mk], [sk])
        ps, pk = self.next_ps()
        for j in range(8):
            self.mm(ps[:], self.ones_b[:], sq[:, j, :], j == 0, j == 7, ["ones_b", sk], [pk])
        self.rstd_from_ps(rstd[:], ps[:], pk, D, [], rk)
        for j in range(8):
            self.stt(msb[:, j, :], msb[:, j, :], self.gg[:, l, which, j, cnd:cnd + 1], rstd[:], ALU.mult, ALU.mult,
                     [mk, f"gg{l}", rk], [f"{mk}_{j}"])
            self.tt("pool", msb[:, j, :], msb[:, j, :], ysb[:, j, :], ALU.add, [f"{mk}_{j}", yk], [f"{mk}_{j}"])
        self.dma("sp", dstv[:, :, tsl], msb[:], [f"{mk}_{j}" for j in range(8)], [ykey])

    def mark(self, name):
        self.marks.append((name, dict(self.S.cnt)))

    def layer(self, l, g):
        X, I = self.X, self.I
        T = GROUPS[g]["T"]
        self.mark(f"L{l}{g} norm+proj")
        ysrc = (I[f"xT_{g}"], f"xT_{g}") if l == 0 else (X[f"y_{g}"], f"y_{g}")
        with self.scope() as sc:
            self.hT = sc.sb("hT", [128, 8, T], BF16)
            nrm = (l, 0, ysrc[0], ysrc[1])
            if l % 2 == 0:
                self.proj_in(sc, g, I["w_in_e"], 4096, tok_blocks={3: 0}, norm=nrm)
            else:
                self.proj_in(sc, g, I["w_in_o"], 3104, tok_blocks={1: 0, 2: 512, 4: 1024}, norm=nrm)
            if self.dbg == "proj":
                self.dump(f"hT_{g}", self.hT[:, :, 0:T], [128, 8, T], BF16, [f"hT{i}" for i in range(T // 512)])
                self.dump(f"proj_{g}", X[f"proj_{g}"], [4096, T], F32, [f"proj_{g}"])
                self.dump(f"vtok_{g}", X[f"vtok_{g}"], [T, 1536], F32, [f"vtok_{g}"])
        if self.dbg == "proj":
            return
        self.mark(f"L{l}{g} mixerA")
        if l % 2 == 0:
            if g == "p" and self.nlayers > 1:
                with self.scope() as scm:
                    aw = [scm.sb(f"aw{i}", [128, 6 * D]) for i in range(2)]
                    self.bg = self.mods_gen(1, aw)
                    self.hgrn(g)
                    self.bg_step(100)
            else:
                self.hgrn(g)
            if self.dbg == "hgrn":
                return
            self.mark(f"L{l}{g} mixerB")
            self.hyena(g)
            if self.dbg == "hyena":
                return
            w_out = I["w_out_e"]
        else:
            self.diffattn(g)
            if self.dbg == "diff":
                return
            self.mark(f"L{l}{g} mixerB")
            self.gla(g)
            if self.dbg == "gla":
                return
            w_out = I["w_out_o"]
        self.mark(f"L{l}{g} outproj")
        with self.scope() as sc:
            self.hT = sc.sb("hT", [128, 8, T], BF16)
            wb = sc.sb("wout", [128, 8, D], BF16)
            self.load_w(wb, "wout", w_out, 8, 0, D)
            self.dma("sp", self.hT[:, :, 0:T], X[f"mix_{g}"].rearrange("(j p) t -> p j t", p=128), [f"mix_{g}"], [f"hT{i}" for i in range(T // 512)])
            rts = self.res_tiles(sc)
            self.residual_load(g, rts[0], ysrc[0], ysrc[1], 0)
            for tt in range(T // 512):
                tsl = slice(tt * 512, (tt + 1) * 512)
                ts_ = rts[tt % 2]
                if tt + 1 < T // 512:
                    self.residual_load(g, rts[(tt + 1) % 2], ysrc[0], ysrc[1], tt + 1)
                for m in range(8):
                    ps, pk = self.next_ps()
                    for k in range(8):
                        self.mm(ps[:], wb[:, k, m * 128:(m + 1) * 128], self.hT[:, k, tsl], k == 0, k == 7, ["wout", f"hT{tt}"], [pk])
                    self.cp("act" if m % 2 else "dve", ts_["msb"][:, m, :], ps[:], [pk], [f"msb{tt % 2}", f"msb{tt % 2}_{m}"])
                self.residual(g, l, 0, ts_, ysrc[0], ysrc[1], tt)
        self.mark(f"L{l}{g} ffn")
        with self.scope() as scA:
            wd = scA.sb("wd", [128, 22, D], BF16)
            with self.scope() as scB:
                self.hT = scB.sb("hT", [128, 8, T], BF16)
                with self.scope() as scC:
                    self.norm_mod(scC, g, l, 1, X[f"y_{g}"], f"y_{g}")
                with self.scope() as scD:
                    self.ffn_up(scD, g, l, wd)
            self.mark(f"L{l}{g} ffn_down")
            with self.scope() as scE:
                self.ffn_down(scE, g, l, wd)

    def proj_in(self, sc, g, w_ap, ncols, tok_blocks, norm=None):
        T = GROUPS[g]["T"]
        X = self.X
        wbs = [sc.sb(f"wb{i}", [128, 8, 512], BF16) for i in range(2)]
        stg = [sc.sb(f"stg{i}", [128, 512]) for i in range(4)]
        st_ = {"si": 0}
        nblk = (ncols + 511) // 512

        stg4 = [sc.sb(f"stgq{i}", [128, 4, 512]) for i in range(2)]

        def fm_tile(cb, tt):
            c0 = cb * 512
            nc_ = min(512, ncols - c0)
            wb, wk = wbs[cb % 2], f"wb{cb % 2}"
            tsl = slice(tt * 512, (tt + 1) * 512)
            if nc_ == 512:
                qi = st_["qi"] = st_.get("qi", 0) + 1
                st4, s4k = stg4[qi % 2], f"stgq{qi % 2}"
                for m in range(4):
                    ps, pk = self.next_ps()
                    for k in range(8):
                        self.mm(ps[:], wb[:, k, m * 128:(m + 1) * 128], self.hT[:, k, tsl], k == 0, k == 7, [wk, f"hT{tt}"], [pk])
                    self.cp("act" if m % 2 else "dve", st4[:, m, :], ps[:], [pk], [f"{s4k}_{m}"])
                self.dma("sp", X[f"proj_{g}"][c0:c0 + 512, tsl].rearrange("(m p) t -> p m t", p=128), st4[:],
                         [f"{s4k}_{m}" for m in range(4)], [f"proj_{g}"])
                return
            for m in range((nc_ + 127) // 128):
                mw = min(128, nc_ - m * 128)
                ps, pk = self.next_ps()
                for k in range(8):
                    self.mm(ps[0:mw, :], wb[:, k, m * 128:m * 128 + mw], self.hT[:, k, tsl], k == 0, k == 7, [wk, f"hT{tt}"], [pk])
                si = st_["si"]
                st, sk = stg[si % 4], f"stg{si % 4}"
                self.cp("act" if si % 2 else "dve", st[0:mw, :], ps[0:mw, :], [pk], [sk])
                st_["si"] += 1
                self.dma("sp", X[f"proj_{g}"][c0 + m * 128:c0 + m * 128 + mw, tsl], st[0:mw, :], [sk], [f"proj_{g}"])

        self.load_w(wbs[0], "wb0", w_ap, 8, 0, min(512, ncols))
        if norm is not None:
            l, which, src_ap, src_key = norm
            self.norm_mod(sc, g, l, which, src_ap, src_key, on_tile=lambda tt: fm_tile(0, tt))
        for cb in range(nblk):
            c0 = cb * 512
            nc_ = min(512, ncols - c0)
            wb, wk = wbs[cb % 2], f"wb{cb % 2}"
            if cb > 0:
                self.load_w(wb, wk, w_ap, 8, c0, nc_)
            if cb > 0 or norm is None:
                for tt in range(T // 512):
                    fm_tile(cb, tt)
            if cb in tok_blocks and tok_blocks[cb] is not None:
                off = tok_blocks[cb]
                for tq in range(T // 512):
                    qi = st_["qi"] = st_.get("qi", 0) + 1
                    st4, s4k = stg4[qi % 2], f"stgq{qi % 2}"
                    for b4 in range(4):
                        tb = tq * 4 + b4
                        ps, pk = self.next_ps()
                        for k in range(8):
                            self.mm(ps[:], self.hT[:, k, tb * 128:(tb + 1) * 128], wb[:, k, :], k == 0, k == 7, [wk, f"hT{tq}"], [pk])
                        self.cp("act" if b4 % 2 else "dve", st4[:, b4, :], ps[:], [pk], [f"{s4k}_{b4}"])
                    self.dma("sp", X[f"vtok_{g}"][tq * 512:(tq + 1) * 512, off:off + 512].rearrange("(b p) c -> p b c", p=128), st4[:],
                             [f"{s4k}_{m}" for m in range(4)], [f"vtok_{g}"])

    def ffn_up(self, sc, g, l, wd):
        I, X = self.I, self.X
        gi = GROUPS[g]
        T, L, nseq = gi["T"], gi["L"], gi["nseq"]
        wbs = [sc.sb(f"wu{i}", [128, 8, 256], BF16) for i in range(2)]
        cw = sc.sb("cw", [128, 44, 3])
        cb = sc.sb("cb", [128, 44])
        self.dma("sp", cw[:], I["ffn_cwT"][:, l], [], ["cw"])
        self.dma("sp", cb[:], I["ffn_cbT"][:, l], [], ["cb"])
        raw4 = [sc.sb(f"raw{i}", [128, T]) for i in range(4)]
        cv4 = [sc.sb(f"cv{i}", [128, T]) for i in range(4)]
        obs = [sc.sb(f"ffo{i}", [128, T], BF16) for i in range(2)]
        wv = I["ffn_up"][l].rearrange("(k p) n -> p k n", p=128)
        def tail(i):
            ob, obk = obs[i % 2], f"ffo{i % 2}"
            b0, b1 = (i % 2) * 2, (i % 2) * 2 + 1
            for half in range(2):
                chn = i + 22 * half
                bi = (i % 2) * 2 + half
                self.dwconv_shift(raw4[bi], f"raw{bi}", cv4[bi], f"cv{bi}", cw[:, chn, :], ["cw"], nseq, L)
            self.act(cv4[b1][:], cv4[b1][:], AF.Silu, [f"cv{b1}"], [f"cv{b1}"])
            self.tt("dve", ob[:], cv4[b1][:], cv4[b0][:], ALU.mult, [f"cv{b0}", f"cv{b1}"], [obk])
            self.dma("sp", X[f"ffa_{g}"][i * 128:(i + 1) * 128, :], ob[:], [obk], [f"ffa_{g}"])

        wdv = I["ffn_down"][l].rearrange("(k p) n -> p k n", p=128)
        for i in range(23):
            if i < 22:
                wb, wk = wbs[i % 2], f"wu{i % 2}"
                self.dma("pool", wb[:, :, 0:128], wv[:, :, i * 128:(i + 1) * 128], [], [wk])
                self.dma("pool", wb[:, :, 128:256], wv[:, :, DFF + i * 128:DFF + (i + 1) * 128], [], [wk])
                if 2 <= i < 13:
                    k0 = (i - 2) * 2
                    self.dma("pool", wd[:, k0:k0 + 2, :], wdv[:, k0:k0 + 2, :], [], ["wd"])
                for half in range(2):
                    chn = i + 22 * half
                    bi = (i % 2) * 2 + half
                    for tt in range(T // 512):
                        tsl = slice(tt * 512, (tt + 1) * 512)
                        ps, pk = self.next_ps()
                        for k in range(8):
                            self.mm(ps[:], wb[:, k, half * 128:(half + 1) * 128], self.hT[:, k, tsl], k == 0, k == 7, [wk, f"hT{tt}"], [pk])
                        self.cp("act", raw4[bi][:, tsl], ps[:], [pk], [f"raw{bi}"])
                    self.act(cv4[bi][:], raw4[bi][:], AF.Identity, [f"raw{bi}", "cw", "cb"], [f"cv{bi}"],
                             scale=cw[:, chn, 1:2], bias=cb[:, chn:chn + 1])
            if i >= 1:
                tail(i - 1)

    def dwconv_shift(self, x, xk, o, ok, w3, wkeys, nseq, L):
        xv = x[:].rearrange("p (s t) -> p s t", t=L)
        ov = o[:].rearrange("p (s t) -> p s t", t=L)
        self.stt(ov[:, :, 1:L], xv[:, :, 0:L - 1], w3[:, 0:1], ov[:, :, 1:L], ALU.mult, ALU.add, [xk, ok] + wkeys, [ok])
        self.stt(ov[:, :, 0:L - 1], xv[:, :, 1:L], w3[:, 2:3], ov[:, :, 0:L - 1], ALU.mult, ALU.add, [xk, ok] + wkeys, [ok])

    def dwconv(self, x, xk, o, ok, w3, b, wkeys, nseq, L):
        xv = x[:].rearrange("p (s t) -> p s t", t=L)
        ov = o[:].rearrange("p (s t) -> p s t", t=L)
        self.act(o[:], x[:], AF.Identity, [xk] + wkeys, [ok], scale=w3[:, 1:2], bias=b)
        self.stt(ov[:, :, 1:L], xv[:, :, 0:L - 1], w3[:, 0:1], ov[:, :, 1:L], ALU.mult, ALU.add, [xk, ok] + wkeys, [ok])
        self.stt(ov[:, :, 0:L - 1], xv[:, :, 1:L], w3[:, 2:3], ov[:, :, 0:L - 1], ALU.mult, ALU.add, [xk, ok] + wkeys, [ok])

    def ffn_down(self, sc, g, l, wd):
        I, X = self.I, self.X
        T = GROUPS[g]["T"]
        aa = [sc.sb(f"ffa{i}", [128, 22, 512], BF16) for i in range(2)]
        rts = self.res_tiles(sc)
        av = X[f"ffa_{g}"].rearrange("(k p) t -> p k t", p=128)
        NT_ = T // 512
        self.dma("sp", aa[0][:], av[:, :, 0:512], [f"ffa_{g}"], ["ffa0"])
        self.residual_load(g, rts[0], X[f"y_{g}"], f"y_{g}", 0)
        for tt in range(NT_):
            tsl = slice(tt * 512, (tt + 1) * 512)
            a, ak = aa[tt % 2], f"ffa{tt % 2}"
            ts_ = rts[tt % 2]
            if tt + 1 < NT_:
                self.dma("sp", aa[(tt + 1) % 2][:], av[:, :, (tt + 1) * 512:(tt + 2) * 512], [f"ffa_{g}"], [f"ffa{(tt + 1) % 2}"])
                self.residual_load(g, rts[(tt + 1) % 2], X[f"y_{g}"], f"y_{g}", tt + 1)
            for m in range(8):
                ps, pk = self.next_ps()
                for k in range(22):
                    self.mm(ps[:], wd[:, k, m * 128:(m + 1) * 128], a[:, k, :], k == 0, k == 21, ["wd", ak], [pk])
                self.cp("act" if m % 2 else "dve", ts_["msb"][:, m, :], ps[:], [pk], [f"msb{tt % 2}", f"msb{tt % 2}_{m}"])
            self.residual(g, l, 1, ts_, X[f"y_{g}"], f"y_{g}", tt)

    def scan_prep_dir(self, sc, d, L, dk, pb, qs, qk, qscale, kk, kkk, lf, lfk, tmpA, qtf, qt, kt, kh, el, kT, CH=CH):
        P = slice(pb, pb + dk)
        NB = L // 128
        NCH = L // CH
        cum = tmpA
        ak, hk = f"scA{d}", f"kh{d}"
        if d == "f":
            self.S.op("dve", lambda e: e.tensor_tensor_scan(out=cum[P, 0:L], data0=self.mask_f[P, 0:L], data1=lf[P, 0:L],
                                                             initial=0.0, op0=ALU.mult, op1=ALU.add), [lfk, "mask_f"], [ak])
        else:
            self.S.op("dve", lambda e: e.tensor_tensor_scan(out=cum[P, 0:L][:, ::-1],
                                                             data0=self.mask_b[P, 0:L][:, ::-1], data1=lf[P, 0:L][:, ::-1],
                                                             initial=0.0, op0=ALU.mult, op1=ALU.add), [lfk, "mask_b"], [ak])
        yield
        self.act(qtf[P, 0:L], cum[P, 0:L], AF.Exp, [ak], [f"qtf{d}"])
        yield
        ev = qtf[P, 0:L].rearrange("p (c t) -> p c t", t=CH)
        pos = CH - 1 if d == "f" else 0
        self.cp("dve", el[P, 0:NCH], ev[:, :, pos], [f"qtf{d}"], [f"el{d}"])
        yield
        self.stt(qtf[P, 0:L], qs[P, 0:L], float(qscale), qtf[P, 0:L], ALU.mult, ALU.mult, [qk, f"qtf{d}", f"el{d}"], [f"qtf{d}"])
        yield
        self.cp("act", qt[P, 0:L], qtf[P, 0:L], [f"qtf{d}"], [f"qt{d}"])
        yield
        self.act(cum[P, 0:L], cum[P, 0:L], AF.Exp, [ak], [ak], scale=-1.0)
        yield
        self.tt("dve", kt[P, 0:L], kk[P, 0:L], cum[P, 0:L], ALU.mult, [kkk, ak], [f"kt{d}"])
        yield
        elb = el[P, 0:NCH].unsqueeze(2).broadcast_to([dk, NCH, CH])
        self.tt("dve", kh[P, 0:L].rearrange("p (c t) -> p c t", t=CH), kt[P, 0:L].rearrange("p (c t) -> p c t", t=CH), elb,
                ALU.mult, [f"kt{d}", f"el{d}"], [hk])
        yield
        for blk in range(NB):
            j = blk % 8
            self.tr(self.psT[:, j * 128:j * 128 + dk], kh[P, blk * 128:(blk + 1) * 128], self.ident_b[P, P], [hk, "ident_b"], ["ps7"])
            if j == 7 or blk == NB - 1:
                n = j + 1
                b0 = blk - j
                self.cp("act" if (blk // 8) % 2 else "dve", kT[:, b0:b0 + n, 0:dk],
                        self.psT[:, 0:n * 128].rearrange("p (b c) -> p b c", c=128)[:, :, 0:dk], ["ps7"], [f"kT{d}"])
                yield

    @staticmethod
    def interleave(gens):
        gens = list(gens)
        while gens:
            for g_ in list(gens):
                try:
                    next(g_)
                except StopIteration:
                    gens.remove(g_)

    def scan_chain(self, sc, L, dk, pb, qtf, qt, kt, el, kT, Vt, vk, Sst, oT, cm, init_state, fin_fn, CH=CH, seg_blocks=None):
        P = slice(pb, pb + dk)
        NB = L // 128
        SB = seg_blocks or NB
        DIRS = ("f", "b")
        NR = 12
        PA, PAK = self.ps[0], "ps0"
        POV, POVK = self.ps[1], "ps1"
        PD = {"f": ((self.ps[2], "ps2"), (self.ps[3], "ps3")), "b": ((self.ps[4], "ps4"), (self.ps[5], "ps5"))}
        PIN = {"f": (self.ps[6], "ps6"), "b": (self.ps[7], "ps7")}
        blk_of = lambda d, i: i if d == "f" else NB - 1 - i
        NCB = 128 // CH
        order = {"f": tuple(range(NCB)), "b": tuple(reversed(range(NCB)))}
        nst = {"f": 0, "b": 0}
        cur = {}
        prevref = {}

        def stage_att(i):
            for di, d in enumerate(DIRS):
                blk = blk_of(d, i)
                t0 = blk * 128
                sl = (i % 2) * 2 + di
                pa = PA[:, sl * 128:(sl + 1) * 128]
                self.mm(pa, kt[d][P, t0:t0 + 128], qt[d][P, t0:t0 + 128], True, True, [f"kt{d}", f"qt{d}"], [PAK])
                self.tt("dve", self.AT[d][i % 2][:], pa, cm[d][:], ALU.mult, [PAK, "cm"], [f"AT{d}{i % 2}"])

        def stage_pe(i):
            for di, d in enumerate(DIRS):
                blk = blk_of(d, i)
                sl = (i % 2) * 2 + di
                pov = POV[:, sl * 128:(sl + 1) * 128]
                self.mm(pov, Vt[:, blk, :], self.AT[d][i % 2][:], True, True, [vk, f"AT{d}{i % 2}"], [POVK])
                pd, pdk = PD[d][i % 2]
                if NCB == 1:
                    self.mm(pd[P, 0:128], kT[d][:, blk, 0:dk], Vt[:, blk, :], True, True, [f"kT{d}", vk], [pdk])
                else:
                    for c in range(NCB):
                        self.mm(pd[P, c * 128:(c + 1) * 128], kT[d][:, blk, 0:dk], self.Vm[:, c, blk, :], True, True,
                                [f"kT{d}", "Vm"], [pdk])

        def stage_chain(i):
            for idx in range(NCB):
                for di, d in enumerate(DIRS):
                    blk = blk_of(d, i)
                    seg = blk // SB
                    c = order[d][idx]
                    pd, pdk = PD[d][i % 2]
                    if i % SB == 0 and idx == 0:
                        cur[d] = init_state(d, seg)
                    st0, sk0 = cur[d]
                    prevref[(d, i, idx)] = cur[d]
                    k1 = nst[d] % NR
                    nst[d] += 1
                    ci = blk * NCB + c
                    self.stt(Sst[d][k1][P, :], st0[P, :], el[d][P, ci:ci + 1], pd[P, c * 128:(c + 1) * 128], ALU.mult, ALU.add,
                             [sk0, f"el{d}", pdk], [f"S{d}{k1}"])
                    cur[d] = (Sst[d][k1], f"S{d}{k1}")
                    if i % SB == SB - 1 and idx == NCB - 1:
                        fin_fn(d, seg, Sst[d][k1], f"S{d}{k1}")
            for di, d in enumerate(DIRS):
                blk = blk_of(d, i)
                sl = (i % 2) * 2 + di
                self.cp("act", oT[d][:, blk * 128:(blk + 1) * 128], POV[:, sl * 128:(sl + 1) * 128], [POVK], [f"oT{d}"])

        def stage_inter(i):
            for di, d in enumerate(DIRS):
                blk = blk_of(d, i)
                t0 = blk * 128
                pin, pink = PIN[d]
                sl = i % 4
                for idx in range(NCB):
                    c = order[d][idx]
                    st0, sk0 = prevref.pop((d, i, idx))
                    cs = slice(t0 + CH * c, t0 + CH * (c + 1))
                    self.mm(pin[:, sl * 128 + CH * c:sl * 128 + CH * (c + 1)], st0[P, :], qtf[d][P, cs], True, True,
                            [sk0, f"qtf{d}"], [pink])
                self.tt("dve", oT[d][:, t0:t0 + 128], oT[d][:, t0:t0 + 128], pin[:, sl * 128:(sl + 1) * 128], ALU.add,
                        [f"oT{d}", pink], [f"oT{d}"])

        stage_att(0)
        for i in range(NB + 1):
            if i + 1 < NB:
                stage_att(i + 1)
            if i < NB:
                stage_pe(i)
                stage_chain(i)
            if i >= 1:
                stage_inter(i - 1)

    def head_out(self, sc, g, L, col0, seq, oTf, oTb, gate, gatek, gnorm_col, gnk, sq, rstd, outb, mixrow0, extra_scale=None):
        T0 = seq * L
        if oTb is not None:
            self.tt("dve", oTf[:, 0:L], oTf[:, 0:L], oTb[:, 0:L], ALU.add, ["oTf", "oTb"], ["oTf"])
        self.act(sq[:, 0:L], oTf[:, 0:L], AF.Square, ["oTf"], ["hsq"])
        if gate is not None:
            self.act(gate[:, 0:L], gate[:, 0:L], AF.Silu, [gatek], [gatek])
        for t0 in range(0, L, 512):
            w = min(512, L - t0)
            ps, pk = self.ps[0], "ps0"
            self.mm(ps[:, 0:w], self.ones_b[:], sq[:, t0:t0 + w], True, True, ["ones_b", "hsq"], [pk])
            self.rstd_from_ps(rstd[:, 0:w], ps[:, 0:w], pk, 128, [], "hrstd")
            self.stt(oTf[:, t0:t0 + w], oTf[:, t0:t0 + w], gnorm_col, rstd[:, 0:w], ALU.mult, ALU.mult, ["oTf", gnk, "hrstd"], ["oTf"])
        if gate is not None:
            self.tt("dve", outb[:, 0:L], oTf[:, 0:L], gate[:, 0:L], ALU.mult, ["oTf", gatek], ["houtb"])
        else:
            self.act(outb[:, 0:L], oTf[:, 0:L], AF.Copy, ["oTf"], ["houtb"], scale=float(extra_scale))
        self.dma("sp", self.X[f"mix_{g}"][mixrow0:mixrow0 + 128, T0:T0 + L], outb[:, 0:L], ["houtb"], [f"mix_{g}"])

    def load_scan_masks(self, sc, L, ch=CH):
        self.mask_f = sc.sb("mask_f", [128, L])
        self.mask_b = sc.sb("mask_b", [128, L])
        pre = "mask" if ch == CH else f"mask{ch}"
        self.dma("sp", self.mask_f[:], self.C[pre + "_f"][:, 0:L], [], ["mask_f"])
        self.dma("sp", self.mask_b[:], self.C[pre + "_b"][:, 0:L], [], ["mask_b"])

    def hgrn(self, g):
        I, X, O = self.I, self.X, self.O
        gi = GROUPS[g]
        T, Lseq, nseq0 = gi["T"], gi["L"], gi["nseq"]
        L, nseq = T, 1
        SEGB = Lseq // 128
        NB = L // 128
        with self.scope() as sc:
            self.load_scan_masks(sc, L)
            Zst = sc.sb("Zst", [128, 128])
            self.S.op("dve", lambda e: e.memset(Zst[:], 0.0), [], ["Zst"])
            Sin = {d: sc.sb(f"Sin{d}", [128, 128]) for d in "fb"}
            lbt = sc.sb("lbt", [128, 3, 8])
            lb = sc.sb("lb", [128, 8])
            oml = sc.sb("oml", [128, 8])
            gn = sc.sb("gn", [128, 4])
            self.dma("sp", lbt[:], I["lbT"], [], ["lbt"])
            self.dma("sp", gn[:], I["hgrn_normT"], [], ["gn"])
            self.act(lbt[:], lbt[:], AF.Exp, ["lbt"], ["lbt"])
            self.tt("dve", lb[:], lbt[:, 0, :], lbt[:, 1, :], ALU.add, ["lbt"], ["lb"])
            self.tt("dve", lb[:], lb[:], lbt[:, 2, :], ALU.add, ["lbt", "lb"], ["lb"])
            self.S.op("dve", lambda e: e.reciprocal(out=lb[:], in_=lb[:]), ["lb"], ["lb"])
            self.tt("dve", lb[:], lb[:], lbt[:, 0, :], ALU.mult, ["lbt", "lb"], ["lb"])
            self.ts("dve", oml[:], lb[:], -1.0, 1.0, ALU.mult, ALU.add, ["lb"], ["oml"])
            qs = sc.sb("qs", [128, L])
            t1 = {d: sc.sb(f"t1{d}", [128, L]) for d in "fb"}
            t2 = {d: sc.sb(f"t2{d}", [128, L]) for d in "fb"}
            tA = {d: sc.sb(f"tA{d}", [128, L]) for d in "fb"}
            gr = tA["f"]
            qtf = {d: sc.sb(f"qtf{d}", [128, L]) for d in "fb"}
            kh = {d: sc.sb(f"kh{d}", [128, L], BF16) for d in "fb"}
            qt = {d: sc.sb(f"qt{d}", [128, L], BF16) for d in "fb"}
            kt = {d: sc.sb(f"kt{d}", [128, L], BF16) for d in "fb"}
            el = {d: sc.sb(f"el{d}", [128, L // CH]) for d in "fb"}
            kT = {d: sc.sb(f"kT{d}", [128, NB, 128], BF16) for d in "fb"}
            Vt = sc.sb("Vt", [128, NB, 128], BF16)
            self.Vm = sc.sb("Vm", [128, 4, NB, 128], BF16)
            Sst = {d: [sc.sb(f"S{d}{i}", [128, 128]) for i in range(12)] for d in "fb"}
            oT = {d: sc.sb(f"oT{d}", [128, L]) for d in "fb"}
            self.AT = {d: [sc.sb(f"AT{d}{i}", [128, 128], BF16) for i in range(2)] for d in "fb"}
            sq = sc.sb("hsq", [128, L], BF16)
            rstd = sc.sb("hrstd", [128, 512])
            outb = sc.sb("houtb", [128, L], BF16)
            cm = {"f": self.cm_f, "b": self.cm_b}
            pj = X[f"proj_{g}"]
            for s in range(nseq):
                tsl = slice(s * L, (s + 1) * L)
                def issue_loads(h):
                    self.dma("sp", qs[:], pj[h * 128:(h + 1) * 128, tsl], [f"proj_{g}"], ["qs"])
                    for di, d in enumerate("fb"):
                        r0 = (1 + di) * 512 + h * 128
                        self.dma("sp", t1[d][:], pj[r0:r0 + 128, tsl], [f"proj_{g}"], [f"t1{d}"])
                    self.dma("pool", Vt[:], X[f"vtok_{g}"][tsl, h * 128:(h + 1) * 128].rearrange("(b p) v -> p b v", p=128),
                             [f"vtok_{g}"], ["Vt"])

                issue_loads(0)
                for h in range(4):
                    rows = lambda blk: slice(blk * 512 + h * 128, blk * 512 + (h + 1) * 128)
                    self.act(qs[:], qs[:], AF.Silu, ["qs"], ["qs"])
                    for c in range(4):
                        self.act(self.Vm[:, c], Vt[:], AF.Copy, ["Vt", "ind4"], ["Vm"], scale=self.ind4[:, c:c + 1])
                    def prep(di, d, h=h, rows=rows, tsl=tsl):
                        a1, a2 = t1[d], t2[d]
                        k1, k2 = f"t1{d}", f"t2{d}"
                        self.act(a1[:], a1[:], AF.Sigmoid, [k1], [k1])
                        yield
                        c8 = di * 4 + h
                        self.act(a1[:], a1[:], AF.Identity, [k1, "oml", "lb"], [k1], scale=oml[:, c8:c8 + 1], bias=lb[:, c8:c8 + 1])
                        yield
                        self.act(a2[:], a1[:], AF.Ln, [k1], [k2])
                        yield
                        self.act(a1[:], a1[:], AF.Identity, [k1, "ones_col"], [k1], scale=-1.0, bias=self.ones_col[:, 0:1])
                        yield
                        yield from self.scan_prep_dir(sc, d, L, 128, 0, qs, "qs", 128 ** -0.5, a1, k1, a2, k2, tA[d], qtf[d], qt[d],
                                                      kt[d], kh[d], el[d], kT[d])

                    self.interleave([prep(0, "f"), prep(1, "b")])
                    if g == "s":
                        for di, d in enumerate("fb"):
                            self.dma("sp", Sin[d][:], I["st_hgrn"][:, di, h, :], [], [f"Sin{d}"])

                    def init_state(d, seg, h=h):
                        return (Zst, "Zst") if g == "p" else (Sin[d], f"Sin{d}")

                    def fin_fn(d, seg, St, sk, h=h):
                        if g == "p":
                            self.dma("sp", O["nst_hgrn"][seg, 0 if d == "f" else 1, h], St[:], [sk], ["nst_hgrn"])

                    self.bg_step(3)
                    self.dma("sp", gr[:], pj[rows(4), tsl], [f"proj_{g}"], ["scAf"])
                    self.scan_chain(sc, L, 128, 0, qtf, qt, kt, el, kT, Vt, "Vt", Sst, oT, cm, init_state, fin_fn, seg_blocks=SEGB)
                    self.bg_step(2)
                    if h + 1 < 4:
                        issue_loads(h + 1)
                    self.head_out(sc, g, L, 0, s, oT["f"], oT["b"], gr, "scAf", gn[:, h:h + 1], "gn", sq, rstd, outb, h * 128)
            if self.dbg == "hgrn":
                self.dump(f"mixa_{g}", X[f"mix_{g}"][0:512, :], [512, T], BF16, [f"mix_{g}"])

    def wrap_pi(self, ap, key, tmp, tmpk):
        for _ in range(2):
            self.ts("dve", tmp, ap, -math.pi, 2.0 * math.pi, ALU.is_lt, ALU.mult, [key], [tmpk])
            self.tt("dve", ap, ap, tmp, ALU.add, [key, tmpk], [key])
            self.ts("dve", tmp, ap, math.pi, -2.0 * math.pi, ALU.is_gt, ALU.mult, [key], [tmpk])
            self.tt("dve", ap, ap, tmp, ALU.add, [key, tmpk], [key])

    def hyena(self, g):
        I, X, C = self.I, self.X, self.C
        gi = GROUPS[g]
        T, L, nseq = gi["T"], gi["L"], gi["nseq"]
        SC = L // 128
        NF = SC + 1
        NT = max(1, L // 512)
        TW = min(L, 512)
        ksp = X[f"ksp_{g}"]
        self.mark(f"hy{g} A:mlp")
        with self.scope() as sc:
            w1 = sc.sb("hw1", [33, 64]); w2 = sc.sb("hw2", [64, 64]); w3 = sc.sb("hw3", [64, 2048])
            b1 = sc.sb("hb1", [64, 1]); b2 = sc.sb("hb2", [64, 1]); fr = sc.sb("hfr", [64, 1])
            for t_, nm in ((w1, "hy_w1"), (w2, "hy_w2"), (w3, "hy_w3"), (b1, "hy_b1"), (b2, "hy_b2"), (fr, "hy_freq")):
                self.dma("sp", t_[:], I[nm], [], ["hyw"])
            h2 = sc.sb("h2", [64, L])
            with self.scope() as sc_mlp:
                zT = sc_mlp.sb("zT", [33, L])
                self.dma("sp", zT[:], C[f"zT_{g}"], [], ["zT"])
                h1 = sc_mlp.sb("h1", [64, L]); hw = sc_mlp.sb("hwrap", [64, L])
                for (src, srck, wt_, bb, dst, dstk) in ((zT, "zT", w1, b1, h1, "h1"), (h1, "h1", w2, b2, h2, "h2")):
                    for t0 in range(0, L, 512):
                        w = min(512, L - t0)
                        ps, pk = self.next_ps()
                        self.mm(ps[0:64, 0:w], wt_[:], src[:, t0:t0 + w], True, True, ["hyw", srck], [pk])
                        self.ts("dve", dst[:, t0:t0 + w], ps[0:64, 0:w], bb[:, 0:1], fr[:, 0:1], ALU.add, ALU.mult, [pk, "hyw"], [dstk])
                    self.wrap_pi(dst[:], dstk, hw[:], "hwrap")
                    self.act(dst[:], dst[:], AF.Sin, [dstk], [dstk])
            wt = [[sc.sb(f"win{i}{j}", [128, 512]) for j in range(2)] for i in range(2)]
            ge = [sc.sb(f"ge{o}", [128, SC, 512], BF16) for o in range(2)]
            go = [sc.sb(f"go{o}", [128, SC, 512], BF16) for o in range(2)]
            ff = [[sc.sb(f"ff{o}{i}", [128, 512]) for i in range(2)] for o in range(2)]
            Af = [[sc.sb(f"Af{i}{j}", [128, SC, 128], BF16) for j in range(2)] for i in range(2)]
            kst = [sc.sb(f"kst{i}", [128, 512], BF16) for i in range(4)]
            cw = sc.sb("hcw", [128, 12, 3]); cb = sc.sb("hcb", [128, 12])
            self.dma("sp", cw[:], I["hy_cwT"], [], ["hcw"])
            self.dma("sp", cb[:], I["hy_cbT"], [], ["hcb"])
            raw = [sc.sb(f"hraw{i}", [128, T]) for i in range(2)]
            cvo = [sc.sb(f"hcv{i}", [128, T]) for i in range(2)]

            def stage_a():
                for lc in range(SC):
                    for side in range(2):
                        wtile, wk_ = wt[side][lc % 2], f"win{side}{lc % 2}"
                        self.dma("sp", wtile[:], C[f"win{side}_{g}"][:, lc, :], [], [wk_])
                        for o in range(2):
                            ps, pk = self.next_ps()
                            col0 = o * 1024 + side * 512
                            self.mm(ps[:], h2[:, lc * 128:(lc + 1) * 128], w3[:, col0:col0 + 512], True, True, ["h2", "hyw"], [pk])
                            self.tt("dve", ff[o][side][:], ps[:], wtile[:], ALU.mult, [pk, wk_], [f"ff{o}{side}"])
                    for o in range(2):
                        self.tt("pool", ge[o][:, lc, :], ff[o][0][:], ff[o][1][:], ALU.add, [f"ff{o}0", f"ff{o}1"], [f"ge{o}"])
                        self.tt("pool", go[o][:, lc, :], ff[o][0][:], ff[o][1][:], ALU.subtract, [f"ff{o}0", f"ff{o}1"], [f"go{o}"])
                    if lc % 2 == 1:
                        yield
                self.dma("sp", Af[0][0][:], C[f"Ac_{g}"][0], [], ["Ac0"])
                self.dma("sp", Af[0][1][:], C[f"As_{g}"][0], [], ["As0"])
                for fc in range(NF):
                    a_c, a_s = Af[fc % 2]
                    if fc + 1 < NF:
                        n_c, n_s = Af[(fc + 1) % 2]
                        self.dma("sp", n_c[:], C[f"Ac_{g}"][fc + 1], [], [f"Ac{(fc + 1) % 2}"])
                        self.dma("sp", n_s[:], C[f"As_{g}"][fc + 1], [], [f"As{(fc + 1) % 2}"])
                    for o in range(2):
                        for ri, (am, amk, gm, gmk) in enumerate(((a_c, f"Ac{fc % 2}", ge[o], f"ge{o}"), (a_s, f"As{fc % 2}", go[o], f"go{o}"))):
                            ps, pk = self.next_ps()
                            for lc in range(SC):
                                self.mm(ps[:], am[:, lc, :], gm[:, lc, :], lc == 0, lc == SC - 1, [amk, gmk], [pk])
                            ki = o * 2 + ri
                            self.cp("act" if ri else "dve", kst[ki][:], ps[:], [pk], [f"kst{ki}"])
                            self.dma("sp", ksp[o, ri, fc], kst[ki][:], [f"kst{ki}"], [f"ksp_{g}"])
                    yield

            def stage_b():
                self.dma("sp", raw[0][:], X[f"proj_{g}"][2560:2560 + 128, :], [f"proj_{g}"], ["hraw0"])
                for ch in range(12):
                    r_, rk = raw[ch % 2], f"hraw{ch % 2}"
                    o_, ok = cvo[ch % 2], f"hcv{ch % 2}"
                    if ch + 1 < 12:
                        self.dma("sp", raw[(ch + 1) % 2][:], X[f"proj_{g}"][2560 + (ch + 1) * 128:2560 + (ch + 2) * 128, :],
                                 [f"proj_{g}"], [f"hraw{(ch + 1) % 2}"])
                    yield
                    self.dwconv(r_, rk, o_, ok, cw[:, ch, :], cb[:, ch:ch + 1], ["hcw", "hcb"], nseq, L)
                    yield
                    self.dma("sp", X[f"hyc_{g}"][ch * 128:(ch + 1) * 128, :], o_[:], [ok], [f"hyc_{g}"])
                    yield

            self.mark(f"hy{g} A:spectra+conv")
            self.interleave([stage_a(), stage_b()])
        with self.scope() as sc:
            hd = sc.sb("hd", [128, 2, 4])
            self.dma("sp", hd[:], I["hy_dT"], [], ["hd"])
            z = sc.sb("z", [128, 4, L])
            zb = sc.sb("zb", [128, L], BF16)
            uT = sc.sb("uT", [128, SC, 512], BF16)
            Yre = sc.sb("Yre", [128, NF, 512], BF16)
            Yim = sc.sb("Yim", [128, NF, 512], BF16)
            Af = [[sc.sb(f"Af{i}{j}", [128, SC, 128], BF16) for j in range(2)] for i in range(2)]
            Kt = [[sc.sb(f"Kt{i}{j}", [128, 512], BF16) for j in range(2)] for i in range(2)]
            tm = [sc.sb(f"tm{i}", [128, 512]) for i in range(4)]
            Bc = sc.sb("Bc", [128, NF, TW], BF16)
            Bs = sc.sb("Bs", [128, NF, TW], BF16)
            gt = [sc.sb(f"gt{i}", [128, TW]) for i in range(2)]
            zo = sc.sb("zo", [128, L], BF16)
            for s_ in range(nseq):
                tsl = slice(s_ * L, (s_ + 1) * L)
                self.dma("sp", z[:], X[f"hyc_{g}"][0:512, tsl].rearrange("(c p) t -> p c t", p=128), [f"hyc_{g}"], ["z"])
                for o in range(2):
                    self.mark(f"hy{g} C{o}:transp")
                    for cc in range(4):
                        self.cp("act", zb[:], z[:, cc, :], ["z"], ["zb"])
                        for lc in range(SC):
                            self.tr(self.psT[:, (lc % 8) * 128:(lc % 8 + 1) * 128], zb[:, lc * 128:(lc + 1) * 128], self.ident_b[:],
                                    ["zb", "ident_b"], ["ps7"])
                            if lc % 8 == 7 or lc == SC - 1:
                                n = lc % 8 + 1
                                l0 = lc - n + 1
                                self.cp("act" if cc % 2 else "dve", uT[:, l0:l0 + n, cc * 128:(cc + 1) * 128],
                                        self.psT[:, 0:n * 128].rearrange("p (l c) -> p l c", c=128), ["ps7"], ["uT"])
                    self.mark(f"hy{g} C{o}:fwd")
                    for fc in range(NF):
                        a_c, a_s = Af[fc % 2]
                        k_r, k_i = Kt[fc % 2]
                        self.dma("sp", a_c[:], C[f"Ac_{g}"][fc], [], [f"Ac{fc % 2}"])
                        self.dma("sp", a_s[:], C[f"As_{g}"][fc], [], [f"As{fc % 2}"])
                        self.dma("sp", k_r[:], ksp[o, 0, fc], [f"ksp_{g}"], [f"Kr{fc % 2}"])
                        self.dma("sp", k_i[:], ksp[o, 1, fc], [f"ksp_{g}"], [f"Ki{fc % 2}"])
                        pr, prk = self.next_ps()
                        for lc in range(SC):
                            self.mm(pr[:], a_c[:, lc, :], uT[:, lc, :], lc == 0, lc == SC - 1, [f"Ac{fc % 2}", "uT"], [prk])
                        pi_, pik = self.next_ps()
                        for lc in range(SC):
                            self.mm(pi_[:], a_s[:, lc, :], uT[:, lc, :], lc == 0, lc == SC - 1, [f"As{fc % 2}", "uT"], [pik])
                        self.tt("dve", tm[0][:], pr[:], k_r[:], ALU.mult, [prk, f"Kr{fc % 2}"], ["tm0"])
                        self.tt("dve", tm[1][:], pi_[:], k_i[:], ALU.mult, [pik, f"Ki{fc % 2}"], ["tm1"])
                        self.tt("dve", tm[2][:], pr[:], k_i[:], ALU.mult, [prk, f"Ki{fc % 2}"], ["tm2"])
                        self.tt("dve", tm[3][:], pi_[:], k_r[:], ALU.mult, [pik, f"Kr{fc % 2}"], ["tm3"])
                        self.tt("pool", Yre[:, fc, :], tm[0][:], tm[1][:], ALU.subtract, ["tm0", "tm1"], ["Yre"])
                        self.tt("pool", Yim[:, fc, :], tm[2][:], tm[3][:], ALU.add, ["tm2", "tm3"], ["Yim"])
                    self.mark(f"hy{g} C{o}:inv")
                    for tt in range(NT):
                        self.dma("sp", Bc[:], C[f"Bc_{g}"][tt], [], ["Bc"])
                        self.dma("sp", Bs[:], C[f"Bs_{g}"][tt], [], ["Bs"])
                        for cc in range(4):
                            ps, pk = self.next_ps()
                            for fc in range(NF):
                                self.mm(ps[:, 0:TW], Yre[:, fc, cc * 128:(cc + 1) * 128], Bc[:, fc, :], fc == 0, False, ["Yre", "Bc"], [pk])
                            for fc in range(NF):
                                self.mm(ps[:, 0:TW], Yim[:, fc, cc * 128:(cc + 1) * 128], Bs[:, fc, :], False, fc == NF - 1, ["Yim", "Bs"], [pk])
                            gtile, gk = gt[cc % 2], f"gt{cc % 2}"
                            grow = 512 * (o + 1) + cc * 128
                            self.dma("sp", gtile[:], X[f"hyc_{g}"][grow:grow + 128, s_ * L + tt * TW:s_ * L + (tt + 1) * TW], [f"hyc_{g}"], [gk])
                            zsl = z[:, cc, tt * TW:(tt + 1) * TW]
                            self.stt(zsl, zsl, hd[:, o, cc:cc + 1], ps[:, 0:TW], ALU.mult, ALU.add, ["z", "hd", pk], ["z"])
                            self.tt("dve", zsl, zsl, gtile[:], ALU.mult, ["z", gk], ["z"])
                for cc in range(4):
                    self.cp("act", zo[:], z[:, cc, :], ["z"], ["zo"])
                    self.dma("sp", X[f"mix_{g}"][512 + cc * 128:512 + (cc + 1) * 128, tsl], zo[:], ["zo"], [f"mix_{g}"])
            if self.dbg == "hyena":
                self.dump(f"mixz_{g}", X[f"mix_{g}"][512:1024, :], [512, T], BF16, [f"mix_{g}"])
                self.dump(f"hyc_{g}", X[f"hyc_{g}"], [1536, T], F32, [f"hyc_{g}"])
                self.dump(f"ksp_{g}", ksp, [2, 2, NF, 128, 512], BF16, [f"ksp_{g}"])

    def diffattn(self, g):
        I, X, O, C = self.I, self.X, self.O, self.C
        gi = GROUPS[g]
        T, L, nseq = gi["T"], gi["L"], gi["nseq"]
        NB = L // 128
        NCK = 2 if g == "s" else 0
        NK = NB + NCK
        QW = min(512, L)
        lam_init = 0.8 - 0.6 * math.exp(-0.3 * 1)
        with self.scope() as sc:
            dl = sc.sb("dl", [128, 4, 64])
            pr = sc.sb("dlp", [128, 2, 64])
            lam = sc.sb("lam", [128, 2])
            lamneg = sc.sb("lamneg", [128, 1])
            dn = sc.sb("dn", [128, 4])
            self.dma("sp", dl[:], I["dlam"], [], ["dl"])
            self.dma("sp", dn[:], I["diff_normT"], [], ["dn"])
            self.tt("dve", pr[:, 0, :], dl[:, 0, :], dl[:, 1, :], ALU.mult, ["dl"], ["dlp"])
            self.tt("dve", pr[:, 1, :], dl[:, 2, :], dl[:, 3, :], ALU.mult, ["dl"], ["dlp"])
            self.S.op("dve", lambda e: e.reduce_sum(out=lam[:], in_=pr[:], axis=mybir.AxisListType.X), ["dlp"], ["lam"])
            self.act(lam[:], lam[:], AF.Exp, ["lam"], ["lam"])
            self.tt("dve", lamneg[:], lam[:, 1:2], lam[:, 0:1], ALU.subtract, ["lam"], ["lamneg"])
            self.ts("dve", lamneg[:], lamneg[:], -lam_init, None, ALU.add, None, ["lamneg"], ["lamneg"])
            self.ts("dve", dn[:], dn[:], 1.0 - lam_init, None, ALU.mult, None, ["dn"], ["dn"])
            q = sc.sb("aq", [128, L]); k = sc.sb("ak", [128, L])
            qb2 = [sc.sb(f"aqb{i}", [128, L], BF16) for i in range(2)]
            kall2 = [sc.sb(f"akall{i}", [128, NK * 128], BF16) for i in range(2)]
            Vall2 = [sc.sb(f"aV{i}", [128, NK, 128], BF16) for i in range(2)]
            E = [sc.sb(f"aE{i}", [128, NK, QW], BF16) for i in range(2)]
            on = [sc.sb(f"aon{i}", [128, QW]) for i in range(2)]
            rl = sc.sb("arl", [128, QW])
            Er = sc.sb("aEr", [128, QW])
            ones_f = sc.sb("ones_f", [128, 128])
            self.S.op("pool", lambda e: e.memset(ones_f[:], 1.0), [], ["ones_f"])
            oc = sc.sb("aoc", [128, L])
            sq = sc.sb("hsq", [128, L], BF16)
            rstd = sc.sb("hrstd", [128, 512])
            outb = sc.sb("houtb", [128, L], BF16)
            if g == "s":
                rC = sc.sb("ropeC", [128, L]); rS = sc.sb("ropeS", [128, L]); rP = sc.sb("ropeP", [128, 128])
                self.dma("sp", rC[:], C["ropeC"], [], ["ropeC"])
                self.dma("sp", rS[:], C["ropeS"], [], ["ropeS"])
                self.dma("sp", rP[:], C["ropeP"], [], ["ropeP"])
                rt = sc.sb("ropet", [128, 512])
                kc32 = sc.sb("kc32", [128, 2, 128])
            pj = X[f"proj_{g}"]
            units = [(s_, h) for s_ in range(nseq) for h in range(4)]
            steps = [(qt, p) for qt in range(L // QW) for p in range(2)]

            def setup(u):
                s_, h = units[u]
                pb_ = u % 2
                tsl = slice(s_ * L, (s_ + 1) * L)
                kal, kalk = kall2[pb_], f"akall{pb_}"
                Va, Vak = Vall2[pb_], f"aV{pb_}"
                self.dma("sp", q[:], pj[h * 128:(h + 1) * 128, tsl], [f"proj_{g}"], ["aq"])
                self.dma("sp", k[:], pj[512 + h * 128:512 + (h + 1) * 128, tsl], [f"proj_{g}"], ["ak"])
                self.dma("pool", Va[:, NCK:NK, :],
                         X[f"vtok_{g}"][tsl, 512 + h * 128:512 + (h + 1) * 128].rearrange("(b p) v -> p b v", p=128),
                         [f"vtok_{g}"], [Vak])
                yield
                if g == "s":
                    self.dma("sp", kc32[:], I["ck"][h].rearrange("(c p) d -> p c d", p=128), [], ["kc32"])
                    self.dma("pool", Va[:, 0:2, :], I["cv"][h].rearrange("(c p) d -> p c d", p=128), [], [Vak])
                    yield
                    for (x, xk) in ((q, "aq"), (k, "ak")):
                        for t0 in range(0, L, 512):
                            ps, pk = self.next_ps()
                            self.mm(ps[:], rP[:], x[:, t0:t0 + 512], True, True, ["ropeP", xk], [pk])
                            self.tt("dve", rt[:], ps[:], rS[:, t0:t0 + 512], ALU.mult, [pk, "ropeS"], ["ropet"])
                            yield
                            self.tt("pool", x[:, t0:t0 + 512], x[:, t0:t0 + 512], rC[:, t0:t0 + 512], ALU.mult, [xk, "ropeC"], [xk])
                            self.tt("pool", x[:, t0:t0 + 512], x[:, t0:t0 + 512], rt[:], ALU.add, [xk, "ropet"], [xk])
                            yield
                    for c in range(2):
                        ps, pk = self.next_ps()
                        self.tr(ps[:, 0:128], kc32[:, c, :], self.ident_f[:], ["kc32", "ident_f"], [pk])
                        self.cp("act", kal[:, c * 128:(c + 1) * 128], ps[:, 0:128], [pk], [kalk])
                        yield
                self.cp("act", qb2[pb_][:], q[:], ["aq"], [f"aqb{pb_}"])
                yield
                self.cp("dve", kal[:, NCK * 128:NK * 128], k[:], ["ak"], [kalk])
                yield

            def run(u):
                s_, h = units[u]
                pb_ = u % 2
                tsl = slice(s_ * L, (s_ + 1) * L)
                kal, kalk = kall2[pb_], f"akall{pb_}"
                Va, Vak = Vall2[pb_], f"aV{pb_}"
                qb_, qbk = qb2[pb_], f"aqb{pb_}"

                NKP = NK if NK <= 4 else (2 * NK) // 3
                def s_mm(n, kc):
                    qt, p = steps[n]
                    qsl = slice(qt * QW, (qt + 1) * QW)
                    PP = slice(64 * p, 64 * p + 64)
                    Et, Ek = E[n % 2], f"aE{n % 2}"
                    ps, pk = self.ps[kc % 4], f"ps{kc % 4}"
                    self.mm(ps[:, 0:QW], kal[PP, kc * 128:(kc + 1) * 128], qb_[PP, qsl], True, True, [kalk, qbk], [pk])
                    self.act(Et[:, kc, :], ps[:, 0:QW], AF.Exp, [pk], [Ek], scale=0.125)

                def step(n):
                    qt, p = steps[n]
                    qsl = slice(qt * QW, (qt + 1) * QW)
                    Et, Ek = E[n % 2], f"aE{n % 2}"
                    has_next = n + 1 < len(steps)
                    PSO, PSOK = self.ps[4 + n % 2], f"ps{4 + n % 2}"
                    PSL, PSLK = self.ps[6 + n % 2], f"ps{6 + n % 2}"
                    if NKP < NK:
                        self.S.op("dve", lambda e: e.tensor_reduce(out=Er[:], in_=Et[:, NKP:NK, :].rearrange("p k q -> p q k"),
                                                                   axis=mybir.AxisListType.X, op=ALU.add), [Ek], ["aEr"])
                    for kc in range(NK):
                        if has_next:
                            s_mm(n + 1, kc)
                        self.mm(PSO[:, 0:QW], Va[:, kc, :], Et[:, kc, :], kc == 0, kc == NK - 1, [Vak, Ek], [PSOK])
                        if kc < NKP:
                            self.mm(PSL[:, 0:QW], self.ones_b[:], Et[:, kc, :], kc == 0, (kc == NKP - 1) and NKP == NK,
                                    ["ones_b", Ek], [PSLK])
                    if NKP < NK:
                        self.mm(PSL[:, 0:QW], ones_f[:], Er[:], False, True, ["ones_f", "aEr"], [PSLK])
                    self.act(rl[:], PSL[:, 0:QW], AF.Ln, [PSLK], ["arl"])
                    self.act(rl[:], rl[:], AF.Exp, ["arl"], ["arl"], scale=-1.0)
                    self.tt("dve", on[p][:], PSO[:, 0:QW], rl[:], ALU.mult, [PSOK, "arl"], [f"aon{p}"])
                    if p == 1:
                        self.stt(oc[:, qsl], on[1][:], lamneg[:, 0:1], on[0][:], ALU.mult, ALU.add, ["aon0", "aon1", "lamneg"], ["oTf"])

                for kc in range(NK):
                    s_mm(0, kc)
                yield
                for n in range(len(steps)):
                    step(n)
                    yield
                self.head_out(sc, g, L, 0, s_, oc, None, None, None, dn[:, h:h + 1], "dn", sq, rstd, outb, h * 128, extra_scale=1.0)
                if g == "p":
                    self.dma("sp", O["nck"][s_, h], X[f"vtok_{g}"][tsl, h * 128:(h + 1) * 128], [f"vtok_{g}"], ["nck"])
                    self.dma("sp", O["ncv"][s_, h], X[f"vtok_{g}"][tsl, 512 + h * 128:512 + (h + 1) * 128], [f"vtok_{g}"], ["ncv"])
                yield

            self.interleave([setup(0)])
            for u in range(len(units)):
                gens = [run(u)]
                if u + 1 < len(units):
                    gens.append(setup(u + 1))
                self.interleave(gens)
            if self.dbg == "diff":
                self.dump(f"mixc_{g}", X[f"mix_{g}"][0:512, :], [512, T], BF16, [f"mix_{g}"])

    def gla(self, g):
        I, X, O = self.I, self.X, self.O
        gi = GROUPS[g]
        T, Lseq, nseq0 = gi["T"], gi["L"], gi["nseq"]
        L, nseq = T, 1
        SEGB = Lseq // 128
        NB = L // 128
        GCH = 128
        with self.scope() as sc:
            self.load_scan_masks(sc, L, GCH)
            Zst = sc.sb("Zst", [128, 128])
            self.S.op("dve", lambda e: e.memset(Zst[:], 0.0), [], ["Zst"])
            Sin = {d: sc.sb(f"Sin{d}", [128, 128]) for d in "fb"}
            aw = sc.sb("gaw", [16, 2, 256])
            nab = sc.sb("gnab", [128, 2, 2])
            gn = sc.sb("ggn", [128, 4])
            self.dma("sp", aw[:], I["gla_aw"].rearrange("d r c -> r d c"), [], ["gaw"])
            self.dma("sp", nab[:], I["gla_abT"], [], ["gnab"])
            self.dma("sp", gn[:], I["gla_normT"], [], ["ggn"])
            self.ts("dve", nab[:], nab[:], -1.0, None, ALU.mult, None, ["gnab"], ["gnab"])
            da = {d: sc.sb(f"gda{d}", [16, L]) for d in "fb"}
            qs = sc.sb("qs", [128, L]); kk = sc.sb("kk", [128, L])
            lft = {d: sc.sb(f"lft{d}", [128, L]) for d in "fb"}
            tA = {d: sc.sb(f"tA{d}", [128, L]) for d in "fb"}
            gr = tA["f"]
            qtf = {d: sc.sb(f"qtf{d}", [128, L]) for d in "fb"}
            kh = {d: sc.sb(f"kh{d}", [128, L], BF16) for d in "fb"}
            qt = {d: sc.sb(f"qt{d}", [128, L], BF16) for d in "fb"}
            kt = {d: sc.sb(f"kt{d}", [128, L], BF16) for d in "fb"}
            el = {d: sc.sb(f"el{d}", [128, L // GCH]) for d in "fb"}
            kT = {d: sc.sb(f"kT{d}", [128, NB, 128], BF16) for d in "fb"}
            Vt = sc.sb("Vt", [128, NB, 128], BF16)
            Sst = {d: [sc.sb(f"S{d}{i}", [128, 128]) for i in range(12)] for d in "fb"}
            oT = {d: sc.sb(f"oT{d}", [128, L]) for d in "fb"}
            self.AT = {d: [sc.sb(f"AT{d}{i}", [128, 128], BF16) for i in range(2)] for d in "fb"}
            sq = sc.sb("hsq", [128, L], BF16)
            rstd = sc.sb("hrstd", [128, 512])
            outb = sc.sb("houtb", [128, L], BF16)
            cm = {"f": self.cm128_f, "b": self.cm128_b}
            pj = X[f"proj_{g}"]
            for s in range(nseq):
                tsl = slice(s * L, (s + 1) * L)
                def issue_loads(h):
                    P_ = slice(64 * (h % 2), 64 * (h % 2) + 64)
                    self.dma("sp", qs[P_, :], pj[1536 + 64 * h:1536 + 64 * (h + 1), tsl], [f"proj_{g}"], ["qs"])
                    self.dma("sp", kk[P_, :], pj[1792 + 64 * h:1792 + 64 * (h + 1), tsl], [f"proj_{g}"], ["kk"])
                    if h == 0:
                        for di, d in enumerate("fb"):
                            self.dma("sp", da[d][:], pj[3072 + 16 * di:3088 + 16 * di, tsl], [f"proj_{g}"], [f"gda{d}"])
                    self.dma("pool", Vt[:], X[f"vtok_{g}"][tsl, 1024 + h * 128:1024 + (h + 1) * 128].rearrange("(b p) v -> p b v", p=128),
                             [f"vtok_{g}"], ["Vt"])

                issue_loads(0)
                for h in range(4):
                    pb = 64 * (h % 2)
                    chk = h // 2
                    P = slice(pb, pb + 64)
                    def prep(di, d, h=h, pb=pb, chk=chk, P=P, tsl=tsl):
                        dd, lf_ = da[d], lft[d]
                        dk_, lk_ = f"gda{d}", f"lft{d}"
                        for t0 in range(0, L, 512):
                            w = min(512, L - t0)
                            ps, pk = self.next_ps()
                            self.mm(ps[P, 0:w], aw[:, di, 64 * h:64 * (h + 1)], dd[:, t0:t0 + w], True, True, ["gaw", dk_], [pk])
                            self.act(lf_[P, t0:t0 + w], ps[P, 0:w], AF.Exp, [pk, "gnab"], [lk_], scale=-1.0, bias=nab[P, di, chk:chk + 1])
                            yield
                        self.act(lf_[P, :], lf_[P, :], AF.Ln, [lk_, "ones_col"], [lk_], bias=self.ones_col[P, 0:1])
                        yield
                        self.act(lf_[P, :], lf_[P, :], AF.Copy, [lk_], [lk_], scale=-1.0 / 16.0)
                        yield
                        yield from self.scan_prep_dir(sc, d, L, 64, pb, qs, "qs", 64 ** -0.5, kk, "kk", lf_, lk_, tA[d], qtf[d], qt[d],
                                                      kt[d], kh[d], el[d], kT[d], CH=GCH)

                    self.interleave([prep(0, "f"), prep(1, "b")])
                    if g == "s":
                        for di, d in enumerate("fb"):
                            self.dma("sp", Sin[d][P, :], I["st_gla"][:, di, h, :], [], [f"Sin{d}"])

                    def init_state(d, seg, h=h):
                        return (Zst, "Zst") if g == "p" else (Sin[d], f"Sin{d}")

                    def fin_fn(d, seg, St, sk, h=h, P=P):
                        if g == "p":
                            self.dma("sp", O["nst_gla"][seg, 0 if d == "f" else 1, h], St[P, :], [sk], ["nst_gla"])

                    self.dma("sp", gr[:], pj[2560 + 128 * h:2560 + 128 * (h + 1), tsl], [f"proj_{g}"], ["scAf"])
                    self.scan_chain(sc, L, 64, pb, qtf, qt, kt, el, kT, Vt, "Vt", Sst, oT, cm, init_state, fin_fn, CH=GCH, seg_blocks=SEGB)
                    if h + 1 < 4:
                        issue_loads(h + 1)
                    self.head_out(sc, g, L, 0, s, oT["f"], oT["b"], gr, "scAf", gn[:, h:h + 1], "ggn", sq, rstd, outb, 512 + h * 128)
            if self.dbg == "gla":
                self.dump(f"mixd_{g}", X[f"mix_{g}"][512:1024, :], [512, T], BF16, [f"mix_{g}"])


def _fm(vec, nchunk):
    v = np.asarray(vec, dtype=np.float32)
    lead = v.shape[:-1]
    v = v.reshape(lead + (nchunk, 128))
    return np.ascontiguousarray(np.moveaxis(v, -1, 0))


def _rope_tables():
    half = 32
    inv = (10000.0 ** (-np.arange(0, half, 2, dtype=np.float32) / half)).astype(np.float32)
    t = np.arange(2048)
    row = (t // 64).astype(np.float32)
    col = (t % 64).astype(np.float32)
    Cc = np.zeros((128, 2048), np.float32)
    Ss = np.zeros((128, 2048), np.float32)
    P = np.zeros((128, 128), np.float32)
    for r in range(128):
        d = r % 64
        pos = row if d < 32 else col
        i = d % 32
        fi = i % 16
        ang = (pos * inv[fi]).astype(np.float32)
        Cc[r] = np.cos(ang)
        if i < 16:
            Ss[r] = -np.sin(ang)
            partner = r + 16
        else:
            Ss[r] = np.sin(ang)
            partner = r - 16
        P[partner, r] = 1.0
    return Cc, Ss, P


def prep_core_inputs(inp, core):
    b = core // 4
    f32 = lambda a: np.ascontiguousarray(np.asarray(a, dtype=np.float32))
    m = {}
    xp = f32(inp["x_prompt"][4 * core:4 * core + 4]).reshape(1024, D)
    m["xT_p"] = np.ascontiguousarray(xp.T)
    m["xT_s"] = np.ascontiguousarray(f32(inp["x_sample"][b]).T)
    m["cT"] = np.ascontiguousarray(np.stack([_fm(inp["c_ctx"], 8), _fm(inp["c"][b], 8)], axis=-1))
    m["ada_w"] = f32(inp["ada_w"])
    m["ada_bT"] = _fm(inp["ada_b"], 48)
    m["norm_gT"] = _fm(inp["norm_g"], 8)
    m["ffn_up"] = f32(inp["ffn_up"])
    m["ffn_cwT"] = np.ascontiguousarray(_fm(inp["ffn_conv_w"], 44).transpose(0, 1, 3, 2))
    m["ffn_cbT"] = _fm(inp["ffn_conv_b"], 44)
    m["ffn_down"] = f32(inp["ffn_down"])
    m["w_in_e"] = f32(inp["w_in_even"][0])
    m["w_out_e"] = f32(inp["w_out_even"][0])
    m["lbT"] = np.ascontiguousarray(_fm(inp["hgrn_lb"], 4).reshape(128, 3, 8))
    m["hgrn_normT"] = _fm(inp["hgrn_norm"][0], 4)
    m["hy_cwT"] = np.ascontiguousarray(_fm(inp["hy_conv_w"][0], 12).transpose(0, 2, 1))
    m["hy_cbT"] = _fm(inp["hy_conv_b"][0], 12)
    m["hy_w1"] = f32(inp["hy_w1"][0])
    m["hy_b1"] = f32(inp["hy_b1"][0]).reshape(64, 1)
    m["hy_w2"] = f32(inp["hy_w2"][0])
    m["hy_b2"] = f32(inp["hy_b2"][0]).reshape(64, 1)
    m["hy_w3"] = f32(inp["hy_w3"][0])
    m["hy_freq"] = f32(inp["hy_freq"][0]).reshape(64, 1)
    m["hy_dT"] = _fm(inp["hy_d"][0], 4)
    m["st_hgrn"] = np.ascontiguousarray(f32(inp["state_hgrn"][b, 0]).transpose(2, 0, 1, 3))
    m["w_in_o"] = f32(inp["w_in_odd"][0])
    m["w_out_o"] = f32(inp["w_out_odd"][0])
    m["dlam"] = np.ascontiguousarray(np.broadcast_to(f32(inp["diff_lambda"][0])[None], (128, 4, 64)))
    m["diff_normT"] = _fm(inp["diff_norm"][0], 4)
    m["gla_aw"] = f32(inp["gla_aw"][0])
    m["gla_abT"] = _fm(inp["gla_ab"][0], 2)
    m["gla_normT"] = _fm(inp["gla_norm"][0], 4)
    m["ck"] = f32(inp["cache_diff_k"][b, 0])
    m["cv"] = f32(inp["cache_diff_v"][b, 0])
    m["st_gla"] = np.ascontiguousarray(f32(inp["state_gla"][b, 0]).transpose(2, 0, 1, 3))
    for k, v in make_consts().items():
        m["c_" + k] = v
    Cc, Ss, P = _rope_tables()
    m["c_ropeC"], m["c_ropeS"], m["c_ropeP"] = Cc, Ss, P
    return m


_PROG = {}


def get_prog(dbg=None, nlayers=2):
    key = (dbg, nlayers)
    if key not in _PROG:
        kb = KB(dbg=dbg, nlayers=nlayers)
        kb.build()
        _PROG[key] = kb
    return _PROG[key]


def run_cores(inputs, cores, dbg=None, nlayers=2):
    kb = get_prog(dbg, nlayers)
    in_maps = []
    for c in cores:
        m = prep_core_inputs(inputs, c)
        in_maps.append({k: m[k] for k in kb.in_shapes})
    res = run_bass_kernel_spmd(kb.nc, in_maps, core_ids=list(range(len(cores))))
    return res.results


def kernel(**inputs):
    res = run_cores(inputs, list(range(8)))
    yp = np.zeros((32, 256, D), np.float32)
    ys = np.zeros((2, 2048, D), np.float32)
    nsh = np.zeros((32, 1, 2, 4, 128, 128), np.float32)
    nck = np.zeros((32, 1, 4, 256, 128), np.float32)
    ncv = np.zeros((32, 1, 4, 256, 128), np.float32)
    nsg = np.zeros((32, 1, 2, 4, 64, 128), np.float32)
    for c in range(8):
        r = res[c]
        yp[4 * c:4 * c + 4] = np.asarray(r["yT_p"]).T.reshape(4, 256, D)
        if c % 4 == 0:
            ys[c // 4] = np.asarray(r["yT_s"]).T
        nsh[4 * c:4 * c + 4, 0] = np.asarray(r["nst_hgrn"])
        nck[4 * c:4 * c + 4, 0] = np.asarray(r["nck"])
        ncv[4 * c:4 * c + 4, 0] = np.asarray(r["ncv"])
        nsg[4 * c:4 * c + 4, 0] = np.asarray(r["nst_gla"])
    return (yp, ys, nsh, nck, ncv, nsg)
```
